# Optimizing a Trainium2 kernel written in Bass

```python
import math
import jax
import jax.numpy as jnp
from jax import lax
import numpy as np

D_MODEL = 2048
BATCH = 2
SEQ = 8192
DEPTH = 1

GRID_W = 64
CTX_LEN = 256
EPS = 1e-6

M_HEADS = 8
M_HEAD_DIM = 128
M_WIDTH = M_HEADS * M_HEAD_DIM
M_CHUNK = 128
CONV_K = 5
F_BIAS_LO = 3.0
F_BIAS_HI = 6.0

A_HEADS = 8
A_HALF_DIM = 64
A_V_DIM = 2 * A_HALF_DIM
A_WIDTH = A_HEADS * A_V_DIM
Q_BLOCK = 128
ROPE_BASE = 10000.0

P_HEADS = 8
N_KEYS = 128
N_EXPERTS = N_KEYS * N_KEYS
P_KEY_DIM = 256
P_TOPK = 16
TOKEN_BLOCK = 128

IN_SIZES = (M_WIDTH, M_WIDTH, M_WIDTH, M_WIDTH, 4 * M_HEADS, A_WIDTH, A_WIDTH, A_WIDTH, D_MODEL, D_MODEL)
P_IN = sum(IN_SIZES)
GATE_OFFSET = 4 * M_WIDTH

kernel_name = 'hybrid_mlstm_diffattn_peer_dit_layer'


def rmsnorm(x, w):
    xf = x.astype(jnp.float32)
    y = xf * lax.rsqrt(jnp.mean(xf * xf, axis=-1, keepdims=True) + EPS)
    return (y * w.astype(jnp.float32)).astype(x.dtype)


def modulate(h, shift, scale):
    return h * (1 + scale) + shift


def split_proj(p):
    return jnp.split(p, np.cumsum(IN_SIZES)[:-1].tolist(), axis=-1)


def to_heads(a, n_heads):
    B, T, _ = a.shape
    return a.reshape(B, T, n_heads, -1).transpose(0, 2, 1, 3)


def axial_rope(n_tokens):
    rows = n_tokens // GRID_W
    row = jnp.repeat(jnp.arange(rows, dtype=jnp.float32), GRID_W)
    col = jnp.tile(jnp.arange(GRID_W, dtype=jnp.float32), rows)
    axis_dim = A_HALF_DIM // 2
    inv_freq = ROPE_BASE ** (-jnp.arange(0, axis_dim, 2, dtype=jnp.float32) / axis_dim)
    ang = jnp.concatenate([row[:, None] * inv_freq, col[:, None] * inv_freq], axis=-1)
    return jnp.cos(ang), jnp.sin(ang)


def apply_rope(x, cos, sin):
    x1, x2 = jnp.split(x, 2, axis=-1)
    cos = cos.astype(x.dtype)
    sin = sin.astype(x.dtype)
    return jnp.concatenate([x1 * cos - x2 * sin, x2 * cos + x1 * sin], axis=-1)


def dwconv_centred(x, w, b):
    C = x.shape[-1]
    y = lax.conv_general_dilated(x, w[:, None, :], window_strides=(1,),
                                 padding=[(CONV_K // 2, CONV_K // 2)],
                                 dimension_numbers=('NWC', 'WIO', 'NWC'),
                                 feature_group_count=C)
    return y + b


def mlstm_prep(qm, km, vm, gm, conv_w, conv_b):
    qk = jax.nn.silu(dwconv_centred(jnp.concatenate([qm, km], axis=-1), conv_w, conv_b))
    qm, km = jnp.split(qk, 2, axis=-1)
    q = to_heads(qm, M_HEADS).astype(jnp.float32) * (M_HEAD_DIM ** -0.5)
    k = to_heads(km, M_HEADS).astype(jnp.float32)
    v = to_heads(vm, M_HEADS).astype(jnp.float32)
    B, T, _ = gm.shape
    g = gm.astype(jnp.float32).reshape(B, T, 4, M_HEADS).transpose(2, 0, 3, 1)
    fwd = (g[0], jax.nn.log_sigmoid(g[1]))
    bwd = (g[2], jax.nn.log_sigmoid(g[3]))
    return q, k, v, fwd, bwd


def mlstm_scan(q, k, v, i_pre, log_f, state, with_output):
    B, H, T, d = q.shape
    nc = T // M_CHUNK

    def chunks(a):
        return jnp.moveaxis(a.reshape(B, H, nc, M_CHUNK, *a.shape[3:]), 2, 0)

    tri = jnp.tril(jnp.ones((M_CHUNK, M_CHUNK), dtype=bool))

    def step(carry, xs):
        C, n, m = carry
        qc, kc, vc, ic, fc = xs
        b = jnp.cumsum(fc, axis=-1)
        b_end = b[..., -1]
        log_w_end = b_end[..., None] - b + ic
        m_end = jnp.maximum(b_end + m, jnp.max(log_w_end, axis=-1))
        w_end = jnp.exp(log_w_end - m_end[..., None])
        decay = jnp.exp(b_end + m - m_end)
        C_new = decay[..., None, None] * C + jnp.einsum('bhsk,bhsv->bhkv', kc * w_end[..., None], vc)
        n_new = decay[..., None] * n + jnp.einsum('bhs,bhsk->bhk', w_end, kc)
        out = None
        if with_output:
            log_d = jnp.where(tri, b[..., :, None] - b[..., None, :] + ic[..., None, :], -jnp.inf)
            m_t = jnp.maximum(b + m[..., None], jnp.max(log_d, axis=-1))
            inter = jnp.exp(b + m[..., None] - m_t)
            qk = jnp.einsum('bhtk,bhsk->bhts', qc, kc) * jnp.exp(log_d - m_t[..., None])
            num = inter[..., None] * jnp.einsum('bhtk,bhkv->bhtv', qc, C) + jnp.einsum('bhts,bhsv->bhtv', qk, vc)
            den = inter * jnp.einsum('bhtk,bhk->bht', qc, n) + jnp.sum(qk, axis=-1)
            out = num / jnp.maximum(jnp.abs(den), jnp.exp(-m_t))[..., None]
        return (C_new, n_new, m_end), out

    state, h = lax.scan(step, state, tuple(chunks(a) for a in (q, k, v, i_pre, log_f)))
    if with_output:
        h = jnp.moveaxis(h, 0, 2).reshape(B, H, T, d)
    return h, state


def mlstm_bidirectional(lat, ctx_s, ctx_out):
    q, k, v, gf, gb = lat
    qc, kc, vc, gfc, gbc = ctx_s
    B, H = q.shape[:2]
    init = (jnp.zeros((B, H, M_HEAD_DIM, M_HEAD_DIM), jnp.float32),
            jnp.zeros((B, H, M_HEAD_DIM), jnp.float32),
            jnp.zeros((B, H), jnp.float32))

    def flip(a):
        return jnp.flip(a, axis=2)

    hc_f, st_f = mlstm_scan(qc, kc, vc, gfc[0], gfc[1], init, ctx_out)
    h_f, _ = mlstm_scan(q, k, v, gf[0], gf[1], st_f, True)
    hc_b, st_b = mlstm_scan(flip(qc), flip(kc), flip(vc), flip(gbc[0]), flip(gbc[1]), init, ctx_out)
    h_b, _ = mlstm_scan(flip(q), flip(k), flip(v), flip(gb[0]), flip(gb[1]), st_b, True)
    h_lat = h_f + flip(h_b)
    h_ctx = hc_f + flip(hc_b) if ctx_out else None
    return h_lat, h_ctx


def mlstm_out(h, z, norm_w):
    B, H, T, d = h.shape
    hn = rmsnorm(h.transpose(0, 2, 1, 3), norm_w).reshape(B, T, H * d)
    return (jax.nn.sigmoid(z.astype(jnp.float32)) * hn).astype(z.dtype)


def diff_prep(qa, ka, va):
    B, T, _ = qa.shape
    q = qa.reshape(B, T, A_HEADS, 2, A_HALF_DIM).transpose(0, 2, 3, 1, 4)
    k = ka.reshape(B, T, A_HEADS, 2, A_HALF_DIM).transpose(0, 2, 3, 1, 4)
    v = to_heads(va, A_HEADS)
    return q, k, v


def diff_attention(q, k, v, lam):
    B, H, _, Tq, d = q.shape
    nb = Tq // Q_BLOCK
    qb = jnp.moveaxis(q.reshape(B, H, 2, nb, Q_BLOCK, d), 3, 0)
    scale = A_HALF_DIM ** -0.5

    def block(qx):
        s = jnp.einsum('bhmqd,bhmkd->bhmqk', qx, k).astype(jnp.float32) * scale
        p = jax.nn.softmax(s, axis=-1)
        a = p[:, :, 0] - lam * p[:, :, 1]
        return jnp.einsum('bhqk,bhkv->bhqv', a.astype(v.dtype), v)

    o = lax.map(block, qb)
    return jnp.moveaxis(o, 0, 2).reshape(B, H, Tq, v.shape[-1])


def diff_out(o, norm_w, lam_init):
    B, H, T, dv = o.shape
    return (rmsnorm(o.transpose(0, 2, 1, 3), norm_w) * (1 - lam_init)).reshape(B, T, H * dv)


def merge_branches(y_m, y_d, ga, gb, w_pa, w_pb, w_out):
    return (jax.nn.sigmoid(ga) * (y_m @ w_pa) + jax.nn.sigmoid(gb) * (y_d @ w_pb)) @ w_out


def token_mixer(h, hc, cos, sin, lam, lam_init, w_in, b_in, conv_w, conv_b, m_norm_w,
                a_norm_w, w_pa, w_pb, w_out, ctx_out):
    qm, km, vm, zm, gm, qa, ka, va, ga, gb = split_proj(h @ w_in + b_in)
    qmc, kmc, vmc, zmc, gmc, qac, kac, vac, gac, gbc = split_proj(hc @ w_in + b_in)
    hm, hmc = mlstm_bidirectional(mlstm_prep(qm, km, vm, gm, conv_w, conv_b),
                                  mlstm_prep(qmc, kmc, vmc, gmc, conv_w, conv_b), ctx_out)
    q, k, v = diff_prep(qa, ka, va)
    q = apply_rope(q, cos, sin)
    k = apply_rope(k, cos, sin)
    qc, kc, vc = diff_prep(qac, kac, vac)
    o = diff_attention(q, jnp.concatenate([k, kc], axis=3), jnp.concatenate([v, vc], axis=2), lam)
    y = merge_branches(mlstm_out(hm, zm, m_norm_w), diff_out(o, a_norm_w, lam_init), ga, gb, w_pa, w_pb, w_out)
    if not ctx_out:
        return y, None
    oc = diff_attention(qc, kc, vc, lam)
    yc = merge_branches(mlstm_out(hmc, zmc, m_norm_w), diff_out(oc, a_norm_w, lam_init), gac, gbc, w_pa, w_pb, w_out)
    return y, yc


def peer(h, w_pq, sub_keys, expert_u, expert_v):
    B, T, D = h.shape
    q = (h @ w_pq).reshape(B, T, P_HEADS, 2, P_KEY_DIM // 2)
    s = jnp.einsum('bthpd,hpnd->bthpn', q, sub_keys).astype(jnp.float32)
    s1, i1 = lax.top_k(s[..., 0, :], P_TOPK)
    s2, i2 = lax.top_k(s[..., 1, :], P_TOPK)
    cand_s = (s1[..., :, None] + s2[..., None, :]).reshape(B, T, P_HEADS, P_TOPK * P_TOPK)
    cand_i = (i1[..., :, None] * N_KEYS + i2[..., None, :]).reshape(B, T, P_HEADS, P_TOPK * P_TOPK)
    top_s, pos = lax.top_k(cand_s, P_TOPK)
    idx = jnp.take_along_axis(cand_i, pos, axis=-1)
    g = jax.nn.softmax(top_s, axis=-1)
    nb = (B * T) // TOKEN_BLOCK
    E = P_HEADS * P_TOPK

    def block(args):
        hx, ix, gx = args
        act = jax.nn.gelu(jnp.einsum('td,ted->te', hx, expert_u[ix]).astype(jnp.float32), approximate=False)
        return jnp.einsum('te,ted->td', (gx * act).astype(h.dtype), expert_v[ix])

    out = lax.map(block, (h.reshape(nb, TOKEN_BLOCK, D), idx.reshape(nb, TOKEN_BLOCK, E),
                          g.reshape(nb, TOKEN_BLOCK, E)))
    return out.reshape(B, T, D)


def setup_inputs(seed: int = 0) -> dict:
    key = jax.random.key(seed)
    ks = jax.random.split(key, 24)
    f32 = jnp.float32
    D = D_MODEL

    def nrm(k, shape, scale):
        return jax.random.normal(k, shape, f32) * scale

    f_bias = np.zeros((P_IN,), np.float32)
    f_lin = np.linspace(F_BIAS_LO, F_BIAS_HI, M_HEADS, dtype=np.float32)
    f_bias[GATE_OFFSET + M_HEADS:GATE_OFFSET + 2 * M_HEADS] = f_lin
    f_bias[GATE_OFFSET + 3 * M_HEADS:GATE_OFFSET + 4 * M_HEADS] = f_lin
    return {
        'x': nrm(ks[0], (BATCH, SEQ, D), 1.0),
        'c': nrm(ks[1], (BATCH, D), 1.0),
        'ctx': nrm(ks[2], (BATCH, CTX_LEN, D), 1.0),
        'c_ctx': nrm(ks[3], (D,), 1.0),
        'w_mod': nrm(ks[4], (DEPTH, D, 6 * D), 0.5 * D ** -0.5),
        'b_mod': nrm(ks[5], (DEPTH, 6 * D), 0.02),
        'norm1_w': 1.0 + nrm(ks[6], (DEPTH, D), 0.02),
        'w_in': nrm(ks[7], (DEPTH, D, P_IN), D ** -0.5),
        'b_in': nrm(ks[8], (DEPTH, P_IN), 0.02) + jnp.asarray(f_bias),
        'conv_w': nrm(ks[9], (DEPTH, CONV_K, 2 * M_WIDTH), CONV_K ** -0.5),
        'conv_b': nrm(ks[10], (DEPTH, 2 * M_WIDTH), 0.02),
        'm_norm_w': 1.0 + nrm(ks[11], (DEPTH, M_HEADS, M_HEAD_DIM), 0.02),
        'lambdas': nrm(ks[12], (DEPTH, 4, A_HALF_DIM), 0.1),
        'a_norm_w': 1.0 + nrm(ks[13], (DEPTH, A_V_DIM), 0.02),
        'w_pa': nrm(ks[14], (DEPTH, M_WIDTH, D), M_WIDTH ** -0.5),
        'w_pb': nrm(ks[15], (DEPTH, A_WIDTH, D), A_WIDTH ** -0.5),
        'w_out': nrm(ks[16], (DEPTH, D, D), D ** -0.5),
        'norm2_w': 1.0 + nrm(ks[17], (DEPTH, D), 0.02),
        'w_pq': nrm(ks[18], (DEPTH, D, P_HEADS * P_KEY_DIM), D ** -0.5),
        'sub_keys': nrm(ks[19], (DEPTH, P_HEADS, 2, N_KEYS, P_KEY_DIM // 2), (P_KEY_DIM // 2) ** -0.5),
        'expert_u': nrm(ks[20], (DEPTH, N_EXPERTS, D), D ** -0.5),
        'expert_v': nrm(ks[21], (DEPTH, N_EXPERTS, D), (P_HEADS * P_TOPK) ** -0.5),
        'final_norm_w': 1.0 + nrm(ks[22], (D,), 0.02),
    }


def reference(x, c, ctx, c_ctx, w_mod, b_mod, norm1_w, w_in, b_in, conv_w, conv_b, m_norm_w,
              lambdas, a_norm_w, w_pa, w_pb, w_out, norm2_w, w_pq, sub_keys, expert_u, expert_v,
              final_norm_w):
    cos, sin = axial_rope(x.shape[1])
    for l in range(DEPTH):
        ctx_out = l < DEPTH - 1
        lam_init = 0.8 - 0.6 * math.exp(-0.3 * l)
        lq1, lk1, lq2, lk2 = lambdas[l].astype(jnp.float32)
        lam = jnp.exp(jnp.sum(lq1 * lk1)) - jnp.exp(jnp.sum(lq2 * lk2)) + lam_init
        mod = jax.nn.silu(c) @ w_mod[l] + b_mod[l]
        sh1, sc1, g1, sh2, sc2, g2 = jnp.split(mod[:, None, :], 6, axis=-1)
        mod_c = jax.nn.silu(c_ctx) @ w_mod[l] + b_mod[l]
        csh1, csc1, cg1, csh2, csc2, cg2 = jnp.split(mod_c, 6, axis=-1)
        h = modulate(rmsnorm(x, norm1_w[l]), sh1, sc1)
        hc = modulate(rmsnorm(ctx, norm1_w[l]), csh1, csc1)
        y, yc = token_mixer(h, hc, cos, sin, lam, lam_init, w_in[l], b_in[l], conv_w[l], conv_b[l],
                            m_norm_w[l], a_norm_w[l], w_pa[l], w_pb[l], w_out[l], ctx_out)
        x = x + g1 * y
        x = x + g2 * peer(modulate(rmsnorm(x, norm2_w[l]), sh2, sc2), w_pq[l], sub_keys[l], expert_u[l], expert_v[l])
        if ctx_out:
            ctx = ctx + cg1 * yc
            ctx = ctx + cg2 * peer(modulate(rmsnorm(ctx, norm2_w[l]), csh2, csc2),
                                   w_pq[l], sub_keys[l], expert_u[l], expert_v[l])
    return rmsnorm(x, final_norm_w)
```

```python
import numpy as np
from contextlib import ExitStack
import concourse.bass as bass
import concourse.mybir as mybir
from concourse.bass_utils import run_bass_kernel_spmd

F32 = mybir.dt.float32
BF16 = mybir.dt.bfloat16
I32 = mybir.dt.int32
AF = mybir.ActivationFunctionType
ALU = mybir.AluOpType
AX = mybir.AxisListType

D = 2048
KC = 16
EPS = 1e-6
BIG = 30000.0
OFF = dict(qm=0, km=1024, vm=2048, zm=3072, g=4096, qa=4128, ka=5152, va=6176, ga=7200, gb=9248)
P_IN = 11296
NEXP = 16384
LAM_INIT = 0.8 - 0.6

ENG_ATTR = {'pe': 'tensor', 'dve': 'vector', 'act': 'scalar', 'pool': 'gpsimd', 'sp': 'sync'}
NDMASEM = 12


class Phase:
    def __init__(self, nc, name):
        self.nc = nc
        self.name = name
        self.ops = []
        self.state = {}
        st = SEMSTATE[0]
        self.ncomp = dict(st.ncomp)
        self.ncomp0 = dict(st.ncomp)
        self.ndma = {e: 0 for e in ENG_ATTR}
        self.dmasem_total = dict(st.dtot)
        self.dmasem_last = {}
        self.es = ExitStack()

    def sb(self, name, shape, dt=F32):
        return self.es.enter_context(self.nc.sbuf_tensor(self.name + '_' + name, list(shape), dt))

    def ps(self, name, shape, dt=F32):
        esz = 4 if dt == F32 else 2
        n = int(np.prod(shape[1:]))
        assert n * esz <= 2048
        t = self.es.enter_context(self.nc.psum_tensor(self.name + '_' + name, [128, 2048 // esz], dt))
        v = t[:shape[0], 0:n]
        if len(shape) == 3:
            v = v.rearrange("p (a b) -> p a b", b=shape[2])
        return v

    def _deps(self, reads, writes):
        deps = set()
        for k in reads:
            st = self.state.get(k)
            if st and st[0] is not None:
                deps.add(st[0])
        for k in writes:
            st = self.state.get(k)
            if st:
                if st[0] is not None:
                    deps.add(st[0])
                deps.update(st[1])
        return deps

    def _commit(self, oid, reads, writes):
        for k in reads:
            self.state.setdefault(k, [None, []])[1].append(oid)
        for k in writes:
            self.state[k] = [oid, []]

    def op(self, eng, fn, reads=(), writes=()):
        oid = len(self.ops)
        deps = self._deps(reads, writes)
        self.ncomp[eng] += 1
        self.ops.append(dict(id=oid, eng=eng, fn=fn, dma=False, deps=deps, seq=self.ncomp[eng]))
        self._commit(oid, reads, writes)
        return oid

    def dma(self, eng, fn, reads=(), writes=()):
        oid = len(self.ops)
        deps = self._deps(reads, writes)
        slot = self.ndma[eng] % NDMASEM
        self.ndma[eng] += 1
        key = (eng, slot)
        prev = self.dmasem_last.get(key)
        if prev is not None:
            deps.add(prev)
        tot = self.dmasem_total.get(key, 0) + 16
        self.dmasem_total[key] = tot
        self.dmasem_last[key] = oid
        self.ops.append(dict(id=oid, eng=eng, fn=fn, dma=True, deps=deps, semkey=key, semval=tot))
        self._commit(oid, reads, writes)
        return oid

    def emit(self):
        nc = self.nc
        st = SEMSTATE[0]
        csem, dsem = st.csem, st.dsem
        waited = {e: {} for e in ENG_ATTR}
        for o in self.ops:
            w = {}
            for d in o['deps']:
                do = self.ops[d]
                if do['dma']:
                    s, v = ('d', do['semkey']), do['semval']
                else:
                    if do['eng'] == 'pe' and o['eng'] == 'pe' and not o['dma']:
                        continue
                    s, v = ('c', do['eng']), do['seq']
                if w.get(s, 0) < v:
                    w[s] = v
            wl = []
            for s, v in w.items():
                if waited[o['eng']].get(s, 0) < v:
                    waited[o['eng']][s] = v
                    wl.append((s, v))
            o['waits'] = wl
        finals = [(('d', k), v) for k, v in self.dmasem_total.items() if v > st.dtot.get(k, 0)]
        finals += [(('c', e), self.ncomp[e]) for e in ENG_ATTR if self.ncomp[e] > self.ncomp0[e]]

        def semof(s):
            return dsem[s[1]] if s[0] == 'd' else csem[s[1]]

        with nc.Block() as block:
            for e, attr in ENG_ATTR.items():
                mine = [o for o in self.ops if o['eng'] == e]
                if not mine and e != 'sp':
                    continue

                def body(engine, mine=mine, e=e):
                    for o in mine:
                        for s, v in o['waits']:
                            engine.wait_ge(semof(s), v)
                        ins = o['fn'](engine)
                        if o['dma']:
                            ins.then_inc(dsem[o['semkey']], 16)
                        else:
                            ins.then_inc(csem[e], 1)
                    if e == 'sp':
                        for s, v in finals:
                            engine.wait_ge(semof(s), v)
                getattr(block, attr)(body)
        st.ncomp = dict(self.ncomp)
        st.dtot = dict(self.dmasem_total)
        self.es.close()


class SemState:
    def __init__(self, nc, es):
        self.csem = {e: es.enter_context(nc.semaphore('c_' + e)) for e in ENG_ATTR}
        self.dsem = {(e, i): es.enter_context(nc.semaphore('d_%s%d' % (e, i))) for e in ('sp', 'pool') for i in range(NDMASEM)}
        self.ncomp = {e: 0 for e in ENG_ATTR}
        self.dtot = {}


SEMSTATE = [None]


def bc(ap, shape):
    return ap.to_broadcast(list(shape))


def blocks(cfg):
    out = []
    s = 0
    while s < cfg['NCTX']:
        n = min(512, cfg['NCTX'] - s)
        out.append((s, n)); s += n
    while s < cfg['NS']:
        n = min(512, cfg['NS'] - s)
        out.append((s, n)); s += n
    return out


def pcol(cfg, s):
    return s + 2 if s < cfg['NCTX'] else s + 4


def load_w(ph, wt, key, w_dram, c0, c1):
    K = w_dram.shape[0]
    wv = w_dram.rearrange("(k p) c -> p k c", p=128)
    for k in range(K // 128):
        ph.dma('pool', lambda e, k=k: e.dma_start(out=wt[:, k, 0:c1 - c0], in_=wv[:, k, c0:c1]), writes=[key])


def phase_init(nc, G, io):
    ph = Phase(nc, 'I')
    for n in ('ident', 'maskf', 'maskb', 'ones'):
        ph.dma('sp', lambda e, n=n: e.dma_start(out=G[n][:], in_=io[n]), writes=[n])
    ph.dma('sp', lambda e: e.dma_start(out=G['sel'][:], in_=io['sel']), writes=['sel'])
    ph.op('dve', lambda e: e.tensor_copy(out=G['identb'][:], in_=G['ident'][:]), reads=['ident'], writes=['identb'])
    ph.op('dve', lambda e: e.tensor_copy(out=G['onesb'][:], in_=G['ones'][:]), reads=['ones'], writes=['onesb'])
    lt = ph.sb('lt', [128, 4, 64]); pr = ph.sb('pr', [128, 2, 64]); sm = ph.sb('sm', [128, 2])
    ph.dma('sp', lambda e: e.dma_start(out=lt[:].rearrange("p a b -> p (a b)"), in_=bc(io['lambdas'], [128, 256])),
           writes=['lt'])
    ph.op('dve', lambda e: e.tensor_tensor(out=pr[:, 0, :], in0=lt[:, 0, :], in1=lt[:, 1, :], op=ALU.mult),
          reads=['lt'], writes=['pr0'])
    ph.op('dve', lambda e: e.tensor_tensor(out=pr[:, 1, :], in0=lt[:, 2, :], in1=lt[:, 3, :], op=ALU.mult),
          reads=['lt'], writes=['pr1'])
    ph.op('dve', lambda e: e.tensor_reduce(out=sm[:], in_=pr[:], axis=AX.X, op=ALU.add), reads=['pr0', 'pr1'],
          writes=['sm'])
    ph.op('act', lambda e: e.activation(out=sm[:], in_=sm[:], func=AF.Exp), reads=['sm'], writes=['sm'])
    ph.op('dve', lambda e: e.tensor_tensor(out=G['nlam'][:], in0=sm[:, 1:2], in1=sm[:, 0:1], op=ALU.subtract),
          reads=['sm'], writes=['nlam'])
    ph.op('dve', lambda e: e.tensor_scalar(out=G['nlam'][:], in0=G['nlam'][:], scalar1=-LAM_INIT, scalar2=None,
                                           op0=ALU.add), reads=['nlam'], writes=['nlam'])
    ph.emit()


def phase_mod(nc, G, io):
    ph = Phase(nc, 'A')
    ct = ph.sb('ct', [128, 16, 2]); sc = ph.sb('sc', [128, 16, 2])
    bm = ph.sb('bm', [128, 96]); n1 = ph.sb('n1', [128, 16]); n2 = ph.sb('n2', [128, 16])
    slab = [ph.sb('slab%d' % i, [128, 16, 512]) for i in range(2)]
    pm = ph.ps('pm', [128, 96, 2])
    tmp = ph.sb('tmp', [128, 16])
    ph.dma('sp', lambda e: e.dma_start(out=ct[:], in_=io['c_t']), writes=['ct'])
    ph.dma('sp', lambda e: e.dma_start(out=bm[:], in_=io['b_modT']), writes=['bm'])
    ph.dma('sp', lambda e: e.dma_start(out=n1[:], in_=io['norm1T']), writes=['n1'])
    ph.dma('sp', lambda e: e.dma_start(out=n2[:], in_=io['norm2T']), writes=['n2'])
    ph.op('act', lambda e: e.activation(out=sc[:], in_=ct[:], func=AF.Silu), reads=['ct'], writes=['sc'])
    wv = io['w_mod'].rearrange("(k p) c -> p k c", p=128)
    for s in range(24):
        sl = slab[s % 2]
        ph.dma('sp', lambda e, sl=sl, s=s: e.dma_start(out=sl[:], in_=wv[:, :, s * 512:(s + 1) * 512]),
               writes=['slab%d' % (s % 2)])
        for jj in range(4):
            j = s * 4 + jj
            for k in range(16):
                ph.op('pe', lambda e, sl=sl, jj=jj, j=j, k=k: e.matmul(
                    pm[:, j, :], lhsT=sl[:, k, jj * 128:(jj + 1) * 128], rhs=sc[:, k, :],
                    start=(k == 0), stop=(k == 15)),
                    reads=['slab%d' % (s % 2), 'sc'], writes=['pm'])
    modT = G['modT']
    ph.op('dve', lambda e: e.tensor_tensor(out=modT[:], in0=pm[:], in1=bc(bm[:].unsqueeze(2), [128, 96, 2]),
                                           op=ALU.add), reads=['bm'], writes=['modT', 'pm'])
    for name, nrm, lo, col in (('a1', n1, 16, 0), ('a1c', n1, 16, 1), ('a2', n2, 64, 0)):
        t = G[name]
        ph.op('dve', lambda e, lo=lo, col=col: e.tensor_scalar(out=tmp[:], in0=modT[:, lo:lo + 16, col], scalar1=1.0,
                                                              scalar2=None, op0=ALU.add),
              reads=['modT'], writes=['tmp'])
        ph.op('dve', lambda e, t=t, nrm=nrm: e.tensor_tensor(out=t[:], in0=tmp[:], in1=nrm[:], op=ALU.mult),
              reads=['tmp', 'n1', 'n2'], writes=[name])
    gv = io['gvec']
    for gi, lo in ((0, 32), (1, 80)):
        ph.dma('sp', lambda e, gi=gi, lo=lo: e.dma_start(
            out=gv[gi].rearrange("(k p) -> p k", p=128), in_=modT[:, lo:lo + 16, 0],
            allow_slow_non_contiguous=True), reads=['modT'], writes=['gv%d' % gi])
        ph.dma('sp', lambda e, gi=gi: e.dma_start(
            out=G['g%d_bc' % (gi + 1)][:], in_=bc(gv[gi:gi + 1, :], [128, 2048])),
            reads=['gv%d' % gi], writes=['gbc%d' % gi])
    ph.emit()


def phase_norm(nc, G, name, x, hT, ntok, segs, xres=None):
    ph = Phase(nc, name)
    ntile = (ntok + 127) // 128
    xt = [ph.sb('xt%d' % i, [128, 2048]) for i in range(2)]
    junk = ph.sb('junk', [128, 2048], BF16)
    xn = [ph.sb('xn%d' % i, [128, 2048]) for i in range(2)]
    ss = ph.sb('ss', [128, 2]); rs = ph.sb('rs', [128, 2])
    pt = [ph.ps('pt%d' % i, [128, 4, 128]) for i in range(4)]
    stg = [ph.sb('stg%d' % i, [128, 16, 512], BF16) for i in range(2)]
    ident = G['ident']
    for t in range(ntile):
        np_ = min(128, ntok - t * 128)
        b = t % 2
        for (lo, hi, a_t, s_t) in segs:
            if lo <= t < hi:
                A, S = a_t, s_t
        grp = t // 4; sg = stg[grp % 2]; sgk = 'stg%d' % (grp % 2)
        ph.dma('sp', lambda e, b=b, t=t, np_=np_: e.dma_start(out=xt[b][:np_, :], in_=x[t * 128:t * 128 + np_, :]),
               writes=['xt%d' % b])
        ph.op('act', lambda e, b=b, np_=np_: e.activation(out=junk[:np_, :], in_=xt[b][:np_, :], func=AF.Square,
                                                           accum_out=ss[:np_, b:b + 1]),
              reads=['xt%d' % b], writes=['junk', 'ss%d' % b])
        ph.op('dve', lambda e, b=b, np_=np_: e.tensor_scalar(out=rs[:np_, b:b + 1], in0=ss[:np_, b:b + 1],
                                                             scalar1=1.0 / 2048, scalar2=EPS, op0=ALU.mult, op1=ALU.add),
              reads=['ss%d' % b], writes=['rs%d' % b])
        ph.op('act', lambda e, b=b, np_=np_: e.activation(out=rs[:np_, b:b + 1], in_=rs[:np_, b:b + 1], func=AF.Sqrt),
              reads=['rs%d' % b], writes=['rs%d' % b])
        ph.op('dve', lambda e, b=b, np_=np_: e.reciprocal(out=rs[:np_, b:b + 1], in_=rs[:np_, b:b + 1]),
              reads=['rs%d' % b], writes=['rs%d' % b])
        ph.op('dve', lambda e, b=b, np_=np_: e.tensor_scalar(out=xn[b][:np_, :], in0=xt[b][:np_, :],
                                                             scalar1=rs[:np_, b:b + 1], scalar2=None, op0=ALU.mult),
              reads=['xt%d' % b, 'rs%d' % b], writes=['xn%d' % b])
        for q in range(4):
            pq = q
            for kk in range(4):
                k = q * 4 + kk
                ph.op('pe', lambda e, b=b, k=k, kk=kk, pq=pq, np_=np_: e.transpose(
                    out=pt[pq][:, kk, :np_], in_=xn[b][:np_, k * 128:(k + 1) * 128], identity=ident[:np_, :np_]),
                    reads=['xn%d' % b], writes=['pt%d' % pq])
            c0 = (t % 4) * 128
            eng = 'dve' if q % 2 == 0 else 'act'
            for kk in range(4):
                k = q * 4 + kk
                if eng == 'dve':
                    ph.op('dve', lambda e, k=k, kk=kk, pq=pq, np_=np_, sg=sg, c0=c0, A=A, S=S: e.tensor_scalar(
                        out=sg[:, k, c0:c0 + np_], in0=pt[pq][:, kk, :np_], scalar1=A[:, k:k + 1], scalar2=S[:, k:k + 1],
                        op0=ALU.mult, op1=ALU.add), writes=[sgk + '_%d' % k, 'pt%d' % pq])
                else:
                    ph.op('act', lambda e, k=k, kk=kk, pq=pq, np_=np_, sg=sg, c0=c0, A=A, S=S: e.activation(
                        out=sg[:, k, c0:c0 + np_], in_=pt[pq][:, kk, :np_], func=AF.Identity,
                        scale=A[:, k:k + 1], bias=S[:, k:k + 1]), writes=[sgk + '_%d' % k, 'pt%d' % pq])
        if t % 4 == 3 or t == ntile - 1:
            t0 = grp * 512
            n = t * 128 + np_ - t0
            ph.dma('sp', lambda e, sg=sg, t0=t0, n=n: e.dma_start(
                out=hT[:, :, t0:t0 + n].rearrange("k p t -> p k t"), in_=sg[:, :, :n]),
                reads=[sgk + '_%d' % k for k in range(16)], writes=['hT'])
    ph.emit()


def phase_gemm_fm(nc, G, name, io, cfg, hT, ntok, blks, jobs, w_dram, kchunks=KC):
    ph = Phase(nc, name)
    tot_cols = sum(j['ncb'] * j.get('m', 128) for j in jobs)
    wt = ph.sb('w', [128, kchunks, tot_cols], BF16)
    off = 0
    for j in jobs:
        cw = j['ncb'] * j.get('m', 128)
        j['woff'] = off
        wv = w_dram.rearrange("(k p) c -> p k c", p=128)
        for k in range(kchunks):
            ph.dma('pool', lambda e, k=k, off=off, cw=cw, j=j: e.dma_start(
                out=wt[:, k, off:off + cw], in_=wv[:, k, j['c0']:j['c0'] + cw]), writes=['w'])
        off += cw
        j['bt'] = ph.sb('b%d' % j['c0'], [128, j['ncb']])
        ph.dma('sp', lambda e, j=j: e.dma_start(out=j['bt'][:j.get('m', 128), :], in_=j['bias']), writes=['bias'])
    hb = [ph.sb('h%d' % i, [128, kchunks, 512], BF16) for i in range(2)]
    pp = [ph.ps('pp%d' % i, [128, 512]) for i in range(4)]
    so = {}
    for odt in set(j['odt'] for j in jobs):
        so[odt] = [ph.sb('so%s%d' % (str(odt)[-4:], i), [128, 512], odt) for i in range(4)]
    cnt = 0
    for bi, (s0, n) in enumerate(blks):
        h = hb[bi % 2]; hk = 'h%d' % (bi % 2)
        ph.dma('sp', lambda e, h=h, s0=s0, n=n: e.dma_start(
            out=h[:, :, :n], in_=hT[:, :, s0:s0 + n].rearrange("k p t -> p k t")), writes=[hk])
        for j in jobs:
            m = j.get('m', 128)
            for cb in range(j['ncb']):
                p = pp[cnt % 4]; pk = 'pp%d' % (cnt % 4)
                o = so[j['odt']][cnt % 4]; ok = 'so%s%d' % (str(j['odt'])[-4:], cnt % 4)
                cnt += 1
                for k in range(kchunks):
                    ph.op('pe', lambda e, p=p, k=k, j=j, cb=cb, m=m, h=h, n=n: e.matmul(
                        p[:m, :n], lhsT=wt[:, k, j['woff'] + cb * m:j['woff'] + (cb + 1) * m], rhs=h[:, k, :n],
                        start=(k == 0), stop=(k == kchunks - 1)), reads=['w', hk], writes=[pk])
                ph.op('act', lambda e, p=p, o=o, j=j, cb=cb, m=m, n=n: e.activation(
                    out=o[:m, :n], in_=p[:m, :n], func=j['func'], bias=j['bt'][:m, cb:cb + 1], scale=1.0),
                    reads=['bias'], writes=[pk, ok])
                ph.dma('sp', lambda e, o=o, j=j, cb=cb, m=m, s0=s0, n=n: e.dma_start(
                    out=j['dst'](cb, s0, n), in_=o[:m, :n]), reads=[ok], writes=['out'])
    ph.emit()


def rope_ops(ph, X, cs, O, xkeys, cskey, okey):
    Xv = X[:].rearrange("p (g two d) -> p g two d", two=2, d=32)
    Ov = O[:].rearrange("p (g two d) -> p g two d", two=2, d=32)
    cb_ = bc(cs[:, 0:32].unsqueeze(1), [128, 16, 32])
    sb_ = bc(cs[:, 32:64].unsqueeze(1), [128, 16, 32])
    t1, t2 = ph.rope_tmp
    x1, x2 = Xv[:, :, 0, :], Xv[:, :, 1, :]
    rd = list(xkeys) + [cskey]
    ph.op('pool', lambda e: e.tensor_tensor(out=t1[:], in0=x1, in1=cb_, op=ALU.mult), reads=rd, writes=['rt1'])
    ph.op('pool', lambda e: e.tensor_tensor(out=t2[:], in0=x2, in1=sb_, op=ALU.mult), reads=rd, writes=['rt2'])
    ph.op('pool', lambda e: e.tensor_tensor(out=Ov[:, :, 0, :], in0=t1[:], in1=t2[:], op=ALU.subtract),
          reads=['rt1', 'rt2'], writes=[okey + 'a'])
    ph.op('pool', lambda e: e.tensor_tensor(out=t1[:], in0=x2, in1=cb_, op=ALU.mult), reads=rd, writes=['rt1'])
    ph.op('pool', lambda e: e.tensor_tensor(out=t2[:], in0=x1, in1=sb_, op=ALU.mult), reads=rd, writes=['rt2'])
    ph.op('pool', lambda e: e.tensor_tensor(out=Ov[:, :, 1, :], in0=t1[:], in1=t2[:], op=ALU.add),
          reads=['rt1', 'rt2'], writes=[okey + 'b'])


def phase_gemm_tm(nc, G, name, io, cfg, hT, ntok, w_dram, secs, rope_tab=None):
    ph = Phase(nc, name)
    tot = sum(s['ncols'] for s in secs)
    wt = ph.sb('w', [128, KC, tot], BF16)
    bb = ph.sb('bb', [128, tot])
    off = 0
    wv = w_dram.rearrange("(k p) c -> p k c", p=128)
    for s in secs:
        s['woff'] = off
        for k in range(KC):
            ph.dma('pool', lambda e, k=k, off=off, s=s: e.dma_start(
                out=wt[:, k, off:off + s['ncols']], in_=wv[:, k, s['c0']:s['c0'] + s['ncols']]), writes=['w'])
        ph.dma('sp', lambda e, off=off, s=s: e.dma_start(
            out=bb[:, off:off + s['ncols']], in_=bc(io['b_in'][0:1, s['c0']:s['c0'] + s['ncols']], [128, s['ncols']])),
            writes=['bb'])
        off += s['ncols']
    ntile = ntok // 128
    hb = [ph.sb('h%d' % i, [128, KC, 128], BF16) for i in range(2)]
    pp = [ph.ps('pp%d' % i, [128, 512]) for i in range(4)]
    ptr = [ph.ps('ptr%d' % i, [128, 8, 128], BF16) for i in range(2)]
    obf = [ph.sb('obf%d' % i, [128, 1024], BF16) for i in range(2)]
    of32 = [ph.sb('of%d' % i, [128, 1024]) for i in range(2)]
    cs = [ph.sb('cs%d' % i, [128, 64]) for i in range(2)]
    ph.rope_tmp = (ph.sb('rt1', [128, 16, 32]), ph.sb('rt2', [128, 16, 32]))
    fmst = [ph.sb('fmst%d' % i, [128, 8, 128], BF16) for i in range(2)]
    ropeb = [ph.sb('ropeb%d' % i, [128, 1024], BF16) for i in range(2)]
    cnt = 0
    for t in range(ntile):
        h = hb[t % 2]; hk = 'h%d' % (t % 2)
        ph.dma('sp', lambda e, h=h, t=t: e.dma_start(
            out=h[:], in_=hT[:, :, t * 128:(t + 1) * 128].rearrange("k p t -> p k t")), writes=[hk])
        for si, s in enumerate(secs):
            kind = s['kind']
            ob = obf[(t * len(secs) + si) % 2]; obk = 'obf%d' % ((t * len(secs) + si) % 2)
            of = of32[t % 2]; ofk = 'of%d' % (t % 2)
            for ch in range(s['ncols'] // 512):
                p = pp[cnt % 4]; pk = 'pp%d' % (cnt % 4); cnt += 1
                c = s['woff'] + ch * 512
                for k in range(KC):
                    ph.op('pe', lambda e, p=p, k=k, c=c, h=h: e.matmul(
                        p[:], lhsT=h[:, k, :], rhs=wt[:, k, c:c + 512], start=(k == 0), stop=(k == KC - 1)),
                        reads=['w', hk], writes=[pk])
                dstt = of if kind in ('rope', 'sig') else ob
                dk = ofk if kind in ('rope', 'sig') else obk
                ph.op('dve', lambda e, p=p, c=c, ch=ch, dstt=dstt: e.tensor_tensor(
                    out=dstt[:, ch * 512:(ch + 1) * 512], in0=p[:], in1=bb[:, c:c + 512], op=ALU.add),
                    reads=['bb'], writes=[pk, dk + '_%d' % ch])
            nch = s['ncols'] // 512
            if kind == 'bf16':
                ph.dma('sp', lambda e, ob=ob, s=s, t=t: e.dma_start(out=s['dst'](t), in_=ob[:, :s['ncols']]),
                       reads=[obk + '_%d' % c_ for c_ in range(nch)], writes=['out'])
            elif kind == 'sig':
                ph.op('act', lambda e, of=of: e.activation(out=of[:], in_=of[:], func=AF.Sigmoid),
                      reads=[], writes=[ofk + '_%d' % c_ for c_ in range(nch)])
                ph.dma('sp', lambda e, of=of, s=s, t=t: e.dma_start(out=s['dst'](t), in_=of[:, :s['ncols']]),
                       reads=[ofk + '_%d' % c_ for c_ in range(nch)], writes=['out'])
            elif kind == 'rope':
                c_s = cs[t % 2]; csk = 'cs%d' % (t % 2)
                ph.dma('sp', lambda e, c_s=c_s, t=t: e.dma_start(out=c_s[:], in_=rope_tab[t * 128:(t + 1) * 128, :]),
                       writes=[csk])
                rb = ropeb[t % 2]; rbk = 'ropeb%d' % (t % 2)
                rope_ops(ph, of, c_s, rb, [ofk + '_0', ofk + '_1'], csk, rbk)
                fs = fmst[t % 2]; fk = 'fmst%d' % (t % 2)
                pt_ = ptr[t % 2]; ptk = 'ptr%d' % (t % 2)
                for hh in range(8):
                    ph.op('pe', lambda e, hh=hh, rb=rb, pt_=pt_: e.transpose(
                        out=pt_[:, hh, :], in_=rb[:, hh * 128:(hh + 1) * 128], identity=G['identb'][:]),
                        reads=[rbk + 'a', rbk + 'b'], writes=[ptk])
                ph.op('act', lambda e, fs=fs, pt_=pt_: e.copy(out=fs[:], in_=pt_[:]), writes=[ptk, fk])
                ph.dma('sp', lambda e, fs=fs, s=s, t=t: e.dma_start(out=s['dst'](t), in_=fs[:]),
                       reads=[fk], writes=['out'])
    ph.emit()


def phase_conv(nc, G, io, cfg):
    ph = Phase(nc, 'D')
    NS, NCTX = cfg['NS'], cfg['NCTX']
    blks = blocks(cfg)
    cw = ph.sb('cw', [128, 16, 5]); cbi = ph.sb('cbi', [128, 16]); zt = ph.sb('zt', [128, 16, 2])
    win = [ph.sb('win%d' % i, [128, 516]) for i in range(2)]
    acc = [ph.sb('acc%d' % i, [128, 512]) for i in range(2)]
    sg = [ph.sb('sg%d' % i, [128, 512]) for i in range(2)]
    ob = [ph.sb('ob%d' % i, [128, 512], BF16) for i in range(2)]
    ptr = [ph.ps('ptr%d' % i, [128, 4, 128], BF16) for i in range(2)]
    kst = [ph.sb('kst%d' % i, [128, 4, 128], BF16) for i in range(2)]
    ph.dma('sp', lambda e: e.dma_start(out=cw[:], in_=io['conv_wT']), writes=['cw'])
    ph.dma('sp', lambda e: e.dma_start(out=cbi[:], in_=io['conv_bT']), writes=['cw'])
    ph.op('dve', lambda e: e.memset(zt[:], 0.0), writes=['zt'])
    QK = io['QKpre']
    for g0 in (0, NCTX + 2, NS + 4):
        ph.dma('sp', lambda e, g0=g0: e.dma_start(out=QK[:, :, g0:g0 + 2].rearrange("c p t -> p c t"), in_=zt[:]),
               reads=['zt'], writes=['gap'])
    it = 0
    for cb in range(16):
        h = cb % 8
        scale = 128 ** -0.5 if cb < 8 else 1.0
        dstT = io['QT_m'] if cb < 8 else io['KT_m']
        for (s0, n) in blks:
            b = it % 2; it += 1
            c0 = pcol(cfg, s0)
            w_, a_, s_, o_ = win[b], acc[b], sg[b], ob[b]
            ph.dma('sp', lambda e, w_=w_, cb=cb, c0=c0, n=n: e.dma_start(out=w_[:, :n + 4], in_=QK[cb, :, c0 - 2:c0 + n + 2]),
                   reads=['gap'], writes=['win%d' % b])
            ph.op('dve', lambda e, w_=w_, a_=a_, cb=cb, n=n: e.tensor_scalar(
                out=a_[:, :n], in0=w_[:, 0:n], scalar1=cw[:, cb, 0:1], scalar2=cbi[:, cb:cb + 1], op0=ALU.mult, op1=ALU.add),
                reads=['win%d' % b, 'cw'], writes=['acc%d' % b])
            for j in range(1, 5):
                ph.op('dve', lambda e, w_=w_, a_=a_, cb=cb, n=n, j=j: e.scalar_tensor_tensor(
                    out=a_[:, :n], in0=w_[:, j:j + n], scalar=cw[:, cb, j:j + 1], in1=a_[:, :n], op0=ALU.mult, op1=ALU.add),
                    reads=['win%d' % b, 'cw'], writes=['acc%d' % b])
            ph.op('act', lambda e, a_=a_, s_=s_, n=n: e.activation(out=s_[:, :n], in_=a_[:, :n], func=AF.Sigmoid),
                  reads=['acc%d' % b], writes=['sg%d' % b])
            ph.op('dve', lambda e, a_=a_, s_=s_, o_=o_, n=n, scale=scale: e.scalar_tensor_tensor(
                out=o_[:, :n], in0=a_[:, :n], scalar=scale, in1=s_[:, :n], op0=ALU.mult, op1=ALU.mult),
                reads=['acc%d' % b, 'sg%d' % b], writes=['ob%d' % b])
            ph.dma('sp', lambda e, o_=o_, h=h, s0=s0, n=n, dstT=dstT: e.dma_start(out=dstT[h, :, s0:s0 + n], in_=o_[:, :n]),
                   reads=['ob%d' % b], writes=['out'])
            if cb >= 8:
                nt_ = n // 128
                for ti in range(nt_):
                    ph.op('pe', lambda e, o_=o_, ti=ti, b=b: e.transpose(
                        out=ptr[b][:, ti, :], in_=o_[:, ti * 128:(ti + 1) * 128], identity=G['identb'][:]),
                        reads=['ob%d' % b], writes=['ptr%d' % b])
                ph.op('act', lambda e, b=b, nt_=nt_: e.copy(out=kst[b][:, :nt_, :], in_=ptr[b][:, :nt_, :]),
                      writes=['ptr%d' % b, 'kst%d' % b])
                t0 = s0 // 128
                ph.dma('sp', lambda e, b=b, nt_=nt_, t0=t0, h=h: e.dma_start(
                    out=io['Ktok'][t0:t0 + nt_, :, h, :].rearrange("t p k -> p t k"), in_=kst[b][:, :nt_, :]),
                    reads=['kst%d' % b], writes=['out'])
    ph.emit()


def phase_gates(nc, G, io, cfg):
    ph = Phase(nc, 'E')
    NS, NCTX, NT = cfg['NS'], cfg['NCTX'], cfg['NS'] // 128
    NCT = NCTX // 128
    T = [ph.sb('T%d' % i, [8, NS]) for i in range(5)]
    zer = ph.sb('zer', [8, 1]); one = ph.sb('one', [8, 1])
    ph.op('dve', lambda e: e.memset(zer[:], 0.0), writes=['zer'])
    ph.op('dve', lambda e: e.memset(one[:], 1.0), writes=['one'])
    MP = ph.sb('MP', [8, NT]); MN = ph.sb('MN', [8, NT]); dc = ph.sb('dc', [8, NT]); xd = ph.sb('xd', [8, NT, 8])
    tot = ph.sb('tot', [8, 4])
    tsp = [ph.ps('tsp%d' % i, [128, 16, 8]) for i in range(2)]
    dps = ph.ps('dps', [128, 512])
    tss = ph.sb('tss', [128, NT, 4, 8]); dcs = ph.sb('dcs', [128, NT * 8])
    GT = io['GT']
    segs = [(0, NCTX), (NCTX, NS)]
    LF, P, A, M, E = T
    for d in range(2):
        ti, tf = (0, 1) if d == 0 else (2, 3)
        ph.dma('sp', lambda e, tf=tf: e.dma_start(out=LF[:], in_=GT[tf]), writes=['LF'])
        ph.op('act', lambda e: e.activation(out=LF[:], in_=LF[:], func=AF.Exp, scale=-1.0), writes=['LF'])
        ph.op('act', lambda e: e.activation(out=LF[:], in_=LF[:], func=AF.Ln, bias=1.0, scale=1.0), writes=['LF'])
        ph.op('dve', lambda e: e.tensor_scalar(out=LF[:], in0=LF[:], scalar1=-1.0, scalar2=None, op0=ALU.mult),
              writes=['LF'])
        for (a0, a1) in segs:
            ph.op('dve', lambda e, a0=a0, a1=a1: e.tensor_tensor_scan(
                out=P[:, a0:a1], data0=bc(one[:, 0:1], [8, a1 - a0]), data1=LF[:, a0:a1], initial=0.0,
                op0=ALU.mult, op1=ALU.add), reads=['LF', 'one'], writes=['P'])
        ph.op('dve', lambda e: e.tensor_copy(out=tot[:, 0:1], in_=P[:, NCTX - 1:NCTX]), reads=['P'], writes=['tot'])
        ph.op('dve', lambda e: e.tensor_copy(out=tot[:, 1:2], in_=P[:, NS - 1:NS]), reads=['P'], writes=['tot'])
        ph.op('dve', lambda e: e.tensor_tensor(out=tot[:, 2:3], in0=tot[:, 0:1], in1=tot[:, 1:2], op=ALU.add),
              writes=['tot'])
        if d == 0:
            ph.op('dve', lambda e: e.tensor_scalar(out=P[:, NCTX:NS], in0=P[:, NCTX:NS], scalar1=tot[:, 0:1],
                                                   scalar2=None, op0=ALU.add), reads=['tot'], writes=['P'])
        else:
            ph.op('dve', lambda e: e.tensor_tensor(out=P[:], in0=LF[:], in1=P[:], op=ALU.subtract), reads=['LF'],
                  writes=['P'])
            ph.op('dve', lambda e: e.tensor_scalar(out=P[:, 0:NCTX], in0=P[:, 0:NCTX], scalar1=tot[:, 0:1],
                                                   scalar2=None, op0=ALU.add), reads=['tot'], writes=['P'])
            ph.op('dve', lambda e: e.tensor_scalar(out=P[:, NCTX:NS], in0=P[:, NCTX:NS], scalar1=tot[:, 2:3],
                                                   scalar2=None, op0=ALU.add), reads=['tot'], writes=['P'])
        ph.dma('sp', lambda e, ti=ti: e.dma_start(out=A[:], in_=GT[ti]), writes=['A'])
        ph.op('dve', lambda e: e.tensor_tensor(out=A[:], in0=A[:], in1=P[:], op=ALU.subtract), reads=['P'], writes=['A'])
        if d == 0:
            ph.op('dve', lambda e: e.tensor_tensor_scan(out=M[:], data0=bc(zer[:, 0:1], [8, NS]), data1=A[:], initial=0.0,
                                                        op0=ALU.add, op1=ALU.max), reads=['A', 'zer'], writes=['M'])
        else:
            ph.op('dve', lambda e: e.tensor_tensor_scan(
                out=M[:, 0:NCTX][:, ::-1], data0=bc(zer[:, 0:1], [8, NCTX]), data1=A[:, 0:NCTX][:, ::-1], initial=0.0,
                op0=ALU.add, op1=ALU.max), reads=['A', 'zer'], writes=['M'])
            ph.op('dve', lambda e: e.tensor_tensor_scan(
                out=M[:, NCTX:NS][:, ::-1], data0=bc(zer[:, 0:1], [8, NS - NCTX]), data1=A[:, NCTX:NS][:, ::-1],
                initial=M[:, 0:1], op0=ALU.add, op1=ALU.max), reads=['A', 'zer'], writes=['M'])
        ph.dma('sp', lambda e, d=d: e.dma_start(out=io['MF'][d], in_=M[:]), reads=['M'], writes=['MFout'])
        Mv = M[:].rearrange("h (c t) -> h c t", t=128)
        if d == 0:
            ph.op('dve', lambda e: e.tensor_copy(out=MN[:], in_=Mv[:, :, 127]), reads=['M'], writes=['MN'])
            ph.op('dve', lambda e: e.memset(MP[:, 0:1], 0.0), writes=['MP'])
            ph.op('dve', lambda e: e.tensor_copy(out=MP[:, 1:NT], in_=Mv[:, 0:NT - 1, 127]), reads=['M'], writes=['MP'])
        else:
            ph.op('dve', lambda e: e.tensor_copy(out=MN[:], in_=Mv[:, :, 0]), reads=['M'], writes=['MN'])
            ph.op('dve', lambda e: e.tensor_copy(out=MP[:, 0:NT - 1], in_=Mv[:, 1:NT, 0]), reads=['M'], writes=['MP'])
            ph.op('dve', lambda e: e.memset(MP[:, NCT - 1:NCT], 0.0), writes=['MP'])
            ph.op('dve', lambda e: e.tensor_copy(out=MP[:, NT - 1:NT], in_=M[:, 0:1]), reads=['M'], writes=['MP'])
        ph.op('dve', lambda e: e.tensor_tensor(out=dc[:], in0=MP[:], in1=MN[:], op=ALU.subtract), reads=['MP', 'MN'],
              writes=['dc'])
        ph.op('act', lambda e: e.activation(out=dc[:], in_=dc[:], func=AF.Exp), writes=['dc'])
        ph.op('dve', lambda e: e.tensor_tensor(out=xd[:], in0=bc(dc[:].unsqueeze(2), [8, NT, 8]),
                                               in1=bc(G['sel'][:, :, 0].unsqueeze(1), [8, NT, 8]), op=ALU.mult),
              reads=['dc'], writes=['xd'])
        xdf = xd[:].rearrange("j c h -> j (c h)")
        for c0 in range(0, NT * 8, 512):
            n = min(512, NT * 8 - c0)
            ph.op('pe', lambda e, c0=c0, n=n: e.matmul(dps[:, :n], lhsT=G['ones'][0:8, :], rhs=xdf[:, c0:c0 + n],
                                                        start=True, stop=True), reads=['xd'], writes=['dps'])
            ph.op('act', lambda e, c0=c0, n=n: e.copy(out=dcs[:, c0:c0 + n], in_=dps[:, :n]), writes=['dps', 'dcs'])
        ph.dma('sp', lambda e, d=d: e.dma_start(out=io['DEC'][d], in_=dcs[:]), reads=['dcs'], writes=['DECout'])
        MPb = bc(MP[:].unsqueeze(2), [8, NT, 128]); MNb = bc(MN[:].unsqueeze(2), [8, NT, 128])
        Ev = E[:].rearrange("h (c t) -> h c t", t=128)
        Av = A[:].rearrange("h (c t) -> h c t", t=128)
        for q in range(4):
            if q == 0:
                src = A; rk = ['A']
            elif q == 1:
                ph.op('dve', lambda e: e.tensor_tensor(out=Ev, in0=MPb, in1=Mv, op=ALU.subtract), reads=['M', 'MP'],
                      writes=['E'])
                ph.op('act', lambda e: e.activation(out=E[:], in_=E[:], func=AF.Exp), writes=['E'])
                src = E; rk = ['E']
            elif q == 2:
                ph.op('dve', lambda e: e.tensor_tensor(out=E[:], in0=P[:], in1=M[:], op=ALU.add), reads=['M', 'P'],
                      writes=['E'])
                ph.op('act', lambda e: e.activation(out=E[:], in_=E[:], func=AF.Exp, scale=-1.0), writes=['E'])
                src = E; rk = ['E']
            else:
                ph.op('dve', lambda e: e.tensor_tensor(out=Ev, in0=Av, in1=MNb, op=ALU.subtract), reads=['A', 'MN'],
                      writes=['E'])
                ph.op('act', lambda e: e.activation(out=E[:], in_=E[:], func=AF.Exp), writes=['E'])
                src = E; rk = ['E']
            for c0 in range(0, NT, 16):
                ncc = min(16, NT - c0)
                pb = (c0 // 16) % 2
                for c in range(c0, c0 + ncc):
                    ph.op('pe', lambda e, c=c, c0=c0, pb=pb, src=src: e.transpose(
                        out=tsp[pb][:, c - c0, :], in_=src[:, c * 128:(c + 1) * 128], identity=G['ident'][0:8, 0:8]),
                        reads=rk, writes=['tsp%d' % pb])
                ph.op('act', lambda e, c0=c0, ncc=ncc, pb=pb, q=q: e.copy(out=tss[:, c0:c0 + ncc, q, :],
                                                                          in_=tsp[pb][:, :ncc, :]),
                      writes=['tsp%d' % pb, 'tss'])
        ph.dma('sp', lambda e, d=d: e.dma_start(out=io['TS'][d], in_=tss[:].rearrange("p c q h -> p (c q h)")),
               reads=['tss'], writes=['TSout'])
    ph.emit()


def phase_scan(nc, G, io, cfg):
    ph = Phase(nc, 'H')
    NS, NCTX, NT = cfg['NS'], cfg['NCTX'], cfg['NS'] // 128
    NCT = NCTX // 128
    S = ph.sb('S', [128, 16, 129]); Sb = ph.sb('Sb', [128, 16, 129], BF16)
    ph.op('dve', lambda e: e.memset(S[:], 0.0), writes=['S%d' % i for i in range(16)])
    ph.op('pool', lambda e: e.memset(Sb[:], 0.0), writes=['Sb%d' % i for i in range(16)])
    TS = [ph.sb('TS%d' % d, [128, NT, 4, 8]) for d in range(2)]
    DEC = [ph.sb('DEC%d' % d, [128, NT, 8]) for d in range(2)]
    MF = [ph.sb('MF%d' % d, [8, NS]) for d in range(2)]
    for d in range(2):
        ph.dma('sp', lambda e, d=d: e.dma_start(out=TS[d][:].rearrange("p c q h -> p (c q h)"), in_=io['TS'][d]),
               writes=['TS'])
        ph.dma('sp', lambda e, d=d: e.dma_start(out=DEC[d][:].rearrange("p c h -> p (c h)"), in_=io['DEC'][d]),
               writes=['TS'])
        ph.dma('sp', lambda e, d=d: e.dma_start(out=MF[d][:], in_=io['MF'][d]), writes=['TS'])
    NB = 2
    QT = [[ph.sb('QT%d_%d' % (d, i), [128, 8, 128], BF16) for i in range(NB)] for d in range(2)]
    KT = [[ph.sb('KT%d_%d' % (d, i), [128, 8, 128], BF16) for i in range(NB)] for d in range(2)]
    Kk = [[ph.sb('Kk%d_%d' % (d, i), [128, 8, 128], BF16) for i in range(NB)] for d in range(2)]
    Va = [[ph.sb('Va%d_%d' % (d, i), [128, 8, 129], BF16) for i in range(NB)] for d in range(2)]
    for d in range(2):
        for i in range(NB):
            ph.op('pool', lambda e, d=d, i=i: e.memset(Va[d][i][:, :, 128:129], 1.0), writes=['Va%d_%d' % (d, i)])
    ho = [[ph.sb('ho%d_%d' % (d, i), [128, 8, 128]) for i in range(2)] for d in range(2)]
    psA = [ph.ps('psA%d' % i, [128, 128]) for i in range(2)]
    psB = [ph.ps('psB%d' % i, [128, 128]) for i in range(2)]
    psC = ph.ps('psC', [128, 129]); psD = ph.ps('psD', [128, 129])
    psE = [ph.ps('psE%d' % i, [128, 129]) for i in range(2)]
    Dm = [ph.sb('Dm%d' % i, [128, 128]) for i in range(2)]
    SD = [ph.sb('SD%d' % i, [128, 128], BF16) for i in range(2)]
    isb = [ph.sb('isb%d' % i, [128, 129]) for i in range(2)]
    tt = [ph.sb('tt%d' % i, [128, 129]) for i in range(2)]
    dn = [ph.sb('dn%d' % i, [128, 1]) for i in range(2)]
    VW = [ph.sb('VW%d' % i, [128, 129], BF16) for i in range(2)]
    order = [list(range(NT)), list(range(NCT - 1, -1, -1)) + list(range(NT - 1, NCT - 1, -1))]
    masks = [G['maskf'], G['maskb']]
    cnt = 0
    for step in range(NT):
        for d in range(2):
            c = order[d][step]
            bi = step % NB
            bk = '%d_%d' % (d, bi)
            lat = c >= NCT
            sl = slice(c * 128, (c + 1) * 128)
            if lat:
                ph.dma('sp', lambda e, d=d, bi=bi, sl=sl: e.dma_start(
                    out=QT[d][bi][:], in_=io['QT_m'][:, :, sl].rearrange("h p t -> p h t")), writes=['QT' + bk])
                ph.dma('sp', lambda e, d=d, bi=bi, sl=sl: e.dma_start(
                    out=KT[d][bi][:], in_=io['KT_m'][:, :, sl].rearrange("h p t -> p h t")), writes=['KT' + bk])
            ph.dma('sp', lambda e, d=d, bi=bi, c=c: e.dma_start(out=Kk[d][bi][:], in_=io['Ktok'][c]), writes=['Kk' + bk])
            ph.dma('sp', lambda e, d=d, bi=bi, c=c: e.dma_start(out=Va[d][bi][:, :, 0:128], in_=io['Vtok'][c]),
                   writes=['Va' + bk])
            hob = ho[d][step % 2]; hok = 'ho%d_%d' % (d, step % 2)
            for h in range(8):
                i2 = cnt % 2; cnt += 1
                sk = d * 8 + h
                if lat:
                    ph.op('pe', lambda e, d=d, bi=bi, h=h, i2=i2: e.matmul(
                        psA[i2][:], lhsT=KT[d][bi][:, h, :], rhs=QT[d][bi][:, h, :], start=True, stop=True),
                        reads=['KT' + bk, 'QT' + bk], writes=['psA%d' % i2])
                    ph.op('pe', lambda e, d=d, h=h, i2=i2, sl=sl: e.matmul(
                        psB[i2][:], lhsT=G['sel'][:, h, :], rhs=MF[d][:, sl], start=True, stop=False),
                        reads=['TS'], writes=['psB%d' % i2])
                    ph.op('pe', lambda e, d=d, i2=i2: e.matmul(
                        psB[i2][:], lhsT=G['ident'][:], rhs=masks[d][:], start=False, stop=True),
                        reads=[], writes=['psB%d' % i2])
                    ph.op('act', lambda e, d=d, c=c, h=h, i2=i2: e.activation(
                        out=Dm[i2][:], in_=psB[i2][:], func=AF.Exp, scale=-1.0, bias=TS[d][:, c, 0, h:h + 1]),
                        reads=['TS'], writes=['psB%d' % i2, 'Dm%d' % i2])
                    ph.op('dve', lambda e, i2=i2: e.tensor_tensor(out=SD[i2][:], in0=psA[i2][:], in1=Dm[i2][:], op=ALU.mult),
                          reads=['Dm%d' % i2], writes=['psA%d' % i2, 'SD%d' % i2])
                    ph.op('pe', lambda e, d=d, bi=bi, h=h, i2=i2: e.matmul(
                        psC[:], lhsT=SD[i2][:], rhs=Va[d][bi][:, h, :], start=True, stop=True),
                        reads=['SD%d' % i2, 'Va' + bk], writes=['psC'])
                    ph.op('pe', lambda e, d=d, bi=bi, h=h, sk=sk: e.matmul(
                        psD[:], lhsT=QT[d][bi][:, h, :], rhs=Sb[:, sk, :], start=True, stop=True),
                        reads=['QT' + bk, 'Sb%d' % sk], writes=['psD'])
                    ph.op('act', lambda e, d=d, c=c, h=h, i2=i2: e.activation(
                        out=isb[i2][:], in_=psD[:], func=AF.Identity, scale=TS[d][:, c, 1, h:h + 1]),
                        reads=['TS'], writes=['psD', 'isb%d' % i2])
                    ph.op('dve', lambda e, i2=i2: e.tensor_tensor(out=tt[i2][:], in0=psC[:], in1=isb[i2][:], op=ALU.add),
                          reads=['isb%d' % i2], writes=['psC', 'tt%d' % i2])
                    ph.op('dve', lambda e, i2=i2: e.tensor_scalar(
                        out=dn[i2][:], in0=tt[i2][:, 128:129], scalar1=-1.0, scalar2=tt[i2][:, 128:129],
                        op0=ALU.mult, op1=ALU.max), reads=['tt%d' % i2], writes=['dn%d' % i2])
                    ph.op('dve', lambda e, d=d, c=c, h=h, i2=i2: e.tensor_scalar(
                        out=dn[i2][:], in0=dn[i2][:], scalar1=TS[d][:, c, 2, h:h + 1], scalar2=None,
                        op0=ALU.max), reads=['TS'], writes=['dn%d' % i2])
                    ph.op('dve', lambda e, i2=i2: e.reciprocal(out=dn[i2][:], in_=dn[i2][:]), writes=['dn%d' % i2])
                    ph.op('dve', lambda e, h=h, i2=i2, hob=hob: e.tensor_scalar(
                        out=hob[:, h, :], in0=tt[i2][:, 0:128], scalar1=dn[i2][:, 0:1], scalar2=None, op0=ALU.mult),
                        reads=['tt%d' % i2, 'dn%d' % i2], writes=[hok + '_%d' % h])
                ph.op('dve', lambda e, d=d, bi=bi, c=c, h=h, i2=i2: e.tensor_scalar(
                    out=VW[i2][:], in0=Va[d][bi][:, h, :], scalar1=TS[d][:, c, 3, h:h + 1], scalar2=None, op0=ALU.mult),
                    reads=['Va' + bk, 'TS'], writes=['VW%d' % i2])
                ph.op('pe', lambda e, d=d, bi=bi, h=h, i2=i2: e.matmul(
                    psE[i2][:], lhsT=Kk[d][bi][:, h, :], rhs=VW[i2][:], start=True, stop=True),
                    reads=['Kk' + bk, 'VW%d' % i2], writes=['psE%d' % i2])
                ph.op('dve', lambda e, d=d, c=c, h=h, i2=i2, sk=sk: e.scalar_tensor_tensor(
                    out=S[:, sk, :], in0=S[:, sk, :], scalar=DEC[d][:, c, h:h + 1], in1=psE[i2][:], op0=ALU.mult, op1=ALU.add),
                    reads=['TS'], writes=['psE%d' % i2, 'S%d' % sk])
                ph.op('act', lambda e, sk=sk: e.copy(out=Sb[:, sk, :], in_=S[:, sk, :]), reads=['S%d' % sk],
                      writes=['Sb%d' % sk])
            if lat:
                dst = io['Hf'] if d == 0 else io['Hb']
                r0 = (c - NCT) * 128
                ph.dma('sp', lambda e, hob=hob, dst=dst, r0=r0: e.dma_start(
                    out=dst[r0:r0 + 128, :], in_=hob[:].rearrange("p h v -> p (h v)")),
                    reads=[hok + '_%d' % h for h in range(8)], writes=['out'])
    ph.emit()


def phase_attn(nc, G, io, cfg):
    ph = Phase(nc, 'T')
    NS, NO, NT = cfg['NS'], cfg['NO'], cfg['NS'] // 128
    scale = 64 ** -0.5
    KTt = [ph.sb('KT%d' % i, [128, NS], BF16) for i in range(2)]
    Vt = [ph.sb('V%d' % i, [128, NT, 129], BF16) for i in range(2)]
    for i in range(2):
        ph.op('pool', lambda e, i=i: e.memset(Vt[i][:, :, 128:129], 1.0), writes=['V%d' % i])
    QTt = [ph.sb('Q%d' % i, [128, 256], BF16) for i in range(2)]
    ps1 = [ph.ps('ps1_%d' % i, [128, 256]) for i in range(2)]
    ps2 = [ph.ps('ps2_%d' % i, [128, 256]) for i in range(2)]
    psO = [ph.ps('psO%d' % i, [128, 129]) for i in range(4)]
    Pt = [ph.sb('Pt%d' % i, [128, 2, 256], BF16) for i in range(2)]
    anw = ph.sb('anw', [128, 128])
    ph.dma('sp', lambda e: e.dma_start(out=anw[:], in_=bc(io['a_norm_w'], [128, 128])), writes=['anw'])
    ph.op('dve', lambda e: e.tensor_scalar(out=anw[:], in0=anw[:], scalar1=1.0 - LAM_INIT, scalar2=None, op0=ALU.mult),
          writes=['anw'])
    r = ph.sb('r', [128, 4]); o = [ph.sb('o%d' % i, [128, 128]) for i in range(2)]
    junk = ph.sb('junk', [128, 128]); ss = ph.sb('ss', [128, 2])
    yb = [ph.sb('yb%d' % i, [128, 128]) for i in range(2)]
    yst = [ph.sb('yst%d' % i, [128, 256], BF16) for i in range(2)]
    it = 0
    for h in range(8):
        hb = h % 2
        ph.dma('sp', lambda e, h=h, hb=hb: e.dma_start(out=KTt[hb][:], in_=io['KaT'][h]), writes=['KT%d' % hb])
        for t0 in range(0, NT, 16):
            t1 = min(NT, t0 + 16)
            ph.dma('sp', lambda e, h=h, hb=hb, t0=t0, t1=t1: e.dma_start(
                out=Vt[hb][:, t0:t1, 0:128], in_=io['Va'][t0:t1, :, h, :].rearrange("t p v -> p t v")), writes=['V%d' % hb])
        for qb in range(NO // 256):
            qi = it % 2; it += 1
            ph.dma('sp', lambda e, h=h, qb=qb, qi=qi: e.dma_start(out=QTt[qi][:], in_=io['QaT'][h, :, qb * 256:(qb + 1) * 256]),
                   writes=['Q%d' % qi])
            for kt in range(NT):
                b = kt % 2
                ks = slice(kt * 128, (kt + 1) * 128)
                ph.op('pe', lambda e, hb=hb, qi=qi, ks=ks, b=b: e.matmul(
                    ps1[b][:], lhsT=KTt[hb][0:64, ks], rhs=QTt[qi][0:64, :], start=True, stop=True),
                    reads=['KT%d' % hb, 'Q%d' % qi], writes=['ps1_%d' % b])
                ph.op('pe', lambda e, hb=hb, qi=qi, ks=ks, b=b: e.matmul(
                    ps2[b][:], lhsT=KTt[hb][64:128, ks], rhs=QTt[qi][64:128, :], start=True, stop=True),
                    reads=['KT%d' % hb, 'Q%d' % qi], writes=['ps2_%d' % b])
                ph.op('act', lambda e, b=b: e.activation(out=Pt[b][:, 0, :], in_=ps1[b][:], func=AF.Exp, scale=scale),
                      writes=['ps1_%d' % b, 'Pt%d' % b])
                ph.op('act', lambda e, b=b: e.activation(out=Pt[b][:, 1, :], in_=ps2[b][:], func=AF.Exp, scale=scale),
                      writes=['ps2_%d' % b, 'Pt%d' % b])
                for mp in range(2):
                    for qs in range(2):
                        oi = mp * 2 + qs
                        ph.op('pe', lambda e, b=b, qs=qs, oi=oi, hb=hb, kt=kt, mp=mp: e.matmul(
                            psO[oi][:], lhsT=Pt[b][:, mp, qs * 128:(qs + 1) * 128], rhs=Vt[hb][:, kt, :],
                            start=(kt == 0), stop=(kt == NT - 1)),
                            reads=['Pt%d' % b, 'V%d' % hb], writes=['psO%d' % oi])
            ysb = yst[qi]; ysk = 'yst%d' % qi
            for qs in range(2):
                o_ = o[qs]; ok = 'o%d' % qs
                ph.op('dve', lambda e, qs=qs: e.reciprocal(out=r[:, qs:qs + 1], in_=psO[qs][:, 128:129]),
                      writes=['psO%d' % qs, 'r%d' % qs])
                ph.op('dve', lambda e, qs=qs: e.reciprocal(out=r[:, 2 + qs:3 + qs], in_=psO[2 + qs][:, 128:129]),
                      writes=['psO%d' % (2 + qs), 'r%d' % (2 + qs)])
                ph.op('dve', lambda e, qs=qs: e.tensor_tensor(out=r[:, 2 + qs:3 + qs], in0=r[:, 2 + qs:3 + qs],
                                                              in1=G['nlam'][:], op=ALU.mult), writes=['r%d' % (2 + qs)])
                ph.op('dve', lambda e, qs=qs, o_=o_: e.tensor_scalar(out=o_[:], in0=psO[qs][:, 0:128], scalar1=r[:, qs:qs + 1],
                                                                     scalar2=None, op0=ALU.mult),
                      reads=['r%d' % qs], writes=['psO%d' % qs, ok])
                ph.op('dve', lambda e, qs=qs, o_=o_: e.scalar_tensor_tensor(
                    out=o_[:], in0=psO[2 + qs][:, 0:128], scalar=r[:, 2 + qs:3 + qs], in1=o_[:], op0=ALU.mult, op1=ALU.add),
                    reads=['r%d' % (2 + qs)], writes=['psO%d' % (2 + qs), ok])
                ph.op('act', lambda e, qs=qs, o_=o_: e.activation(out=junk[:], in_=o_[:], func=AF.Square,
                                                                  accum_out=ss[:, qs:qs + 1]),
                      reads=[ok], writes=['junk', 'ss%d' % qs])
                ph.op('dve', lambda e, qs=qs: e.tensor_scalar(out=ss[:, qs:qs + 1], in0=ss[:, qs:qs + 1], scalar1=1.0 / 128,
                                                              scalar2=EPS, op0=ALU.mult, op1=ALU.add), writes=['ss%d' % qs])
                ph.op('act', lambda e, qs=qs: e.activation(out=ss[:, qs:qs + 1], in_=ss[:, qs:qs + 1], func=AF.Sqrt),
                      writes=['ss%d' % qs])
                ph.op('dve', lambda e, qs=qs: e.reciprocal(out=ss[:, qs:qs + 1], in_=ss[:, qs:qs + 1]), writes=['ss%d' % qs])
                ph.op('dve', lambda e, qs=qs, o_=o_: e.scalar_tensor_tensor(
                    out=yb[qs][:], in0=o_[:], scalar=ss[:, qs:qs + 1], in1=anw[:], op0=ALU.mult, op1=ALU.mult),
                    reads=[ok, 'ss%d' % qs, 'anw'], writes=['yb%d' % qs])
            for qs in range(2):
                ph.op('pe', lambda e, qs=qs: e.transpose(out=psO[qs][:, 0:128], in_=yb[qs][:], identity=G['ident'][:]),
                      reads=['yb%d' % qs], writes=['psO%d' % qs])
                ph.op('act', lambda e, ysb=ysb, qs=qs: e.copy(out=ysb[:, qs * 128:(qs + 1) * 128], in_=psO[qs][:, 0:128]),
                      writes=['psO%d' % qs, ysk])
            ph.dma('sp', lambda e, ysb=ysb, h=h, qb=qb: e.dma_start(out=io['ydT'][h, :, qb * 256:(qb + 1) * 256], in_=ysb[:]),
                   reads=[ysk], writes=['out'])
    ph.emit()


def phase_ym(nc, G, io, cfg):
    ph = Phase(nc, 'Y')
    NO = cfg['NO']
    idx = ph.sb('idx', [128, NO // 128], I32)
    ph.dma('sp', lambda e: e.dma_start(out=idx[:], in_=io['own_idx']), writes=['idx'])
    mw = ph.sb('mw', [128, 1024])
    ph.dma('sp', lambda e: e.dma_start(out=mw[:], in_=bc(io['m_norm_w'], [128, 1024])), writes=['mw'])
    hf = [ph.sb('hf%d' % i, [128, 1024]) for i in range(2)]
    hbt = [ph.sb('hb%d' % i, [128, 1024]) for i in range(2)]
    zt = [ph.sb('z%d' % i, [128, 1024]) for i in range(2)]
    sq = ph.sb('sq', [128, 1024]); ssh = ph.sb('ssh', [128, 8])
    yb = [ph.sb('yb%d' % i, [128, 1024], BF16) for i in range(2)]
    ptr = ph.ps('ptr', [128, 8, 128], BF16)
    yst = [ph.sb('yst%d' % i, [128, 8, 128], BF16) for i in range(2)]
    for t in range(NO // 128):
        b = t % 2
        ph.dma('pool', lambda e, b=b, t=t: e.indirect_dma_start(
            out=hf[b][:], out_offset=None, in_=io['Hf'][:, :],
            in_offset=bass.IndirectOffsetOnAxis(ap=idx[:, t:t + 1], axis=0)), reads=['idx'], writes=['hf%d' % b])
        ph.dma('pool', lambda e, b=b, t=t: e.indirect_dma_start(
            out=hbt[b][:], out_offset=None, in_=io['Hb'][:, :],
            in_offset=bass.IndirectOffsetOnAxis(ap=idx[:, t:t + 1], axis=0)), reads=['idx'], writes=['hb%d' % b])
        ph.dma('sp', lambda e, b=b, t=t: e.dma_start(out=zt[b][:], in_=io['Zo'][t * 128:(t + 1) * 128, :]), writes=['z%d' % b])
        ph.op('dve', lambda e, b=b: e.tensor_tensor(out=hf[b][:], in0=hf[b][:], in1=hbt[b][:], op=ALU.add),
              reads=['hb%d' % b], writes=['hf%d' % b])
        ph.op('pool', lambda e, b=b: e.tensor_tensor(out=sq[:], in0=hf[b][:], in1=hf[b][:], op=ALU.mult),
              reads=['hf%d' % b], writes=['sq'])
        ph.op('dve', lambda e: e.tensor_reduce(out=ssh[:], in_=sq[:].rearrange("p (h v) -> p h v", v=128), axis=AX.X,
                                               op=ALU.add), reads=['sq'], writes=['ssh'])
        ph.op('dve', lambda e: e.tensor_scalar(out=ssh[:], in0=ssh[:], scalar1=1.0 / 128, scalar2=EPS, op0=ALU.mult,
                                               op1=ALU.add), writes=['ssh'])
        ph.op('act', lambda e: e.activation(out=ssh[:], in_=ssh[:], func=AF.Sqrt), writes=['ssh'])
        ph.op('dve', lambda e: e.reciprocal(out=ssh[:], in_=ssh[:]), writes=['ssh'])
        ph.op('dve', lambda e, b=b: e.tensor_tensor(
            out=hf[b][:].rearrange("p (h v) -> p h v", v=128), in0=hf[b][:].rearrange("p (h v) -> p h v", v=128),
            in1=bc(ssh[:].unsqueeze(2), [128, 8, 128]), op=ALU.mult), reads=['ssh'], writes=['hf%d' % b])
        ph.op('pool', lambda e, b=b: e.tensor_tensor(out=zt[b][:], in0=zt[b][:], in1=mw[:], op=ALU.mult), reads=['mw'],
              writes=['z%d' % b])
        ph.op('dve', lambda e, b=b: e.tensor_tensor(out=yb[b][:], in0=hf[b][:], in1=zt[b][:], op=ALU.mult),
              reads=['hf%d' % b, 'z%d' % b], writes=['yb%d' % b])
        for hh in range(8):
            ph.op('pe', lambda e, b=b, hh=hh: e.transpose(out=ptr[:, hh, :], in_=yb[b][:, hh * 128:(hh + 1) * 128],
                                                          identity=G['identb'][:]), reads=['yb%d' % b], writes=['ptr'])
        ph.op('act', lambda e, b=b: e.copy(out=yst[b][:], in_=ptr[:]), writes=['ptr', 'yst%d' % b])
        ph.dma('sp', lambda e, b=b, t=t: e.dma_start(
            out=io['ymT'][:, :, t * 128:(t + 1) * 128].rearrange("h p t -> p h t"), in_=yst[b][:]),
            reads=['yst%d' % b], writes=['out'])
    ph.emit()


def phase_merge(nc, G, io, cfg):
    ph = Phase(nc, 'J')
    NO = cfg['NO']
    wa = ph.sb('wa', [128, 8, 2048], BF16); wb = ph.sb('wb', [128, 8, 2048], BF16)
    load_w(ph, wa, 'wa', io['w_pa'], 0, 2048)
    load_w(ph, wb, 'wb', io['w_pb'], 0, 2048)
    ym = [ph.sb('ym%d' % i, [128, 8, 512], BF16) for i in range(2)]
    yd = [ph.sb('yd%d' % i, [128, 8, 512], BF16) for i in range(2)]
    ga = [ph.sb('ga%d' % i, [128, 512], BF16) for i in range(2)]
    gb = [ph.sb('gb%d' % i, [128, 512], BF16) for i in range(2)]
    pa = [ph.ps('pa%d' % i, [128, 512]) for i in range(2)]
    pb = [ph.ps('pb%d' % i, [128, 512]) for i in range(2)]
    t1 = [ph.sb('t1_%d' % i, [128, 512]) for i in range(2)]
    t2 = [ph.sb('t2_%d' % i, [128, 512]) for i in range(2)]
    mo = [ph.sb('mo%d' % i, [128, 512], BF16) for i in range(2)]
    it = 0
    for bi in range(NO // 512):
        bb = bi % 2
        ts_ = slice(bi * 512, (bi + 1) * 512)
        ph.dma('sp', lambda e, bb=bb, ts_=ts_: e.dma_start(out=ym[bb][:], in_=io['ymT'][:, :, ts_].rearrange("k p t -> p k t")),
               writes=['ym%d' % bb])
        ph.dma('sp', lambda e, bb=bb, ts_=ts_: e.dma_start(out=yd[bb][:], in_=io['ydT'][:, :, ts_].rearrange("k p t -> p k t")),
               writes=['yd%d' % bb])
        for cb in range(16):
            i = it % 2; it += 1
            cs_ = slice(cb * 128, (cb + 1) * 128)
            ph.dma('sp', lambda e, i=i, cb=cb, ts_=ts_: e.dma_start(out=ga[i][:], in_=io['GaT'][cb, :, ts_]), writes=['ga%d' % i])
            ph.dma('sp', lambda e, i=i, cb=cb, ts_=ts_: e.dma_start(out=gb[i][:], in_=io['GbT'][cb, :, ts_]), writes=['gb%d' % i])
            for k in range(8):
                ph.op('pe', lambda e, i=i, k=k, cs_=cs_, bb=bb: e.matmul(pa[i][:], lhsT=wa[:, k, cs_], rhs=ym[bb][:, k, :],
                                                                       start=(k == 0), stop=(k == 7)),
                      reads=['wa', 'ym%d' % bb], writes=['pa%d' % i])
            for k in range(8):
                ph.op('pe', lambda e, i=i, k=k, cs_=cs_, bb=bb: e.matmul(pb[i][:], lhsT=wb[:, k, cs_], rhs=yd[bb][:, k, :],
                                                                       start=(k == 0), stop=(k == 7)),
                      reads=['wb', 'yd%d' % bb], writes=['pb%d' % i])
            ph.op('dve', lambda e, i=i: e.tensor_tensor(out=t1[i][:], in0=pa[i][:], in1=ga[i][:], op=ALU.mult),
                  reads=['ga%d' % i], writes=['pa%d' % i, 't1_%d' % i])
            ph.op('dve', lambda e, i=i: e.tensor_tensor(out=t2[i][:], in0=pb[i][:], in1=gb[i][:], op=ALU.mult),
                  reads=['gb%d' % i], writes=['pb%d' % i, 't2_%d' % i])
            ph.op('pool', lambda e, i=i: e.tensor_tensor(out=mo[i][:], in0=t1[i][:], in1=t2[i][:], op=ALU.add),
                  reads=['t1_%d' % i, 't2_%d' % i], writes=['mo%d' % i])
            ph.dma('sp', lambda e, i=i, cb=cb, ts_=ts_: e.dma_start(out=io['mT'][cb, :, ts_], in_=mo[i][:]),
                   reads=['mo%d' % i], writes=['out'])
    ph.emit()


def phase_wout(nc, G, io, cfg):
    ph = Phase(nc, 'W')
    NO = cfg['NO']
    wo = ph.sb('wo', [128, 16, 2048], BF16)
    load_w(ph, wo, 'wo', io['w_out'], 0, 2048)
    mt = [ph.sb('mt%d' % i, [128, 16, 128], BF16) for i in range(2)]
    xt = [ph.sb('xt%d' % i, [128, 2048]) for i in range(2)]
    pp = [ph.ps('pp%d' % i, [128, 512]) for i in range(4)]
    tm = [ph.sb('tm%d' % i, [128, 512]) for i in range(2)]
    it = 0
    for t in range(NO // 128):
        b = t % 2
        ts_ = slice(t * 128, (t + 1) * 128)
        ph.dma('sp', lambda e, b=b, ts_=ts_: e.dma_start(out=mt[b][:], in_=io['mT'][:, :, ts_].rearrange("k p t -> p k t")),
               writes=['mt%d' % b])
        ph.dma('sp', lambda e, b=b, ts_=ts_: e.dma_start(out=xt[b][:], in_=io['xo'][ts_, :]), reads=[], writes=['xt%d' % b])
        for ch in range(4):
            i = it % 4; j = it % 2; it += 1
            cs_ = slice(ch * 512, (ch + 1) * 512)
            for k in range(16):
                ph.op('pe', lambda e, i=i, k=k, cs_=cs_, b=b: e.matmul(pp[i][:], lhsT=mt[b][:, k, :], rhs=wo[:, k, cs_],
                                                                      start=(k == 0), stop=(k == 15)),
                      reads=['wo', 'mt%d' % b], writes=['pp%d' % i])
            ph.op('dve', lambda e, i=i, j=j, cs_=cs_: e.tensor_tensor(out=tm[j][:], in0=pp[i][:], in1=G['g1_bc'][:, cs_], op=ALU.mult),
                  writes=['pp%d' % i, 'tm%d' % j])
            ph.op('pool', lambda e, j=j, b=b, cs_=cs_: e.tensor_tensor(out=xt[b][:, cs_], in0=xt[b][:, cs_], in1=tm[j][:], op=ALU.add),
                  reads=['tm%d' % j], writes=['xt%d' % b])
        ph.dma('sp', lambda e, b=b, ts_=ts_: e.dma_start(out=io['x1'][ts_, :], in_=xt[b][:]), reads=['xt%d' % b], writes=['out'])
    ph.emit()


def phase_peer_sel(nc, G, io, cfg):
    ph = Phase(nc, 'S')
    NO = cfg['NO']
    skf = ph.sb('skf', [128, 16, 128]); skb = ph.sb('skb', [128, 16, 128], BF16)
    ph.dma('sp', lambda e: e.dma_start(out=skf[:], in_=io['skT']), writes=['skf'])
    ph.op('dve', lambda e: e.tensor_copy(out=skb[:], in_=skf[:]), reads=['skf'], writes=['skb'])
    qt = [ph.sb('qt%d' % i, [128, 16, 128], BF16) for i in range(2)]
    pS = [ph.ps('pS%d' % i, [128, 4, 128]) for i in range(4)]
    Ssb = ph.sb('Ssb', [128, 16, 128]); wk = ph.sb('wk', [128, 128]); v = ph.sb('v', [128, 16, 16])
    cand = ph.sb('cand', [128, 8, 256]); wk2 = ph.sb('wk2', [128, 256]); ts = ph.sb('ts', [128, 8, 16])
    ex = ph.sb('ex', [128, 8, 16]); Z = ph.sb('Z', [128, 8]); th = ph.sb('th', [128, 8])
    E1 = ph.sb('E1', [128, 8, 128]); E2 = ph.sb('E2', [128, 8, 128]); E1p = ph.sb('E1p', [128, 32, 8, 4])
    v4 = v[:].rearrange("p (h two) k -> p h two k", two=2)
    S4 = Ssb[:].rearrange("p (h two) n -> p h two n", two=2)
    for t in range(NO // 128):
        b = t % 2
        ts_ = slice(t * 128, (t + 1) * 128)
        ph.dma('sp', lambda e, b=b, ts_=ts_: e.dma_start(out=qt[b][:], in_=io['qT'][:, :, ts_].rearrange("k p t -> p k t")),
               writes=['qt%d' % b])
        for g in range(4):
            for j in range(4):
                hp = g * 4 + j
                ph.op('pe', lambda e, b=b, g=g, j=j, hp=hp: e.matmul(pS[g][:, j, :], lhsT=qt[b][:, hp, :], rhs=skb[:, hp, :],
                                                                   start=True, stop=True),
                      reads=['qt%d' % b, 'skb'], writes=['pS%d' % g])
            ph.op('act', lambda e, g=g: e.copy(out=Ssb[:, g * 4:(g + 1) * 4, :], in_=pS[g][:]), writes=['pS%d' % g, 'Ssb%d' % g])
        for hp in range(16):
            g = hp // 4
            ph.op('dve', lambda e, hp=hp: e.max(out=v[:, hp, 0:8], in_=Ssb[:, hp, :]), reads=['Ssb%d' % g], writes=['v'])
            ph.op('dve', lambda e, hp=hp: e.match_replace(out=wk[:], in_to_replace=v[:, hp, 0:8], in_values=Ssb[:, hp, :],
                                                          imm_value=-1e30), reads=['Ssb%d' % g], writes=['v', 'wk'])
            ph.op('dve', lambda e, hp=hp: e.max(out=v[:, hp, 8:16], in_=wk[:]), writes=['v', 'wk'])
        ph.op('dve', lambda e: e.tensor_tensor(
            out=cand[:].rearrange("p h (i j) -> p h i j", j=16), in0=bc(v4[:, :, 0, :].unsqueeze(3), [128, 8, 16, 16]),
            in1=bc(v4[:, :, 1, :].unsqueeze(2), [128, 8, 16, 16]), op=ALU.add), reads=['v'], writes=['cand'])
        for h in range(8):
            ph.op('dve', lambda e, h=h: e.max(out=ts[:, h, 0:8], in_=cand[:, h, :]), reads=['cand'], writes=['ts'])
            ph.op('dve', lambda e, h=h: e.match_replace(out=wk2[:], in_to_replace=ts[:, h, 0:8], in_values=cand[:, h, :],
                                                        imm_value=-1e30), reads=['cand'], writes=['ts', 'wk2'])
            ph.op('dve', lambda e, h=h: e.max(out=ts[:, h, 8:16], in_=wk2[:]), writes=['ts', 'wk2'])
        ph.op('dve', lambda e: e.tensor_tensor(out=ex[:], in0=ts[:], in1=bc(ts[:, :, 0:1], [128, 8, 16]), op=ALU.subtract),
              reads=['ts'], writes=['ex'])
        ph.op('act', lambda e: e.activation(out=ex[:], in_=ex[:], func=AF.Exp), writes=['ex'])
        ph.op('dve', lambda e: e.tensor_reduce(out=Z[:], in_=ex[:], axis=AX.X, op=ALU.add), reads=['ex'], writes=['Z'])
        ph.op('dve', lambda e: e.reciprocal(out=Z[:], in_=Z[:]), writes=['Z'])
        ph.op('dve', lambda e: e.tensor_tensor(out=th[:], in0=ex[:, :, 15], in1=Z[:], op=ALU.mult), reads=['ex', 'Z'],
              writes=['th'])
        ph.op('dve', lambda e: e.tensor_tensor(out=E1[:], in0=S4[:, :, 0, :], in1=bc(v4[:, :, 0, 0:1], [128, 8, 128]),
                                               op=ALU.subtract), reads=['Ssb%d' % g for g in range(4)] + ['v'], writes=['E1'])
        ph.op('act', lambda e: e.activation(out=E1[:], in_=E1[:], func=AF.Exp), writes=['E1'])
        ph.op('dve', lambda e: e.tensor_tensor(out=E1[:], in0=E1[:], in1=bc(Z[:].unsqueeze(2), [128, 8, 128]), op=ALU.mult),
              reads=['Z'], writes=['E1'])
        ph.op('pool', lambda e: e.tensor_copy(out=E1p[:], in_=E1[:].rearrange("p h (g j) -> p g h j", j=4)), reads=['E1'],
              writes=['E1p'])
        ph.op('dve', lambda e: e.tensor_tensor(out=E2[:], in0=S4[:, :, 1, :], in1=bc(v4[:, :, 1, 0:1], [128, 8, 128]),
                                               op=ALU.subtract), reads=['Ssb%d' % g for g in range(4)] + ['v'], writes=['E2'])
        ph.op('act', lambda e: e.activation(out=E2[:], in_=E2[:], func=AF.Exp), writes=['E2'])
        ph.dma('sp', lambda e, ts_=ts_: e.dma_start(out=io['E1g'][ts_, :], in_=E1p[:].rearrange("p g h j -> p (g h j)")),
               reads=['E1p'], writes=['out'])
        ph.dma('sp', lambda e, ts_=ts_: e.dma_start(out=io['E2'][ts_, :], in_=E2[:].rearrange("p h n -> p (h n)")),
               reads=['E2'], writes=['out'])
        ph.dma('sp', lambda e, ts_=ts_: e.dma_start(out=io['TH'][ts_, :], in_=th[:]), reads=['th'], writes=['out'])
    ph.emit()


def phase_peer_dense(nc, G, io, cfg):
    ph = Phase(nc, 'K')
    NO = cfg['NO']
    NE = cfg.get('NE', 128)
    hpt = ph.sb('hpt', [128, 16, 512], BF16)
    E2t = ph.sb('E2t', [128, 4, 8, 128]); THt = ph.sb('THt', [128, 4, 8])
    E1g = [ph.sb('E1g%d' % i, [128, 4, 8, 4]) for i in range(2)]
    Gm = ph.sb('Gm', [128, 4, 8, 4, 128], BF16)
    Pt = [ph.sb('Pt%d' % i, [128, 4, 128]) for i in range(2)]
    Ub = [ph.sb('Ub%d' % i, [128, 2048], BF16) for i in range(2)]
    UT = [ph.sb('UT%d' % i, [128, 16, 128], BF16) for i in range(2)]
    Vb = [ph.sb('Vb%d' % i, [128, 2048], BF16) for i in range(4)]
    gel = [ph.sb('gel%d' % i, [128, 512], BF16) for i in range(2)]
    CT = [ph.sb('CT%d' % i, [128, 512], BF16) for i in range(4)]
    acc = ph.sb('acc', [128, 4, 2048]); x1t = ph.sb('x1t', [128, 2048]); fw = ph.sb('fw', [128, 2048])
    junk = ph.sb('junk', [128, 2048], BF16); ss = ph.sb('ss', [128, 1])
    pAT = ph.ps('pAT', [128, 512]); pWT = ph.ps('pWT', [128, 512])
    pUT = [ph.ps('pUT%d' % i, [128, 8, 128], BF16) for i in range(2)]
    pO = [ph.ps('pO%d' % i, [128, 512]) for i in range(4)]
    ph.dma('sp', lambda e: e.dma_start(out=fw[:], in_=bc(io['final_norm_w'], [128, 2048])), writes=['fw'])
    ui = 0
    for tg in range(NO // 512):
        tsl = slice(tg * 512, (tg + 1) * 512)
        ph.dma('sp', lambda e, tsl=tsl: e.dma_start(out=hpt[:], in_=io['hpT'][:, :, tsl].rearrange("k p t -> p k t")),
               writes=['hpt'])
        ph.dma('sp', lambda e, tsl=tsl: e.dma_start(out=E2t[:].rearrange("p j h n -> p j (h n)"),
                                                    in_=io['E2'][tsl, :].rearrange("(j p) c -> p j c", p=128)), writes=['E2t'])
        ph.dma('sp', lambda e, tsl=tsl: e.dma_start(out=THt[:], in_=io['TH'][tsl, :].rearrange("(j p) c -> p j c", p=128)),
               writes=['E2t'])
        for g in range(NE // 4):
            eg = E1g[g % 2]; egk = 'E1g%d' % (g % 2)
            ph.dma('sp', lambda e, eg=eg, tsl=tsl, g=g: e.dma_start(
                out=eg[:].rearrange("p j h q -> p j (h q)"),
                in_=io['E1g'][tsl, g * 32:(g + 1) * 32].rearrange("(j p) c -> p j c", p=128)), writes=[egk])
            for j in range(4):
                for h in range(8):
                    pi = (j * 8 + h) % 2
                    ph.op('pool', lambda e, eg=eg, j=j, h=h, pi=pi: e.tensor_tensor(
                        out=Pt[pi][:], in0=bc(eg[:, j, h, :].unsqueeze(2), [128, 4, 128]),
                        in1=bc(E2t[:, j, h, :].unsqueeze(1), [128, 4, 128]), op=ALU.mult),
                        reads=[egk, 'E2t'], writes=['Pt%d' % pi])
                    ph.op('dve', lambda e, j=j, h=h, pi=pi: e.scalar_tensor_tensor(
                        out=Gm[:, j, h, :, :], in0=Pt[pi][:], scalar=THt[:, j, h:h + 1], in1=Pt[pi][:],
                        op0=ALU.is_ge, op1=ALU.mult), reads=['Pt%d' % pi, 'E2t'], writes=['Gm'])
            for el in range(4):
                e_ = g * 4 + el
                u = ui % 2; ui += 1
                ph.dma('pool', lambda e, u=u, e_=e_: e.dma_start(out=Ub[u][:], in_=io['expert_u'][e_ * 128:(e_ + 1) * 128, :]),
                       writes=['Ub%d' % u])
                ph.dma('pool', lambda e, el=el, e_=e_: e.dma_start(out=Vb[el][:], in_=io['expert_v'][e_ * 128:(e_ + 1) * 128, :]),
                       writes=['Vb%d' % el])
                for half in range(2):
                    for kk in range(8):
                        k = half * 8 + kk
                        ph.op('pe', lambda e, u=u, k=k, kk=kk, half=half: e.transpose(
                            out=pUT[half][:, kk, :], in_=Ub[u][:, k * 128:(k + 1) * 128], identity=G['identb'][:]),
                            reads=['Ub%d' % u], writes=['pUT%d' % half])
                    ph.op('act', lambda e, u=u, half=half: e.copy(out=UT[u][:, half * 8:(half + 1) * 8, :], in_=pUT[half][:]),
                          writes=['pUT%d' % half, 'UT%d_%d' % (u, half)])
                for k in range(16):
                    ph.op('pe', lambda e, u=u, k=k: e.matmul(pAT[:], lhsT=UT[u][:, k, :], rhs=hpt[:, k, :], start=(k == 0),
                                                            stop=(k == 15)),
                          reads=['UT%d_0' % u, 'UT%d_1' % u, 'hpt'], writes=['pAT'])
                for j in range(4):
                    for h in range(8):
                        ph.op('pe', lambda e, j=j, h=h, el=el: e.matmul(
                            pWT[:, j * 128:(j + 1) * 128], lhsT=Gm[:, j, h, el, :], rhs=G['identb'][:], start=(h == 0),
                            stop=(h == 7)), reads=['Gm'], writes=['pWT'])
                gl = gel[u]
                ph.op('act', lambda e, gl=gl: e.activation(out=gl[:], in_=pAT[:], func=AF.Gelu), writes=['pAT', 'gel%d' % u])
                ph.op('dve', lambda e, gl=gl, el=el: e.tensor_tensor(out=CT[el][:], in0=pWT[:], in1=gl[:], op=ALU.mult),
                      reads=['gel%d' % u], writes=['pWT', 'CT%d' % el])
            for j in range(4):
                for cc in range(4):
                    for el in range(4):
                        ph.op('pe', lambda e, j=j, cc=cc, el=el: e.matmul(
                            pO[cc][:], lhsT=CT[el][:, j * 128:(j + 1) * 128], rhs=Vb[el][:, cc * 512:(cc + 1) * 512],
                            start=(el == 0), stop=(el == 3)), reads=['CT%d' % el, 'Vb%d' % el], writes=['pO%d' % cc])
                    if g == 0:
                        ph.op('act', lambda e, j=j, cc=cc: e.copy(out=acc[:, j, cc * 512:(cc + 1) * 512], in_=pO[cc][:]),
                              writes=['pO%d' % cc, 'acc%d' % j])
                    else:
                        ph.op('dve', lambda e, j=j, cc=cc: e.tensor_tensor(
                            out=acc[:, j, cc * 512:(cc + 1) * 512], in0=pO[cc][:], in1=acc[:, j, cc * 512:(cc + 1) * 512],
                            op=ALU.add), writes=['pO%d' % cc, 'acc%d' % j])
        for j in range(4):
            rows = slice(tg * 512 + j * 128, tg * 512 + (j + 1) * 128)
            ak = 'acc%d' % j
            ph.dma('sp', lambda e, rows=rows: e.dma_start(out=x1t[:], in_=io['x1'][rows, :]), writes=['x1t'])
            ph.op('pool', lambda e, j=j: e.tensor_tensor(out=acc[:, j, :], in0=acc[:, j, :], in1=G['g2_bc'][:], op=ALU.mult),
                  writes=[ak])
            ph.op('dve', lambda e, j=j: e.tensor_tensor(out=acc[:, j, :], in0=acc[:, j, :], in1=x1t[:], op=ALU.add),
                  reads=['x1t'], writes=[ak])
            ph.op('act', lambda e, j=j: e.activation(out=junk[:], in_=acc[:, j, :], func=AF.Square, accum_out=ss[:, 0:1]),
                  reads=[ak], writes=['junk', 'ss'])
            ph.op('dve', lambda e: e.tensor_scalar(out=ss[:], in0=ss[:], scalar1=1.0 / 2048, scalar2=EPS, op0=ALU.mult,
                                                   op1=ALU.add), writes=['ss'])
            ph.op('act', lambda e: e.activation(out=ss[:], in_=ss[:], func=AF.Sqrt), writes=['ss'])
            ph.op('dve', lambda e: e.reciprocal(out=ss[:], in_=ss[:]), writes=['ss'])
            ph.op('dve', lambda e, j=j: e.scalar_tensor_tensor(out=acc[:, j, :], in0=acc[:, j, :], scalar=ss[:, 0:1], in1=fw[:],
                                                               op0=ALU.mult, op1=ALU.mult), reads=['ss', 'fw'], writes=[ak])
            ph.dma('sp', lambda e, j=j, rows=rows: e.dma_start(out=io['y'][rows, :], in_=acc[:, j, :]), reads=[ak],
                   writes=['yout'])
    ph.emit()


IN_SPECS = None


def build(cfg, upto=99):
    NS, NCTX, NO, NLAT = cfg['NS'], cfg['NCTX'], cfg['NO'], cfg['NS'] - cfg['NCTX']
    NT = NS // 128
    nc = bass.Bass("TRN2", target_bir_lowering=False)
    io = {}

    def inp(n, shape, dt=F32):
        io[n] = nc.dram_tensor(n, list(shape), dt, kind="ExternalInput").ap()

    def scr(n, shape, dt=F32):
        io[n] = nc.dram_tensor(n, list(shape), dt).ap()

    inp('xs', [NS, D]); inp('xo', [NO, D]); inp('own_idx', [128, NO // 128], I32)
    inp('c_t', [128, 16, 2]); inp('w_mod', [D, 6 * D]); inp('b_modT', [128, 96]); inp('norm1T', [128, 16])
    inp('norm2T', [128, 16]); inp('final_norm_w', [1, D])
    inp('w_in', [D, P_IN]); inp('b_in', [1, P_IN]); inp('b_qk', [128, 16]); inp('b_gate', [8, 4]); inp('b_gab', [128, 32])
    inp('zeros16', [128, 16])
    inp('conv_wT', [128, 16, 5]); inp('conv_bT', [128, 16]); inp('m_norm_w', [1, 1024]); inp('a_norm_w', [1, 128])
    inp('lambdas', [1, 256]); inp('ropeS', [NS, 64]); inp('ropeO', [NO, 64])
    inp('w_pa', [1024, D]); inp('w_pb', [1024, D]); inp('w_out', [D, D]); inp('w_pq', [D, D]); inp('skT', [128, 16, 128])
    nexp_rows = NEXP if upto >= 19 else 128
    inp('expert_u', [nexp_rows, D]); inp('expert_v', [nexp_rows, D])
    inp('ident', [128, 128]); inp('maskf', [128, 128]); inp('maskb', [128, 128]); inp('ones', [128, 128]); inp('sel', [8, 8, 128])
    io['y'] = nc.dram_tensor('y', [NO, D], F32, kind="ExternalOutput").ap()
    scr('gvec', [2, D]); scr('hTs', [16, 128, NS], BF16); scr('hTo', [16, 128, NO], BF16)
    scr('QKpre', [16, 128, NS + 6]); scr('GT', [4, 8, NS]); scr('QT_m', [8, 128, NS], BF16); scr('KT_m', [8, 128, NS], BF16)
    scr('Ktok', [NT, 128, 8, 128], BF16); scr('Vtok', [NT, 128, 8, 128], BF16); scr('KaT', [8, 128, NS], BF16)
    scr('Va', [NT, 128, 8, 128], BF16)
    scr('TS', [2, 128, NT * 32]); scr('DEC', [2, 128, NT * 8]); scr('MF', [2, 8, NS])
    scr('Hf', [NLAT, 1024]); scr('Hb', [NLAT, 1024]); scr('Zo', [NO, 1024]); scr('QaT', [8, 128, NO], BF16)
    scr('GaT', [16, 128, NO], BF16); scr('GbT', [16, 128, NO], BF16); scr('ymT', [8, 128, NO], BF16); scr('ydT', [8, 128, NO], BF16)
    scr('mT', [16, 128, NO], BF16); scr('x1', [NO, D]); scr('hpT', [16, 128, NO], BF16); scr('qT', [16, 128, NO], BF16)
    scr('E1g', [NO, 1024]); scr('E2', [NO, 1024]); scr('TH', [NO, 8])

    es = ExitStack()
    SEMSTATE[0] = SemState(nc, es)
    G = {}
    for n, shp, dt in (('modT', [128, 96, 2], F32), ('a1', [128, 16], F32), ('a1c', [128, 16], F32), ('a2', [128, 16], F32),
                       ('g1_bc', [128, 2048], F32), ('g2_bc', [128, 2048], F32), ('ident', [128, 128], F32),
                       ('identb', [128, 128], BF16), ('maskf', [128, 128], F32), ('maskb', [128, 128], F32),
                       ('ones', [128, 128], F32), ('onesb', [128, 128], BF16), ('sel', [8, 8, 128], F32), ('nlam', [128, 1], F32)):
        G[n] = es.enter_context(nc.sbuf_tensor('G_' + n, shp, dt))
    modT = G['modT']
    NCT = NCTX // 128
    oblk = [(i * 512, min(512, NO - i * 512)) for i in range((NO + 511) // 512)]
    steps = [
        lambda: phase_init(nc, G, io),
        lambda: phase_mod(nc, G, io),
        lambda: phase_norm(nc, G, 'B', io['xs'], io['hTs'], NS,
                           [(0, NCT, G['a1c'], modT[:, 0:16, 1]), (NCT, NT, G['a1'], modT[:, 0:16, 0])]),
        lambda: phase_norm(nc, G, 'Bo', io['xo'], io['hTo'], NO, [(0, 9999, G['a1'], modT[:, 0:16, 0])]),
        lambda: phase_gemm_fm(nc, G, 'C1', io, cfg, io['hTs'], NS, blocks(cfg), [
            dict(c0=OFF['qm'], ncb=16, bias=io['b_qk'], func=AF.Identity, odt=F32,
                 dst=lambda cb, s0, n: io['QKpre'][cb, :, pcol(cfg, s0):pcol(cfg, s0) + n]),
            dict(c0=OFF['g'], ncb=4, m=8, bias=io['b_gate'], func=AF.Identity, odt=F32,
                 dst=lambda cb, s0, n: io['GT'][cb, :, s0:s0 + n])], io['w_in']),
        lambda: phase_gemm_tm(nc, G, 'C2', io, cfg, io['hTs'], NS, io['w_in'], [
            dict(c0=OFF['vm'], ncols=1024, kind='bf16', dst=lambda t: io['Vtok'][t].rearrange("p h v -> p (h v)")),
            dict(c0=OFF['ka'], ncols=1024, kind='rope',
                 dst=lambda t: io['KaT'][:, :, t * 128:(t + 1) * 128].rearrange("h p t -> p h t")),
            dict(c0=OFF['va'], ncols=1024, kind='bf16', dst=lambda t: io['Va'][t].rearrange("p h v -> p (h v)"))],
            rope_tab=io['ropeS']),
        lambda: phase_conv(nc, G, io, cfg),
        lambda: phase_gates(nc, G, io, cfg),
        lambda: phase_scan(nc, G, io, cfg),
        lambda: phase_gemm_fm(nc, G, 'G1', io, cfg, io['hTo'], NO, oblk, [
            dict(c0=OFF['ga'], ncb=16, bias=io['b_gab'][:, 0:16], func=AF.Sigmoid, odt=BF16,
                 dst=lambda cb, s0, n: io['GaT'][cb, :, s0:s0 + n]),
            dict(c0=OFF['gb'], ncb=16, bias=io['b_gab'][:, 16:32], func=AF.Sigmoid, odt=BF16,
                 dst=lambda cb, s0, n: io['GbT'][cb, :, s0:s0 + n])], io['w_in']),
        lambda: phase_gemm_tm(nc, G, 'G2', io, cfg, io['hTo'], NO, io['w_in'], [
            dict(c0=OFF['zm'], ncols=1024, kind='sig', dst=lambda t: io['Zo'][t * 128:(t + 1) * 128, :]),
            dict(c0=OFF['qa'], ncols=1024, kind='rope',
                 dst=lambda t: io['QaT'][:, :, t * 128:(t + 1) * 128].rearrange("h p t -> p h t"))],
            rope_tab=io['ropeO']),
        lambda: phase_attn(nc, G, io, cfg),
        lambda: phase_ym(nc, G, io, cfg),
        lambda: phase_merge(nc, G, io, cfg),
        lambda: phase_wout(nc, G, io, cfg),
        lambda: phase_norm(nc, G, 'N2', io['x1'], io['hpT'], NO, [(0, 9999, G['a2'], modT[:, 48:64, 0])]),
        lambda: phase_gemm_fm(nc, G, 'Q', io, cfg, io['hpT'], NO, oblk, [
            dict(c0=0, ncb=16, bias=io['zeros16'], func=AF.Identity, odt=BF16,
                 dst=lambda cb, s0, n: io['qT'][cb, :, s0:s0 + n])], io['w_pq']),
        lambda: phase_peer_sel(nc, G, io, cfg),
        lambda: phase_peer_dense(nc, G, io, cfg),
    ]
    for i, st in enumerate(steps):
        if i >= upto:
            break
        st()
    es.close()
    return nc, io


def rope_tables(nrows_lat, grid_w=64):
    rows = nrows_lat // grid_w
    row = np.repeat(np.arange(rows, dtype=np.float32), grid_w)
    col = np.tile(np.arange(grid_w, dtype=np.float32), rows)
    inv = (np.float32(10000.0) ** (-np.arange(0, 32, 2, dtype=np.float32) / np.float32(32))).astype(np.float32)
    ang = np.concatenate([row[:, None] * inv, col[:, None] * inv], axis=-1).astype(np.float32)
    return np.concatenate([np.cos(ang), np.sin(ang)], axis=-1).astype(np.float32)


def fm(v):
    return np.ascontiguousarray(np.asarray(v, np.float32).reshape(-1, 128).T)


def host_inputs(inputs, cfg, b, r):
    NCTX, NO = cfg['NCTX'], cfg['NO']
    f = lambda a: np.ascontiguousarray(np.asarray(a, np.float32))
    x = f(inputs['x'][b]); ctx = f(inputs['ctx'][b])
    NLAT = x.shape[0]
    lo = r * NO
    m = {}
    m['xs'] = np.concatenate([ctx, x], axis=0)
    m['xo'] = np.ascontiguousarray(x[lo:lo + NO])
    m['own_idx'] = np.ascontiguousarray(np.arange(lo, lo + NO, dtype=np.int32).reshape(NO // 128, 128).T)
    m['c_t'] = np.ascontiguousarray(np.stack([fm(inputs['c'][b]), fm(inputs['c_ctx'])], axis=-1))
    m['w_mod'] = f(inputs['w_mod'][0]); m['b_modT'] = fm(inputs['b_mod'][0])
    m['norm1T'] = fm(inputs['norm1_w'][0]); m['norm2T'] = fm(inputs['norm2_w'][0])
    m['final_norm_w'] = f(inputs['final_norm_w']).reshape(1, D)
    m['w_in'] = f(inputs['w_in'][0]); bi = f(inputs['b_in'][0]); m['b_in'] = bi.reshape(1, P_IN)
    m['b_qk'] = fm(bi[0:2048]); m['b_gate'] = np.ascontiguousarray(bi[4096:4128].reshape(4, 8).T)
    m['b_gab'] = fm(bi[7200:11296]); m['zeros16'] = np.zeros((128, 16), np.float32)
    cw = f(inputs['conv_w'][0])
    m['conv_wT'] = np.ascontiguousarray(cw.reshape(5, 16, 128).transpose(2, 1, 0))
    m['conv_bT'] = fm(inputs['conv_b'][0])
    m['m_norm_w'] = f(inputs['m_norm_w'][0]).reshape(1, 1024); m['a_norm_w'] = f(inputs['a_norm_w'][0]).reshape(1, 128)
    m['lambdas'] = f(inputs['lambdas'][0]).reshape(1, 256)
    rt = rope_tables(NLAT)
    rs = np.zeros((NCTX, 64), np.float32); rs[:, 0:32] = 1.0
    m['ropeS'] = np.concatenate([rs, rt], axis=0); m['ropeO'] = np.ascontiguousarray(rt[lo:lo + NO])
    m['w_pa'] = f(inputs['w_pa'][0]); m['w_pb'] = f(inputs['w_pb'][0]); m['w_out'] = f(inputs['w_out'][0])
    m['w_pq'] = f(inputs['w_pq'][0])
    sk = f(inputs['sub_keys'][0])
    m['skT'] = np.ascontiguousarray(sk.reshape(16, 128, 128).transpose(2, 0, 1))
    m['expert_u'] = f(inputs['expert_u'][0]); m['expert_v'] = f(inputs['expert_v'][0])
    m['ident'] = np.eye(128, dtype=np.float32)
    s_ = np.arange(128)[:, None]; t_ = np.arange(128)[None, :]
    m['maskf'] = np.where(s_ > t_, BIG, 0.0).astype(np.float32); m['maskb'] = np.where(s_ < t_, BIG, 0.0).astype(np.float32)
    m['ones'] = np.ones((128, 128), np.float32)
    sel = np.zeros((8, 8, 128), np.float32)
    for j in range(8):
        sel[j, j, :] = 1.0
    m['sel'] = sel
    return m


def kernel(**inputs):
    x = np.asarray(inputs['x'])
    B, T, _ = x.shape
    NCTX = np.asarray(inputs['ctx']).shape[1]
    R = 8 // B
    cfg = dict(NCTX=NCTX, NS=NCTX + T, NO=T // R)
    nc, _ = build(cfg)
    in_maps = [host_inputs(inputs, cfg, c // R, c % R) for c in range(8)]
    res = run_bass_kernel_spmd(nc, in_maps, core_ids=list(range(8)))
    out = np.empty((B, T, D), np.float32)
    for c in range(8):
        b, r = c // R, c % R
        out[b, r * cfg['NO']:(r + 1) * cfg['NO']] = res.results[c]['y']
    return out
```

```python
import numpy as np
from contextlib import ExitStack
import concourse.bass as bass
import concourse.mybir as mybir
from concourse.bass_utils import run_bass_kernel_spmd

F32 = mybir.dt.float32
BF16 = mybir.dt.bfloat16
I32 = mybir.dt.int32
AF = mybir.ActivationFunctionType
ALU = mybir.AluOpType
AX = mybir.AxisListType

D = 2048
KC = 16
EPS = 1e-6
BIG = 30000.0
OFF = dict(qm=0, km=1024, vm=2048, zm=3072, g=4096, qa=4128, ka=5152, va=6176, ga=7200, gb=9248)
P_IN = 11296
NEXP = 16384
LAM_INIT = 0.8 - 0.6

ENG_ATTR = {'pe': 'tensor', 'dve': 'vector', 'act': 'scalar', 'pool': 'gpsimd', 'sp': 'sync'}
NDMASEM = 12


class Phase:
    def __init__(self, nc, name):
        self.nc = nc
        self.name = name
        self.ops = []
        self.state = {}
        st = SEMSTATE[0]
        self.ncomp = dict(st.ncomp)
        self.ncomp0 = dict(st.ncomp)
        self.ndma = {e: 0 for e in ENG_ATTR}
        self.dmasem_total = dict(st.dtot)
        self.dmasem_last = {}
        self.es = ExitStack()

    def sb(self, name, shape, dt=F32):
        return self.es.enter_context(self.nc.sbuf_tensor(self.name + '_' + name, list(shape), dt))

    def ps(self, name, shape, dt=F32):
        esz = 4 if dt == F32 else 2
        n = int(np.prod(shape[1:]))
        assert n * esz <= 2048
        t = self.es.enter_context(self.nc.psum_tensor(self.name + '_' + name, [128, 2048 // esz], dt))
        v = t[:shape[0], 0:n]
        if len(shape) == 3:
            v = v.rearrange("p (a b) -> p a b", b=shape[2])
        return v

    def _deps(self, reads, writes):
        deps = set()
        for k in reads:
            st = self.state.get(k)
            if st and st[0] is not None:
                deps.add(st[0])
        for k in writes:
            st = self.state.get(k)
            if st:
                if st[0] is not None:
                    deps.add(st[0])
                deps.update(st[1])
        return deps

    def _commit(self, oid, reads, writes):
        for k in reads:
            self.state.setdefault(k, [None, []])[1].append(oid)
        for k in writes:
            self.state[k] = [oid, []]

    def op(self, eng, fn, reads=(), writes=()):
        oid = len(self.ops)
        deps = self._deps(reads, writes)
        self.ncomp[eng] += 1
        self.ops.append(dict(id=oid, eng=eng, fn=fn, dma=False, deps=deps, seq=self.ncomp[eng]))
        self._commit(oid, reads, writes)
        return oid

    def dma(self, eng, fn, reads=(), writes=()):
        oid = len(self.ops)
        deps = self._deps(reads, writes)
        slot = self.ndma[eng] % NDMASEM
        self.ndma[eng] += 1
        key = (eng, slot)
        prev = self.dmasem_last.get(key)
        if prev is not None:
            deps.add(prev)
        tot = self.dmasem_total.get(key, 0) + 16
        self.dmasem_total[key] = tot
        self.dmasem_last[key] = oid
        self.ops.append(dict(id=oid, eng=eng, fn=fn, dma=True, deps=deps, semkey=key, semval=tot))
        self._commit(oid, reads, writes)
        return oid

    def emit(self):
        nc = self.nc
        st = SEMSTATE[0]
        csem, dsem = st.csem, st.dsem
        waited = {e: {} for e in ENG_ATTR}
        for o in self.ops:
            w = {}
            for d in o['deps']:
                do = self.ops[d]
                if do['dma']:
                    s, v = ('d', do['semkey']), do['semval']
                else:
                    if do['eng'] == 'pe' and o['eng'] == 'pe' and not o['dma']:
                        continue
                    s, v = ('c', do['eng']), do['seq']
                if w.get(s, 0) < v:
                    w[s] = v
            wl = []
            for s, v in w.items():
                if waited[o['eng']].get(s, 0) < v:
                    waited[o['eng']][s] = v
                    wl.append((s, v))
            o['waits'] = wl
        finals = [(('d', k), v) for k, v in self.dmasem_total.items() if v > st.dtot.get(k, 0)]
        finals += [(('c', e), self.ncomp[e]) for e in ENG_ATTR if self.ncomp[e] > self.ncomp0[e]]

        def semof(s):
            return dsem[s[1]] if s[0] == 'd' else csem[s[1]]

        with nc.Block() as block:
            for e, attr in ENG_ATTR.items():
                mine = [o for o in self.ops if o['eng'] == e]
                if not mine and e != 'sp':
                    continue

                def body(engine, mine=mine, e=e):
                    for o in mine:
                        for s, v in o['waits']:
                            engine.wait_ge(semof(s), v)
                        ins = o['fn'](engine)
                        if o['dma']:
                            ins.then_inc(dsem[o['semkey']], 16)
                        else:
                            ins.then_inc(csem[e], 1)
                    if e == 'sp':
                        for s, v in finals:
                            engine.wait_ge(semof(s), v)
                getattr(block, attr)(body)
        st.ncomp = dict(self.ncomp)
        st.dtot = dict(self.dmasem_total)
        self.es.close()


class SemState:
    def __init__(self, nc, es):
        self.csem = {e: es.enter_context(nc.semaphore('c_' + e)) for e in ENG_ATTR}
        self.dsem = {(e, i): es.enter_context(nc.semaphore('d_%s%d' % (e, i))) for e in ('sp', 'pool') for i in range(NDMASEM)}
        self.ncomp = {e: 0 for e in ENG_ATTR}
        self.dtot = {}


SEMSTATE = [None]


def bc(ap, shape):
    return ap.to_broadcast(list(shape))


def blocks(cfg):
    out = []
    s = 0
    while s < cfg['NCTX']:
        n = min(512, cfg['NCTX'] - s)
        out.append((s, n)); s += n
    while s < cfg['NS']:
        n = min(512, cfg['NS'] - s)
        out.append((s, n)); s += n
    return out


def pcol(cfg, s):
    return s + 2 if s < cfg['NCTX'] else s + 4


def load_w(ph, wt, key, w_dram, c0, c1):
    K = w_dram.shape[0]
    wv = w_dram.rearrange("(k p) c -> p k c", p=128)
    for k in range(K // 128):
        ph.dma('pool', lambda e, k=k: e.dma_start(out=wt[:, k, 0:c1 - c0], in_=wv[:, k, c0:c1]), writes=[key])


def phase_init(nc, G, io):
    ph = Phase(nc, 'I')
    for n in ('ident', 'maskf', 'maskb', 'ones'):
        ph.dma('sp', lambda e, n=n: e.dma_start(out=G[n][:], in_=io[n]), writes=[n])
    ph.dma('sp', lambda e: e.dma_start(out=G['sel'][:], in_=io['sel']), writes=['sel'])
    ph.op('dve', lambda e: e.tensor_copy(out=G['identb'][:], in_=G['ident'][:]), reads=['ident'], writes=['identb'])
    ph.op('dve', lambda e: e.tensor_copy(out=G['onesb'][:], in_=G['ones'][:]), reads=['ones'], writes=['onesb'])
    lt = ph.sb('lt', [128, 4, 64]); pr = ph.sb('pr', [128, 2, 64]); sm = ph.sb('sm', [128, 2])
    ph.dma('sp', lambda e: e.dma_start(out=lt[:].rearrange("p a b -> p (a b)"), in_=bc(io['lambdas'], [128, 256])),
           writes=['lt'])
    ph.op('dve', lambda e: e.tensor_tensor(out=pr[:, 0, :], in0=lt[:, 0, :], in1=lt[:, 1, :], op=ALU.mult),
          reads=['lt'], writes=['pr0'])
    ph.op('dve', lambda e: e.tensor_tensor(out=pr[:, 1, :], in0=lt[:, 2, :], in1=lt[:, 3, :], op=ALU.mult),
          reads=['lt'], writes=['pr1'])
    ph.op('dve', lambda e: e.tensor_reduce(out=sm[:], in_=pr[:], axis=AX.X, op=ALU.add), reads=['pr0', 'pr1'],
          writes=['sm'])
    ph.op('act', lambda e: e.activation(out=sm[:], in_=sm[:], func=AF.Exp), reads=['sm'], writes=['sm'])
    ph.op('dve', lambda e: e.tensor_tensor(out=G['nlam'][:], in0=sm[:, 1:2], in1=sm[:, 0:1], op=ALU.subtract),
          reads=['sm'], writes=['nlam'])
    ph.op('dve', lambda e: e.tensor_scalar(out=G['nlam'][:], in0=G['nlam'][:], scalar1=-LAM_INIT, scalar2=None,
                                           op0=ALU.add), reads=['nlam'], writes=['nlam'])
    ph.emit()


def phase_mod(nc, G, io):
    ph = Phase(nc, 'A')
    ct = ph.sb('ct', [128, 16, 2]); sc = ph.sb('sc', [128, 16, 2])
    bm = ph.sb('bm', [128, 96]); n1 = ph.sb('n1', [128, 16]); n2 = ph.sb('n2', [128, 16])
    slab = [ph.sb('slab%d' % i, [128, 16, 512]) for i in range(2)]
    pm = ph.ps('pm', [128, 96, 2])
    tmp = ph.sb('tmp', [128, 16])
    ph.dma('sp', lambda e: e.dma_start(out=ct[:], in_=io['c_t']), writes=['ct'])
    ph.dma('sp', lambda e: e.dma_start(out=bm[:], in_=io['b_modT']), writes=['bm'])
    ph.dma('sp', lambda e: e.dma_start(out=n1[:], in_=io['norm1T']), writes=['n1'])
    ph.dma('sp', lambda e: e.dma_start(out=n2[:], in_=io['norm2T']), writes=['n2'])
    ph.op('act', lambda e: e.activation(out=sc[:], in_=ct[:], func=AF.Silu), reads=['ct'], writes=['sc'])
    wv = io['w_mod'].rearrange("(k p) c -> p k c", p=128)
    for s in range(24):
        sl = slab[s % 2]
        ph.dma('sp', lambda e, sl=sl, s=s: e.dma_start(out=sl[:], in_=wv[:, :, s * 512:(s + 1) * 512]),
               writes=['slab%d' % (s % 2)])
        for jj in range(4):
            j = s * 4 + jj
            for k in range(16):
                ph.op('pe', lambda e, sl=sl, jj=jj, j=j, k=k: e.matmul(
                    pm[:, j, :], lhsT=sl[:, k, jj * 128:(jj + 1) * 128], rhs=sc[:, k, :],
                    start=(k == 0), stop=(k == 15)),
                    reads=['slab%d' % (s % 2), 'sc'], writes=['pm'])
    modT = G['modT']
    ph.op('dve', lambda e: e.tensor_tensor(out=modT[:], in0=pm[:], in1=bc(bm[:].unsqueeze(2), [128, 96, 2]),
                                           op=ALU.add), reads=['bm'], writes=['modT', 'pm'])
    for name, nrm, lo, col in (('a1', n1, 16, 0), ('a1c', n1, 16, 1), ('a2', n2, 64, 0)):
        t = G[name]
        ph.op('dve', lambda e, lo=lo, col=col: e.tensor_scalar(out=tmp[:], in0=modT[:, lo:lo + 16, col], scalar1=1.0,
                                                              scalar2=None, op0=ALU.add),
              reads=['modT'], writes=['tmp'])
        ph.op('dve', lambda e, t=t, nrm=nrm: e.tensor_tensor(out=t[:], in0=tmp[:], in1=nrm[:], op=ALU.mult),
              reads=['tmp', 'n1', 'n2'], writes=[name])
    gv = io['gvec']
    for gi, lo in ((0, 32), (1, 80)):
        ph.dma('sp', lambda e, gi=gi, lo=lo: e.dma_start(
            out=gv[gi].rearrange("(k p) -> p k", p=128), in_=modT[:, lo:lo + 16, 0],
            allow_slow_non_contiguous=True), reads=['modT'], writes=['gv%d' % gi])
        ph.dma('sp', lambda e, gi=gi: e.dma_start(
            out=G['g%d_bc' % (gi + 1)][:], in_=bc(gv[gi:gi + 1, :], [128, 2048])),
            reads=['gv%d' % gi], writes=['gbc%d' % gi])
    ph.emit()


def phase_norm(nc, G, name, x, hT, ntok, segs, xres=None):
    ph = Phase(nc, name)
    ntile = (ntok + 127) // 128
    xt = [ph.sb('xt%d' % i, [128, 2048]) for i in range(2)]
    junk = ph.sb('junk', [128, 2048], BF16)
    xn = [ph.sb('xn%d' % i, [128, 2048]) for i in range(2)]
    ss = ph.sb('ss', [128, 2]); rs = ph.sb('rs', [128, 2])
    pt = [ph.ps('pt%d' % i, [128, 4, 128]) for i in range(4)]
    stg = [ph.sb('stg%d' % i, [128, 16, 512], BF16) for i in range(2)]
    ident = G['ident']
    for t in range(ntile):
        np_ = min(128, ntok - t * 128)
        b = t % 2
        for (lo, hi, a_t, s_t) in segs:
            if lo <= t < hi:
                A, S = a_t, s_t
        grp = t // 4; sg = stg[grp % 2]; sgk = 'stg%d' % (grp % 2)
        ph.dma('sp', lambda e, b=b, t=t, np_=np_: e.dma_start(out=xt[b][:np_, :], in_=x[t * 128:t * 128 + np_, :]),
               writes=['xt%d' % b])
        ph.op('act', lambda e, b=b, np_=np_: e.activation(out=junk[:np_, :], in_=xt[b][:np_, :], func=AF.Square,
                                                           accum_out=ss[:np_, b:b + 1]),
              reads=['xt%d' % b], writes=['junk', 'ss%d' % b])
        ph.op('dve', lambda e, b=b, np_=np_: e.tensor_scalar(out=rs[:np_, b:b + 1], in0=ss[:np_, b:b + 1],
                                                             scalar1=1.0 / 2048, scalar2=EPS, op0=ALU.mult, op1=ALU.add),
              reads=['ss%d' % b], writes=['rs%d' % b])
        ph.op('act', lambda e, b=b, np_=np_: e.activation(out=rs[:np_, b:b + 1], in_=rs[:np_, b:b + 1], func=AF.Sqrt),
              reads=['rs%d' % b], writes=['rs%d' % b])
        ph.op('dve', lambda e, b=b, np_=np_: e.reciprocal(out=rs[:np_, b:b + 1], in_=rs[:np_, b:b + 1]),
              reads=['rs%d' % b], writes=['rs%d' % b])
        ph.op('dve', lambda e, b=b, np_=np_: e.tensor_scalar(out=xn[b][:np_, :], in0=xt[b][:np_, :],
                                                             scalar1=rs[:np_, b:b + 1], scalar2=None, op0=ALU.mult),
              reads=['xt%d' % b, 'rs%d' % b], writes=['xn%d' % b])
        for q in range(4):
            pq = q
            for kk in range(4):
                k = q * 4 + kk
                ph.op('pe', lambda e, b=b, k=k, kk=kk, pq=pq, np_=np_: e.transpose(
                    out=pt[pq][:, kk, :np_], in_=xn[b][:np_, k * 128:(k + 1) * 128], identity=ident[:np_, :np_]),
                    reads=['xn%d' % b], writes=['pt%d' % pq])
            c0 = (t % 4) * 128
            eng = 'dve' if q % 2 == 0 else 'act'
            for kk in range(4):
                k = q * 4 + kk
                if eng == 'dve':
                    ph.op('dve', lambda e, k=k, kk=kk, pq=pq, np_=np_, sg=sg, c0=c0, A=A, S=S: e.tensor_scalar(
                        out=sg[:, k, c0:c0 + np_], in0=pt[pq][:, kk, :np_], scalar1=A[:, k:k + 1], scalar2=S[:, k:k + 1],
                        op0=ALU.mult, op1=ALU.add), writes=[sgk + '_%d' % k, 'pt%d' % pq])
                else:
                    ph.op('act', lambda e, k=k, kk=kk, pq=pq, np_=np_, sg=sg, c0=c0, A=A, S=S: e.activation(
                        out=sg[:, k, c0:c0 + np_], in_=pt[pq][:, kk, :np_], func=AF.Identity,
                        scale=A[:, k:k + 1], bias=S[:, k:k + 1]), writes=[sgk + '_%d' % k, 'pt%d' % pq])
        if t % 4 == 3 or t == ntile - 1:
            t0 = grp * 512
            n = t * 128 + np_ - t0
            ph.dma('sp', lambda e, sg=sg, t0=t0, n=n: e.dma_start(
                out=hT[:, :, t0:t0 + n].rearrange("k p t -> p k t"), in_=sg[:, :, :n]),
                reads=[sgk + '_%d' % k for k in range(16)], writes=['hT'])
    ph.emit()


def phase_gemm_fm(nc, G, name, io, cfg, hT, ntok, blks, jobs, w_dram, kchunks=KC):
    ph = Phase(nc, name)
    tot_cols = sum(j['ncb'] * j.get('m', 128) for j in jobs)
    wt = ph.sb('w', [128, kchunks, tot_cols], BF16)
    off = 0
    for j in jobs:
        cw = j['ncb'] * j.get('m', 128)
        j['woff'] = off
        wv = w_dram.rearrange("(k p) c -> p k c", p=128)
        for k in range(kchunks):
            ph.dma('pool', lambda e, k=k, off=off, cw=cw, j=j: e.dma_start(
                out=wt[:, k, off:off + cw], in_=wv[:, k, j['c0']:j['c0'] + cw]), writes=['w'])
        off += cw
        j['bt'] = ph.sb('b%d' % j['c0'], [128, j['ncb']])
        ph.dma('sp', lambda e, j=j: e.dma_start(out=j['bt'][:j.get('m', 128), :], in_=j['bias']), writes=['bias'])
    hb = [ph.sb('h%d' % i, [128, kchunks, 512], BF16) for i in range(2)]
    pp = [ph.ps('pp%d' % i, [128, 512]) for i in range(4)]
    so = {}
    for odt in set(j['odt'] for j in jobs):
        so[odt] = [ph.sb('so%s%d' % (str(odt)[-4:], i), [128, 512], odt) for i in range(4)]
    cnt = 0
    for bi, (s0, n) in enumerate(blks):
        h = hb[bi % 2]; hk = 'h%d' % (bi % 2)
        ph.dma('sp', lambda e, h=h, s0=s0, n=n: e.dma_start(
            out=h[:, :, :n], in_=hT[:, :, s0:s0 + n].rearrange("k p t -> p k t")), writes=[hk])
        for j in jobs:
            m = j.get('m', 128)
            for cb in range(j['ncb']):
                p = pp[cnt % 4]; pk = 'pp%d' % (cnt % 4)
                o = so[j['odt']][cnt % 4]; ok = 'so%s%d' % (str(j['odt'])[-4:], cnt % 4)
                cnt += 1
                for k in range(kchunks):
                    ph.op('pe', lambda e, p=p, k=k, j=j, cb=cb, m=m, h=h, n=n: e.matmul(
                        p[:m, :n], lhsT=wt[:, k, j['woff'] + cb * m:j['woff'] + (cb + 1) * m], rhs=h[:, k, :n],
                        start=(k == 0), stop=(k == kchunks - 1)), reads=['w', hk], writes=[pk])
                ph.op('act', lambda e, p=p, o=o, j=j, cb=cb, m=m, n=n: e.activation(
                    out=o[:m, :n], in_=p[:m, :n], func=j['func'], bias=j['bt'][:m, cb:cb + 1], scale=1.0),
                    reads=['bias'], writes=[pk, ok])
                ph.dma('sp', lambda e, o=o, j=j, cb=cb, m=m, s0=s0, n=n: e.dma_start(
                    out=j['dst'](cb, s0, n), in_=o[:m, :n]), reads=[ok], writes=['out'])
    ph.emit()


def rope_ops(ph, X, cs, O, xkeys, cskey, okey):
    Xv = X[:].rearrange("p (g two d) -> p g two d", two=2, d=32)
    Ov = O[:].rearrange("p (g two d) -> p g two d", two=2, d=32)
    cb_ = bc(cs[:, 0:32].unsqueeze(1), [128, 16, 32])
    sb_ = bc(cs[:, 32:64].unsqueeze(1), [128, 16, 32])
    t1, t2 = ph.rope_tmp
    x1, x2 = Xv[:, :, 0, :], Xv[:, :, 1, :]
    rd = list(xkeys) + [cskey]
    ph.op('pool', lambda e: e.tensor_tensor(out=t1[:], in0=x1, in1=cb_, op=ALU.mult), reads=rd, writes=['rt1'])
    ph.op('pool', lambda e: e.tensor_tensor(out=t2[:], in0=x2, in1=sb_, op=ALU.mult), reads=rd, writes=['rt2'])
    ph.op('pool', lambda e: e.tensor_tensor(out=Ov[:, :, 0, :], in0=t1[:], in1=t2[:], op=ALU.subtract),
          reads=['rt1', 'rt2'], writes=[okey + 'a'])
    ph.op('pool', lambda e: e.tensor_tensor(out=t1[:], in0=x2, in1=cb_, op=ALU.mult), reads=rd, writes=['rt1'])
    ph.op('pool', lambda e: e.tensor_tensor(out=t2[:], in0=x1, in1=sb_, op=ALU.mult), reads=rd, writes=['rt2'])
    ph.op('pool', lambda e: e.tensor_tensor(out=Ov[:, :, 1, :], in0=t1[:], in1=t2[:], op=ALU.add),
          reads=['rt1', 'rt2'], writes=[okey + 'b'])


def phase_gemm_tm(nc, G, name, io, cfg, hT, ntok, w_dram, secs, rope_tab=None):
    ph = Phase(nc, name)
    tot = sum(s['ncols'] for s in secs)
    wt = ph.sb('w', [128, KC, tot], BF16)
    bb = ph.sb('bb', [128, tot])
    off = 0
    wv = w_dram.rearrange("(k p) c -> p k c", p=128)
    for s in secs:
        s['woff'] = off
        for k in range(KC):
            ph.dma('pool', lambda e, k=k, off=off, s=s: e.dma_start(
                out=wt[:, k, off:off + s['ncols']], in_=wv[:, k, s['c0']:s['c0'] + s['ncols']]), writes=['w'])
        ph.dma('sp', lambda e, off=off, s=s: e.dma_start(
            out=bb[:, off:off + s['ncols']], in_=bc(io['b_in'][0:1, s['c0']:s['c0'] + s['ncols']], [128, s['ncols']])),
            writes=['bb'])
        off += s['ncols']
    ntile = ntok // 128
    hb = [ph.sb('h%d' % i, [128, KC, 128], BF16) for i in range(2)]
    pp = [ph.ps('pp%d' % i, [128, 512]) for i in range(4)]
    ptr = [ph.ps('ptr%d' % i, [128, 8, 128], BF16) for i in range(2)]
    obf = [ph.sb('obf%d' % i, [128, 1024], BF16) for i in range(2)]
    of32 = [ph.sb('of%d' % i, [128, 1024]) for i in range(2)]
    cs = [ph.sb('cs%d' % i, [128, 64]) for i in range(2)]
    ph.rope_tmp = (ph.sb('rt1', [128, 16, 32]), ph.sb('rt2', [128, 16, 32]))
    fmst = [ph.sb('fmst%d' % i, [128, 8, 128], BF16) for i in range(2)]
    ropeb = [ph.sb('ropeb%d' % i, [128, 1024], BF16) for i in range(2)]
    cnt = 0
    for t in range(ntile):
        h = hb[t % 2]; hk = 'h%d' % (t % 2)
        ph.dma('sp', lambda e, h=h, t=t: e.dma_start(
            out=h[:], in_=hT[:, :, t * 128:(t + 1) * 128].rearrange("k p t -> p k t")), writes=[hk])
        for si, s in enumerate(secs):
            kind = s['kind']
            ob = obf[(t * len(secs) + si) % 2]; obk = 'obf%d' % ((t * len(secs) + si) % 2)
            of = of32[t % 2]; ofk = 'of%d' % (t % 2)
            for ch in range(s['ncols'] // 512):
                p = pp[cnt % 4]; pk = 'pp%d' % (cnt % 4); cnt += 1
                c = s['woff'] + ch * 512
                for k in range(KC):
                    ph.op('pe', lambda e, p=p, k=k, c=c, h=h: e.matmul(
                        p[:], lhsT=h[:, k, :], rhs=wt[:, k, c:c + 512], start=(k == 0), stop=(k == KC - 1)),
                        reads=['w', hk], writes=[pk])
                dstt = of if kind in ('rope', 'sig') else ob
                dk = ofk if kind in ('rope', 'sig') else obk
                ph.op('dve', lambda e, p=p, c=c, ch=ch, dstt=dstt: e.tensor_tensor(
                    out=dstt[:, ch * 512:(ch + 1) * 512], in0=p[:], in1=bb[:, c:c + 512], op=ALU.add),
                    reads=['bb'], writes=[pk, dk + '_%d' % ch])
            nch = s['ncols'] // 512
            if kind == 'bf16':
                ph.dma('sp', lambda e, ob=ob, s=s, t=t: e.dma_start(out=s['dst'](t), in_=ob[:, :s['ncols']]),
                       reads=[obk + '_%d' % c_ for c_ in range(nch)], writes=['out'])
            elif kind == 'sig':
                ph.op('act', lambda e, of=of: e.activation(out=of[:], in_=of[:], func=AF.Sigmoid),
                      reads=[], writes=[ofk + '_%d' % c_ for c_ in range(nch)])
                ph.dma('sp', lambda e, of=of, s=s, t=t: e.dma_start(out=s['dst'](t), in_=of[:, :s['ncols']]),
                       reads=[ofk + '_%d' % c_ for c_ in range(nch)], writes=['out'])
            elif kind == 'rope':
                c_s = cs[t % 2]; csk = 'cs%d' % (t % 2)
                ph.dma('sp', lambda e, c_s=c_s, t=t: e.dma_start(out=c_s[:], in_=rope_tab[t * 128:(t + 1) * 128, :]),
                       writes=[csk])
                rb = ropeb[t % 2]; rbk = 'ropeb%d' % (t % 2)
                rope_ops(ph, of, c_s, rb, [ofk + '_0', ofk + '_1'], csk, rbk)
                fs = fmst[t % 2]; fk = 'fmst%d' % (t % 2)
                pt_ = ptr[t % 2]; ptk = 'ptr%d' % (t % 2)
                for hh in range(8):
                    ph.op('pe', lambda e, hh=hh, rb=rb, pt_=pt_: e.transpose(
                        out=pt_[:, hh, :], in_=rb[:, hh * 128:(hh + 1) * 128], identity=G['identb'][:]),
                        reads=[rbk + 'a', rbk + 'b'], writes=[ptk])
                ph.op('act', lambda e, fs=fs, pt_=pt_: e.copy(out=fs[:], in_=pt_[:]), writes=[ptk, fk])
                ph.dma('sp', lambda e, fs=fs, s=s, t=t: e.dma_start(out=s['dst'](t), in_=fs[:]),
                       reads=[fk], writes=['out'])
    ph.emit()


def phase_conv(nc, G, io, cfg):
    ph = Phase(nc, 'D')
    NS, NCTX = cfg['NS'], cfg['NCTX']
    blks = blocks(cfg)
    cw = ph.sb('cw', [128, 16, 5]); cbi = ph.sb('cbi', [128, 16]); zt = ph.sb('zt', [128, 16, 2])
    win = [ph.sb('win%d' % i, [128, 516]) for i in range(2)]
    acc = [ph.sb('acc%d' % i, [128, 512]) for i in range(2)]
    sg = [ph.sb('sg%d' % i, [128, 512]) for i in range(2)]
    ob = [ph.sb('ob%d' % i, [128, 512], BF16) for i in range(2)]
    ptr = [ph.ps('ptr%d' % i, [128, 4, 128], BF16) for i in range(2)]
    kst = [ph.sb('kst%d' % i, [128, 4, 128], BF16) for i in range(2)]
    ph.dma('sp', lambda e: e.dma_start(out=cw[:], in_=io['conv_wT']), writes=['cw'])
    ph.dma('sp', lambda e: e.dma_start(out=cbi[:], in_=io['conv_bT']), writes=['cw'])
    ph.op('dve', lambda e: e.memset(zt[:], 0.0), writes=['zt'])
    QK = io['QKpre']
    for g0 in (0, NCTX + 2, NS + 4):
        ph.dma('sp', lambda e, g0=g0: e.dma_start(out=QK[:, :, g0:g0 + 2].rearrange("c p t -> p c t"), in_=zt[:]),
               reads=['zt'], writes=['gap'])
    it = 0
    for cb in range(16):
        h = cb % 8
        scale = 128 ** -0.5 if cb < 8 else 1.0
        dstT = io['QT_m'] if cb < 8 else io['KT_m']
        for (s0, n) in blks:
            b = it % 2; it += 1
            c0 = pcol(cfg, s0)
            w_, a_, s_, o_ = win[b], acc[b], sg[b], ob[b]
            ph.dma('sp', lambda e, w_=w_, cb=cb, c0=c0, n=n: e.dma_start(out=w_[:, :n + 4], in_=QK[cb, :, c0 - 2:c0 + n + 2]),
                   reads=['gap'], writes=['win%d' % b])
            ph.op('dve', lambda e, w_=w_, a_=a_, cb=cb, n=n: e.tensor_scalar(
                out=a_[:, :n], in0=w_[:, 0:n], scalar1=cw[:, cb, 0:1], scalar2=cbi[:, cb:cb + 1], op0=ALU.mult, op1=ALU.add),
                reads=['win%d' % b, 'cw'], writes=['acc%d' % b])
            for j in range(1, 5):
                ph.op('dve', lambda e, w_=w_, a_=a_, cb=cb, n=n, j=j: e.scalar_tensor_tensor(
                    out=a_[:, :n], in0=w_[:, j:j + n], scalar=cw[:, cb, j:j + 1], in1=a_[:, :n], op0=ALU.mult, op1=ALU.add),
                    reads=['win%d' % b, 'cw'], writes=['acc%d' % b])
            ph.op('act', lambda e, a_=a_, s_=s_, n=n: e.activation(out=s_[:, :n], in_=a_[:, :n], func=AF.Sigmoid),
                  reads=['acc%d' % b], writes=['sg%d' % b])
            ph.op('dve', lambda e, a_=a_, s_=s_, o_=o_, n=n, scale=scale: e.scalar_tensor_tensor(
                out=o_[:, :n], in0=a_[:, :n], scalar=scale, in1=s_[:, :n], op0=ALU.mult, op1=ALU.mult),
                reads=['acc%d' % b, 'sg%d' % b], writes=['ob%d' % b])
            ph.dma('sp', lambda e, o_=o_, h=h, s0=s0, n=n, dstT=dstT: e.dma_start(out=dstT[h, :, s0:s0 + n], in_=o_[:, :n]),
                   reads=['ob%d' % b], writes=['out'])
            if cb >= 8:
                nt_ = n // 128
                for ti in range(nt_):
                    ph.op('pe', lambda e, o_=o_, ti=ti, b=b: e.transpose(
                        out=ptr[b][:, ti, :], in_=o_[:, ti * 128:(ti + 1) * 128], identity=G['identb'][:]),
                        reads=['ob%d' % b], writes=['ptr%d' % b])
                ph.op('act', lambda e, b=b, nt_=nt_: e.copy(out=kst[b][:, :nt_, :], in_=ptr[b][:, :nt_, :]),
                      writes=['ptr%d' % b, 'kst%d' % b])
                t0 = s0 // 128
                ph.dma('sp', lambda e, b=b, nt_=nt_, t0=t0, h=h: e.dma_start(
                    out=io['Ktok'][t0:t0 + nt_, :, h, :].rearrange("t p k -> p t k"), in_=kst[b][:, :nt_, :]),
                    reads=['kst%d' % b], writes=['out'])
    ph.emit()


def phase_gates(nc, G, io, cfg):
    ph = Phase(nc, 'E')
    NS, NCTX, NT = cfg['NS'], cfg['NCTX'], cfg['NS'] // 128
    NCT = NCTX // 128
    T = [ph.sb('T%d' % i, [8, NS]) for i in range(5)]
    zer = ph.sb('zer', [8, 1]); one = ph.sb('one', [8, 1])
    ph.op('dve', lambda e: e.memset(zer[:], 0.0), writes=['zer'])
    ph.op('dve', lambda e: e.memset(one[:], 1.0), writes=['one'])
    MP = ph.sb('MP', [8, NT]); MN = ph.sb('MN', [8, NT]); dc = ph.sb('dc', [8, NT]); xd = ph.sb('xd', [8, NT, 8])
    tot = ph.sb('tot', [8, 4])
    tsp = [ph.ps('tsp%d' % i, [128, 16, 8]) for i in range(2)]
    dps = ph.ps('dps', [128, 512])
    tss = ph.sb('tss', [128, NT, 4, 8]); dcs = ph.sb('dcs', [128, NT * 8])
    GT = io['GT']
    segs = [(0, NCTX), (NCTX, NS)]
    LF, P, A, M, E = T
    for d in range(2):
        ti, tf = (0, 1) if d == 0 else (2, 3)
        ph.dma('sp', lambda e, tf=tf: e.dma_start(out=LF[:], in_=GT[tf]), writes=['LF'])
        ph.op('act', lambda e: e.activation(out=LF[:], in_=LF[:], func=AF.Exp, scale=-1.0), writes=['LF'])
        ph.op('act', lambda e: e.activation(out=LF[:], in_=LF[:], func=AF.Ln, bias=1.0, scale=1.0), writes=['LF'])
        ph.op('dve', lambda e: e.tensor_scalar(out=LF[:], in0=LF[:], scalar1=-1.0, scalar2=None, op0=ALU.mult),
              writes=['LF'])
        for (a0, a1) in segs:
            ph.op('dve', lambda e, a0=a0, a1=a1: e.tensor_tensor_scan(
                out=P[:, a0:a1], data0=bc(one[:, 0:1], [8, a1 - a0]), data1=LF[:, a0:a1], initial=0.0,
                op0=ALU.mult, op1=ALU.add), reads=['LF', 'one'], writes=['P'])
        ph.op('dve', lambda e: e.tensor_copy(out=tot[:, 0:1], in_=P[:, NCTX - 1:NCTX]), reads=['P'], writes=['tot'])
        ph.op('dve', lambda e: e.tensor_copy(out=tot[:, 1:2], in_=P[:, NS - 1:NS]), reads=['P'], writes=['tot'])
        ph.op('dve', lambda e: e.tensor_tensor(out=tot[:, 2:3], in0=tot[:, 0:1], in1=tot[:, 1:2], op=ALU.add),
              writes=['tot'])
        if d == 0:
            ph.op('dve', lambda e: e.tensor_scalar(out=P[:, NCTX:NS], in0=P[:, NCTX:NS], scalar1=tot[:, 0:1],
                                                   scalar2=None, op0=ALU.add), reads=['tot'], writes=['P'])
        else:
            ph.op('dve', lambda e: e.tensor_tensor(out=P[:], in0=LF[:], in1=P[:], op=ALU.subtract), reads=['LF'],
                  writes=['P'])
            ph.op('dve', lambda e: e.tensor_scalar(out=P[:, 0:NCTX], in0=P[:, 0:NCTX], scalar1=tot[:, 0:1],
                                                   scalar2=None, op0=ALU.add), reads=['tot'], writes=['P'])
            ph.op('dve', lambda e: e.tensor_scalar(out=P[:, NCTX:NS], in0=P[:, NCTX:NS], scalar1=tot[:, 2:3],
                                                   scalar2=None, op0=ALU.add), reads=['tot'], writes=['P'])
        ph.dma('sp', lambda e, ti=ti: e.dma_start(out=A[:], in_=GT[ti]), writes=['A'])
        ph.op('dve', lambda e: e.tensor_tensor(out=A[:], in0=A[:], in1=P[:], op=ALU.subtract), reads=['P'], writes=['A'])
        if d == 0:
            ph.op('dve', lambda e: e.tensor_tensor_scan(out=M[:], data0=bc(zer[:, 0:1], [8, NS]), data1=A[:], initial=0.0,
                                                        op0=ALU.add, op1=ALU.max), reads=['A', 'zer'], writes=['M'])
        else:
            ph.op('dve', lambda e: e.tensor_tensor_scan(
                out=M[:, 0:NCTX][:, ::-1], data0=bc(zer[:, 0:1], [8, NCTX]), data1=A[:, 0:NCTX][:, ::-1], initial=0.0,
                op0=ALU.add, op1=ALU.max), reads=['A', 'zer'], writes=['M'])
            ph.op('dve', lambda e: e.tensor_tensor_scan(
                out=M[:, NCTX:NS][:, ::-1], data0=bc(zer[:, 0:1], [8, NS - NCTX]), data1=A[:, NCTX:NS][:, ::-1],
                initial=M[:, 0:1], op0=ALU.add, op1=ALU.max), reads=['A', 'zer'], writes=['M'])
        ph.dma('sp', lambda e, d=d: e.dma_start(out=io['MF'][d], in_=M[:]), reads=['M'], writes=['MFout'])
        Mv = M[:].rearrange("h (c t) -> h c t", t=128)
        if d == 0:
            ph.op('dve', lambda e: e.tensor_copy(out=MN[:], in_=Mv[:, :, 127]), reads=['M'], writes=['MN'])
            ph.op('dve', lambda e: e.memset(MP[:, 0:1], 0.0), writes=['MP'])
            ph.op('dve', lambda e: e.tensor_copy(out=MP[:, 1:NT], in_=Mv[:, 0:NT - 1, 127]), reads=['M'], writes=['MP'])
        else:
            ph.op('dve', lambda e: e.tensor_copy(out=MN[:], in_=Mv[:, :, 0]), reads=['M'], writes=['MN'])
            ph.op('dve', lambda e: e.tensor_copy(out=MP[:, 0:NT - 1], in_=Mv[:, 1:NT, 0]), reads=['M'], writes=['MP'])
            ph.op('dve', lambda e: e.memset(MP[:, NCT - 1:NCT], 0.0), writes=['MP'])
            ph.op('dve', lambda e: e.tensor_copy(out=MP[:, NT - 1:NT], in_=M[:, 0:1]), reads=['M'], writes=['MP'])
        ph.op('dve', lambda e: e.tensor_tensor(out=dc[:], in0=MP[:], in1=MN[:], op=ALU.subtract), reads=['MP', 'MN'],
              writes=['dc'])
        ph.op('act', lambda e: e.activation(out=dc[:], in_=dc[:], func=AF.Exp), writes=['dc'])
        ph.op('dve', lambda e: e.tensor_tensor(out=xd[:], in0=bc(dc[:].unsqueeze(2), [8, NT, 8]),
                                               in1=bc(G['sel'][:, :, 0].unsqueeze(1), [8, NT, 8]), op=ALU.mult),
              reads=['dc'], writes=['xd'])
        xdf = xd[:].rearrange("j c h -> j (c h)")
        for c0 in range(0, NT * 8, 512):
            n = min(512, NT * 8 - c0)
            ph.op('pe', lambda e, c0=c0, n=n: e.matmul(dps[:, :n], lhsT=G['ones'][0:8, :], rhs=xdf[:, c0:c0 + n],
                                                        start=True, stop=True), reads=['xd'], writes=['dps'])
            ph.op('act', lambda e, c0=c0, n=n: e.copy(out=dcs[:, c0:c0 + n], in_=dps[:, :n]), writes=['dps', 'dcs'])
        ph.dma('sp', lambda e, d=d: e.dma_start(out=io['DEC'][d], in_=dcs[:]), reads=['dcs'], writes=['DECout'])
        MPb = bc(MP[:].unsqueeze(2), [8, NT, 128]); MNb = bc(MN[:].unsqueeze(2), [8, NT, 128])
        Ev = E[:].rearrange("h (c t) -> h c t", t=128)
        Av = A[:].rearrange("h (c t) -> h c t", t=128)
        for q in range(4):
            if q == 0:
                src = A; rk = ['A']
            elif q == 1:
                ph.op('dve', lambda e: e.tensor_tensor(out=Ev, in0=MPb, in1=Mv, op=ALU.subtract), reads=['M', 'MP'],
                      writes=['E'])
                ph.op('act', lambda e: e.activation(out=E[:], in_=E[:], func=AF.Exp), writes=['E'])
                src = E; rk = ['E']
            elif q == 2:
                ph.op('dve', lambda e: e.tensor_tensor(out=E[:], in0=P[:], in1=M[:], op=ALU.add), reads=['M', 'P'],
                      writes=['E'])
                ph.op('act', lambda e: e.activation(out=E[:], in_=E[:], func=AF.Exp, scale=-1.0), writes=['E'])
                src = E; rk = ['E']
            else:
                ph.op('dve', lambda e: e.tensor_tensor(out=Ev, in0=Av, in1=MNb, op=ALU.subtract), reads=['A', 'MN'],
                      writes=['E'])
                ph.op('act', lambda e: e.activation(out=E[:], in_=E[:], func=AF.Exp), writes=['E'])
                src = E; rk = ['E']
            for c0 in range(0, NT, 16):
                ncc = min(16, NT - c0)
                pb = (c0 // 16) % 2
                for c in range(c0, c0 + ncc):
                    ph.op('pe', lambda e, c=c, c0=c0, pb=pb, src=src: e.transpose(
                        out=tsp[pb][:, c - c0, :], in_=src[:, c * 128:(c + 1) * 128], identity=G['ident'][0:8, 0:8]),
                        reads=rk, writes=['tsp%d' % pb])
                ph.op('act', lambda e, c0=c0, ncc=ncc, pb=pb, q=q: e.copy(out=tss[:, c0:c0 + ncc, q, :],
                                                                          in_=tsp[pb][:, :ncc, :]),
                      writes=['tsp%d' % pb, 'tss'])
        ph.dma('sp', lambda e, d=d: e.dma_start(out=io['TS'][d], in_=tss[:].rearrange("p c q h -> p (c q h)")),
               reads=['tss'], writes=['TSout'])
    ph.emit()


def phase_scan(nc, G, io, cfg):
    ph = Phase(nc, 'H')
    NS, NCTX, NT = cfg['NS'], cfg['NCTX'], cfg['NS'] // 128
    NCT = NCTX // 128
    S = ph.sb('S', [128, 16, 129]); Sb = ph.sb('Sb', [128, 16, 129], BF16)
    ph.op('dve', lambda e: e.memset(S[:], 0.0), writes=['S%d' % i for i in range(16)])
    ph.op('pool', lambda e: e.memset(Sb[:], 0.0), writes=['Sb%d' % i for i in range(16)])
    TS = [ph.sb('TS%d' % d, [128, NT, 4, 8]) for d in range(2)]
    DEC = [ph.sb('DEC%d' % d, [128, NT, 8]) for d in range(2)]
    MF = [ph.sb('MF%d' % d, [8, NS]) for d in range(2)]
    for d in range(2):
        ph.dma('sp', lambda e, d=d: e.dma_start(out=TS[d][:].rearrange("p c q h -> p (c q h)"), in_=io['TS'][d]),
               writes=['TS'])
        ph.dma('sp', lambda e, d=d: e.dma_start(out=DEC[d][:].rearrange("p c h -> p (c h)"), in_=io['DEC'][d]),
               writes=['TS'])
        ph.dma('sp', lambda e, d=d: e.dma_start(out=MF[d][:], in_=io['MF'][d]), writes=['TS'])
    NB = 2
    QT = [[ph.sb('QT%d_%d' % (d, i), [128, 8, 128], BF16) for i in range(NB)] for d in range(2)]
    KT = [[ph.sb('KT%d_%d' % (d, i), [128, 8, 128], BF16) for i in range(NB)] for d in range(2)]
    Kk = [[ph.sb('Kk%d_%d' % (d, i), [128, 8, 128], BF16) for i in range(NB)] for d in range(2)]
    Va = [[ph.sb('Va%d_%d' % (d, i), [128, 8, 129], BF16) for i in range(NB)] for d in range(2)]
    for d in range(2):
        for i in range(NB):
            ph.op('pool', lambda e, d=d, i=i: e.memset(Va[d][i][:, :, 128:129], 1.0), writes=['Va%d_%d' % (d, i)])
    ho = [[ph.sb('ho%d_%d' % (d, i), [128, 8, 128]) for i in range(2)] for d in range(2)]
    psA = [ph.ps('psA%d' % i, [128, 128]) for i in range(2)]
    psB = [ph.ps('psB%d' % i, [128, 128]) for i in range(2)]
    psC = ph.ps('psC', [128, 129]); psD = ph.ps('psD', [128, 129])
    psE = [ph.ps('psE%d' % i, [128, 129]) for i in range(2)]
    Dm = [ph.sb('Dm%d' % i, [128, 128]) for i in range(2)]
    SD = [ph.sb('SD%d' % i, [128, 128], BF16) for i in range(2)]
    isb = [ph.sb('isb%d' % i, [128, 129]) for i in range(2)]
    tt = [ph.sb('tt%d' % i, [128, 129]) for i in range(2)]
    dn = [ph.sb('dn%d' % i, [128, 1]) for i in range(2)]
    VW = [ph.sb('VW%d' % i, [128, 129], BF16) for i in range(2)]
    order = [list(range(NT)), list(range(NCT - 1, -1, -1)) + list(range(NT - 1, NCT - 1, -1))]
    masks = [G['maskf'], G['maskb']]
    items = [(step, d, h) for step in range(NT) for d in range(2) for h in range(8)]
    loaded = set()

    def loads(step, d):
        if (step, d) in loaded:
            return
        loaded.add((step, d))
        c = order[d][step]; bi = step % NB; bk = '%d_%d' % (d, bi)
        sl = slice(c * 128, (c + 1) * 128)
        if c >= NCT:
            ph.dma('sp', lambda e: e.dma_start(out=QT[d][bi][:], in_=io['QT_m'][:, :, sl].rearrange("h p t -> p h t")),
                   writes=['QT' + bk])
            ph.dma('sp', lambda e: e.dma_start(out=KT[d][bi][:], in_=io['KT_m'][:, :, sl].rearrange("h p t -> p h t")),
                   writes=['KT' + bk])
        ph.dma('sp', lambda e: e.dma_start(out=Kk[d][bi][:], in_=io['Ktok'][c]), writes=['Kk' + bk])
        ph.dma('sp', lambda e: e.dma_start(out=Va[d][bi][:, :, 0:128], in_=io['Vtok'][c]), writes=['Va' + bk])

    def stage_a(i):
        step, d, h = items[i]
        loads(step, d)
        c = order[d][step]; bi = step % NB; bk = '%d_%d' % (d, bi); i2 = i % 2
        if c < NCT:
            return
        sl = slice(c * 128, (c + 1) * 128)
        ph.op('pe', lambda e: e.matmul(psA[i2][:], lhsT=KT[d][bi][:, h, :], rhs=QT[d][bi][:, h, :], start=True, stop=True),
              reads=['KT' + bk, 'QT' + bk], writes=['psA%d' % i2])
        ph.op('pe', lambda e: e.matmul(psB[i2][:], lhsT=G['sel'][:, h, :], rhs=MF[d][:, sl], start=True, stop=False),
              reads=['TS'], writes=['psB%d' % i2])
        ph.op('pe', lambda e: e.matmul(psB[i2][:], lhsT=G['ident'][:], rhs=masks[d][:], start=False, stop=True),
              reads=[], writes=['psB%d' % i2])
        ph.op('act', lambda e: e.activation(out=Dm[i2][:], in_=psB[i2][:], func=AF.Exp, scale=-1.0,
                                            bias=TS[d][:, c, 0, h:h + 1]), reads=['TS'], writes=['psB%d' % i2, 'Dm%d' % i2])
        ph.op('dve', lambda e: e.tensor_tensor(out=SD[i2][:], in0=psA[i2][:], in1=Dm[i2][:], op=ALU.mult),
              reads=['Dm%d' % i2], writes=['psA%d' % i2, 'SD%d' % i2])

    def stage_b(i):
        step, d, h = items[i]
        c = order[d][step]; bi = step % NB; bk = '%d_%d' % (d, bi); i2 = i % 2
        sk = d * 8 + h
        hob = ho[d][step % 2]; hok = 'ho%d_%d' % (d, step % 2)
        if c >= NCT:
            ph.op('pe', lambda e: e.matmul(psC[:], lhsT=SD[i2][:], rhs=Va[d][bi][:, h, :], start=True, stop=True),
                  reads=['SD%d' % i2, 'Va' + bk], writes=['psC'])
            ph.op('pe', lambda e: e.matmul(psD[:], lhsT=QT[d][bi][:, h, :], rhs=Sb[:, sk, :], start=True, stop=True),
                  reads=['QT' + bk, 'Sb%d' % sk], writes=['psD'])
            ph.op('act', lambda e: e.activation(out=isb[i2][:], in_=psD[:], func=AF.Identity, scale=TS[d][:, c, 1, h:h + 1]),
                  reads=['TS'], writes=['psD', 'isb%d' % i2])
            ph.op('dve', lambda e: e.tensor_tensor(out=tt[i2][:], in0=psC[:], in1=isb[i2][:], op=ALU.add),
                  reads=['isb%d' % i2], writes=['psC', 'tt%d' % i2])
            ph.op('dve', lambda e: e.tensor_scalar(out=dn[i2][:], in0=tt[i2][:, 128:129], scalar1=-1.0,
                                                   scalar2=tt[i2][:, 128:129], op0=ALU.mult, op1=ALU.max),
                  reads=['tt%d' % i2], writes=['dn%d' % i2])
            ph.op('dve', lambda e: e.tensor_scalar(out=dn[i2][:], in0=dn[i2][:], scalar1=TS[d][:, c, 2, h:h + 1],
                                                   scalar2=None, op0=ALU.max), reads=['TS'], writes=['dn%d' % i2])
            ph.op('dve', lambda e: e.reciprocal(out=dn[i2][:], in_=dn[i2][:]), writes=['dn%d' % i2])
            ph.op('dve', lambda e: e.tensor_scalar(out=hob[:, h, :], in0=tt[i2][:, 0:128], scalar1=dn[i2][:, 0:1],
                                                   scalar2=None, op0=ALU.mult),
                  reads=['tt%d' % i2, 'dn%d' % i2], writes=[hok + '_%d' % h])
        ph.op('pool', lambda e: e.tensor_scalar(out=VW[i2][:], in0=Va[d][bi][:, h, :], scalar1=TS[d][:, c, 3, h:h + 1],
                                                scalar2=1.0, op0=ALU.mult, op1=ALU.mult), reads=['Va' + bk, 'TS'], writes=['VW%d' % i2])
        ph.op('pe', lambda e: e.matmul(psE[i2][:], lhsT=Kk[d][bi][:, h, :], rhs=VW[i2][:], start=True, stop=True),
              reads=['Kk' + bk, 'VW%d' % i2], writes=['psE%d' % i2])
        ph.op('dve', lambda e: e.scalar_tensor_tensor(out=S[:, sk, :], in0=S[:, sk, :], scalar=DEC[d][:, c, h:h + 1],
                                                      in1=psE[i2][:], op0=ALU.mult, op1=ALU.add),
              reads=['TS'], writes=['psE%d' % i2, 'S%d' % sk])
        ph.op('act', lambda e: e.copy(out=Sb[:, sk, :], in_=S[:, sk, :]), reads=['S%d' % sk], writes=['Sb%d' % sk])
        if h == 7 and c >= NCT:
            dst = io['Hf'] if d == 0 else io['Hb']
            r0 = (c - NCT) * 128
            ph.dma('sp', lambda e: e.dma_start(out=dst[r0:r0 + 128, :], in_=hob[:].rearrange("p h v -> p (h v)")),
                   reads=[hok + '_%d' % hh for hh in range(8)], writes=['out'])

    stage_a(0)
    for i in range(len(items)):
        if i + 1 < len(items):
            stage_a(i + 1)
        stage_b(i)
    ph.emit()


def phase_attn(nc, G, io, cfg):
    ph = Phase(nc, 'T')
    NS, NO, NT = cfg['NS'], cfg['NO'], cfg['NS'] // 128
    scale = 64 ** -0.5
    KTt = [ph.sb('KT%d' % i, [128, NS], BF16) for i in range(2)]
    Vt = [ph.sb('V%d' % i, [128, NT, 129], BF16) for i in range(2)]
    for i in range(2):
        ph.op('pool', lambda e, i=i: e.memset(Vt[i][:, :, 128:129], 1.0), writes=['V%d' % i])
    QTt = [ph.sb('Q%d' % i, [128, 256], BF16) for i in range(2)]
    ps1 = [ph.ps('ps1_%d' % i, [128, 256]) for i in range(2)]
    ps2 = [ph.ps('ps2_%d' % i, [128, 256]) for i in range(2)]
    psO = [ph.ps('psO%d' % i, [128, 129]) for i in range(4)]
    Pt = [ph.sb('Pt%d' % i, [128, 2, 256], BF16) for i in range(2)]
    anw = ph.sb('anw', [128, 128])
    ph.dma('sp', lambda e: e.dma_start(out=anw[:], in_=bc(io['a_norm_w'], [128, 128])), writes=['anw'])
    ph.op('dve', lambda e: e.tensor_scalar(out=anw[:], in0=anw[:], scalar1=1.0 - LAM_INIT, scalar2=None, op0=ALU.mult),
          writes=['anw'])
    r = ph.sb('r', [128, 4]); o = [ph.sb('o%d' % i, [128, 128]) for i in range(2)]
    junk = ph.sb('junk', [128, 128]); ss = ph.sb('ss', [128, 2])
    yb = [ph.sb('yb%d' % i, [128, 128]) for i in range(2)]
    yst = [ph.sb('yst%d' % i, [128, 256], BF16) for i in range(2)]
    it = 0
    for h in range(8):
        hb = h % 2
        ph.dma('sp', lambda e, h=h, hb=hb: e.dma_start(out=KTt[hb][:], in_=io['KaT'][h]), writes=['KT%d' % hb])
        for t0 in range(0, NT, 16):
            t1 = min(NT, t0 + 16)
            ph.dma('sp', lambda e, h=h, hb=hb, t0=t0, t1=t1: e.dma_start(
                out=Vt[hb][:, t0:t1, 0:128], in_=io['Va'][t0:t1, :, h, :].rearrange("t p v -> p t v")), writes=['V%d' % hb])
        for qb in range(NO // 256):
            qi = it % 2; it += 1
            ph.dma('sp', lambda e, h=h, qb=qb, qi=qi: e.dma_start(out=QTt[qi][:], in_=io['QaT'][h, :, qb * 256:(qb + 1) * 256]),
                   writes=['Q%d' % qi])
            def st_(kt):
                b = kt % 2
                ks = slice(kt * 128, (kt + 1) * 128)
                ph.op('pe', lambda e, ks=ks, b=b, hb=hb, qi=qi: e.matmul(
                    ps1[b][:], lhsT=KTt[hb][0:64, ks], rhs=QTt[qi][0:64, :], start=True, stop=True),
                    reads=['KT%d' % hb, 'Q%d' % qi], writes=['ps1_%d' % b])
                ph.op('pe', lambda e, ks=ks, b=b, hb=hb, qi=qi: e.matmul(
                    ps2[b][:], lhsT=KTt[hb][64:128, ks], rhs=QTt[qi][64:128, :], start=True, stop=True),
                    reads=['KT%d' % hb, 'Q%d' % qi], writes=['ps2_%d' % b])

            st_(0)
            for kt in range(NT):
                b = kt % 2
                if kt + 1 < NT:
                    st_(kt + 1)
                ph.op('act', lambda e, b=b: e.activation(out=Pt[b][:, 0, :], in_=ps1[b][:], func=AF.Exp, scale=scale),
                      writes=['ps1_%d' % b, 'Pt%d' % b])
                ph.op('act', lambda e, b=b: e.activation(out=Pt[b][:, 1, :], in_=ps2[b][:], func=AF.Exp, scale=scale),
                      writes=['ps2_%d' % b, 'Pt%d' % b])
                for mp in range(2):
                    for qs in range(2):
                        oi = mp * 2 + qs
                        ph.op('pe', lambda e, b=b, qs=qs, oi=oi, kt=kt, mp=mp, hb=hb: e.matmul(
                            psO[oi][:], lhsT=Pt[b][:, mp, qs * 128:(qs + 1) * 128], rhs=Vt[hb][:, kt, :],
                            start=(kt == 0), stop=(kt == NT - 1)),
                            reads=['Pt%d' % b, 'V%d' % hb], writes=['psO%d' % oi])
            ysb = yst[qi]; ysk = 'yst%d' % qi
            for qs in range(2):
                o_ = o[qs]; ok = 'o%d' % qs
                ph.op('dve', lambda e, qs=qs: e.reciprocal(out=r[:, qs:qs + 1], in_=psO[qs][:, 128:129]),
                      writes=['psO%d' % qs, 'r%d' % qs])
                ph.op('dve', lambda e, qs=qs: e.reciprocal(out=r[:, 2 + qs:3 + qs], in_=psO[2 + qs][:, 128:129]),
                      writes=['psO%d' % (2 + qs), 'r%d' % (2 + qs)])
                ph.op('dve', lambda e, qs=qs: e.tensor_tensor(out=r[:, 2 + qs:3 + qs], in0=r[:, 2 + qs:3 + qs],
                                                              in1=G['nlam'][:], op=ALU.mult), writes=['r%d' % (2 + qs)])
                ph.op('dve', lambda e, qs=qs, o_=o_: e.tensor_scalar(out=o_[:], in0=psO[qs][:, 0:128], scalar1=r[:, qs:qs + 1],
                                                                     scalar2=None, op0=ALU.mult),
                      reads=['r%d' % qs], writes=['psO%d' % qs, ok])
                ph.op('dve', lambda e, qs=qs, o_=o_: e.scalar_tensor_tensor(
                    out=o_[:], in0=psO[2 + qs][:, 0:128], scalar=r[:, 2 + qs:3 + qs], in1=o_[:], op0=ALU.mult, op1=ALU.add),
                    reads=['r%d' % (2 + qs)], writes=['psO%d' % (2 + qs), ok])
                ph.op('act', lambda e, qs=qs, o_=o_: e.activation(out=junk[:], in_=o_[:], func=AF.Square,
                                                                  accum_out=ss[:, qs:qs + 1]),
                      reads=[ok], writes=['junk', 'ss%d' % qs])
                ph.op('dve', lambda e, qs=qs: e.tensor_scalar(out=ss[:, qs:qs + 1], in0=ss[:, qs:qs + 1], scalar1=1.0 / 128,
                                                              scalar2=EPS, op0=ALU.mult, op1=ALU.add), writes=['ss%d' % qs])
                ph.op('act', lambda e, qs=qs: e.activation(out=ss[:, qs:qs + 1], in_=ss[:, qs:qs + 1], func=AF.Sqrt),
                      writes=['ss%d' % qs])
                ph.op('dve', lambda e, qs=qs: e.reciprocal(out=ss[:, qs:qs + 1], in_=ss[:, qs:qs + 1]), writes=['ss%d' % qs])
                ph.op('dve', lambda e, qs=qs, o_=o_: e.scalar_tensor_tensor(
                    out=yb[qs][:], in0=o_[:], scalar=ss[:, qs:qs + 1], in1=anw[:], op0=ALU.mult, op1=ALU.mult),
                    reads=[ok, 'ss%d' % qs, 'anw'], writes=['yb%d' % qs])
            for qs in range(2):
                ph.op('pe', lambda e, qs=qs: e.transpose(out=psO[qs][:, 0:128], in_=yb[qs][:], identity=G['ident'][:]),
                      reads=['yb%d' % qs], writes=['psO%d' % qs])
                ph.op('act', lambda e, ysb=ysb, qs=qs: e.copy(out=ysb[:, qs * 128:(qs + 1) * 128], in_=psO[qs][:, 0:128]),
                      writes=['psO%d' % qs, ysk])
            ph.dma('sp', lambda e, ysb=ysb, h=h, qb=qb: e.dma_start(out=io['ydT'][h, :, qb * 256:(qb + 1) * 256], in_=ysb[:]),
                   reads=[ysk], writes=['out'])
    ph.emit()


def phase_ym(nc, G, io, cfg):
    ph = Phase(nc, 'Y')
    NO = cfg['NO']
    idx = ph.sb('idx', [128, NO // 128], I32)
    ph.dma('sp', lambda e: e.dma_start(out=idx[:], in_=io['own_idx']), writes=['idx'])
    mw = ph.sb('mw', [128, 1024])
    ph.dma('sp', lambda e: e.dma_start(out=mw[:], in_=bc(io['m_norm_w'], [128, 1024])), writes=['mw'])
    hf = [ph.sb('hf%d' % i, [128, 1024]) for i in range(2)]
    hbt = [ph.sb('hb%d' % i, [128, 1024]) for i in range(2)]
    zt = [ph.sb('z%d' % i, [128, 1024]) for i in range(2)]
    sq = ph.sb('sq', [128, 1024]); ssh = ph.sb('ssh', [128, 8])
    yb = [ph.sb('yb%d' % i, [128, 1024], BF16) for i in range(2)]
    ptr = ph.ps('ptr', [128, 8, 128], BF16)
    yst = [ph.sb('yst%d' % i, [128, 8, 128], BF16) for i in range(2)]
    for t in range(NO // 128):
        b = t % 2
        ph.dma('pool', lambda e, b=b, t=t: e.indirect_dma_start(
            out=hf[b][:], out_offset=None, in_=io['Hf'][:, :],
            in_offset=bass.IndirectOffsetOnAxis(ap=idx[:, t:t + 1], axis=0)), reads=['idx'], writes=['hf%d' % b])
        ph.dma('pool', lambda e, b=b, t=t: e.indirect_dma_start(
            out=hbt[b][:], out_offset=None, in_=io['Hb'][:, :],
            in_offset=bass.IndirectOffsetOnAxis(ap=idx[:, t:t + 1], axis=0)), reads=['idx'], writes=['hb%d' % b])
        ph.dma('sp', lambda e, b=b, t=t: e.dma_start(out=zt[b][:], in_=io['Zo'][t * 128:(t + 1) * 128, :]), writes=['z%d' % b])
        ph.op('dve', lambda e, b=b: e.tensor_tensor(out=hf[b][:], in0=hf[b][:], in1=hbt[b][:], op=ALU.add),
              reads=['hb%d' % b], writes=['hf%d' % b])
        ph.op('pool', lambda e, b=b: e.tensor_tensor(out=sq[:], in0=hf[b][:], in1=hf[b][:], op=ALU.mult),
              reads=['hf%d' % b], writes=['sq'])
        ph.op('dve', lambda e: e.tensor_reduce(out=ssh[:], in_=sq[:].rearrange("p (h v) -> p h v", v=128), axis=AX.X,
                                               op=ALU.add), reads=['sq'], writes=['ssh'])
        ph.op('dve', lambda e: e.tensor_scalar(out=ssh[:], in0=ssh[:], scalar1=1.0 / 128, scalar2=EPS, op0=ALU.mult,
                                               op1=ALU.add), writes=['ssh'])
        ph.op('act', lambda e: e.activation(out=ssh[:], in_=ssh[:], func=AF.Sqrt), writes=['ssh'])
        ph.op('dve', lambda e: e.reciprocal(out=ssh[:], in_=ssh[:]), writes=['ssh'])
        ph.op('dve', lambda e, b=b: e.tensor_tensor(
            out=hf[b][:].rearrange("p (h v) -> p h v", v=128), in0=hf[b][:].rearrange("p (h v) -> p h v", v=128),
            in1=bc(ssh[:].unsqueeze(2), [128, 8, 128]), op=ALU.mult), reads=['ssh'], writes=['hf%d' % b])
        ph.op('pool', lambda e, b=b: e.tensor_tensor(out=zt[b][:], in0=zt[b][:], in1=mw[:], op=ALU.mult), reads=['mw'],
              writes=['z%d' % b])
        ph.op('dve', lambda e, b=b: e.tensor_tensor(out=yb[b][:], in0=hf[b][:], in1=zt[b][:], op=ALU.mult),
              reads=['hf%d' % b, 'z%d' % b], writes=['yb%d' % b])
        for hh in range(8):
            ph.op('pe', lambda e, b=b, hh=hh: e.transpose(out=ptr[:, hh, :], in_=yb[b][:, hh * 128:(hh + 1) * 128],
                                                          identity=G['identb'][:]), reads=['yb%d' % b], writes=['ptr'])
        ph.op('act', lambda e, b=b: e.copy(out=yst[b][:], in_=ptr[:]), writes=['ptr', 'yst%d' % b])
        ph.dma('sp', lambda e, b=b, t=t: e.dma_start(
            out=io['ymT'][:, :, t * 128:(t + 1) * 128].rearrange("h p t -> p h t"), in_=yst[b][:]),
            reads=['yst%d' % b], writes=['out'])
    ph.emit()


def phase_merge(nc, G, io, cfg):
    ph = Phase(nc, 'J')
    NO = cfg['NO']
    wa = ph.sb('wa', [128, 8, 2048], BF16); wb = ph.sb('wb', [128, 8, 2048], BF16)
    load_w(ph, wa, 'wa', io['w_pa'], 0, 2048)
    load_w(ph, wb, 'wb', io['w_pb'], 0, 2048)
    ym = [ph.sb('ym%d' % i, [128, 8, 512], BF16) for i in range(2)]
    yd = [ph.sb('yd%d' % i, [128, 8, 512], BF16) for i in range(2)]
    ga = [ph.sb('ga%d' % i, [128, 512], BF16) for i in range(2)]
    gb = [ph.sb('gb%d' % i, [128, 512], BF16) for i in range(2)]
    pa = [ph.ps('pa%d' % i, [128, 512]) for i in range(2)]
    pb = [ph.ps('pb%d' % i, [128, 512]) for i in range(2)]
    t1 = [ph.sb('t1_%d' % i, [128, 512]) for i in range(2)]
    t2 = [ph.sb('t2_%d' % i, [128, 512]) for i in range(2)]
    mo = [ph.sb('mo%d' % i, [128, 512], BF16) for i in range(2)]
    it = 0
    for bi in range(NO // 512):
        bb = bi % 2
        ts_ = slice(bi * 512, (bi + 1) * 512)
        ph.dma('sp', lambda e, bb=bb, ts_=ts_: e.dma_start(out=ym[bb][:], in_=io['ymT'][:, :, ts_].rearrange("k p t -> p k t")),
               writes=['ym%d' % bb])
        ph.dma('sp', lambda e, bb=bb, ts_=ts_: e.dma_start(out=yd[bb][:], in_=io['ydT'][:, :, ts_].rearrange("k p t -> p k t")),
               writes=['yd%d' % bb])
        for cb in range(16):
            i = it % 2; it += 1
            cs_ = slice(cb * 128, (cb + 1) * 128)
            ph.dma('sp', lambda e, i=i, cb=cb, ts_=ts_: e.dma_start(out=ga[i][:], in_=io['GaT'][cb, :, ts_]), writes=['ga%d' % i])
            ph.dma('sp', lambda e, i=i, cb=cb, ts_=ts_: e.dma_start(out=gb[i][:], in_=io['GbT'][cb, :, ts_]), writes=['gb%d' % i])
            for k in range(8):
                ph.op('pe', lambda e, i=i, k=k, cs_=cs_, bb=bb: e.matmul(pa[i][:], lhsT=wa[:, k, cs_], rhs=ym[bb][:, k, :],
                                                                       start=(k == 0), stop=(k == 7)),
                      reads=['wa', 'ym%d' % bb], writes=['pa%d' % i])
            for k in range(8):
                ph.op('pe', lambda e, i=i, k=k, cs_=cs_, bb=bb: e.matmul(pb[i][:], lhsT=wb[:, k, cs_], rhs=yd[bb][:, k, :],
                                                                       start=(k == 0), stop=(k == 7)),
                      reads=['wb', 'yd%d' % bb], writes=['pb%d' % i])
            ph.op('dve', lambda e, i=i: e.tensor_tensor(out=t1[i][:], in0=pa[i][:], in1=ga[i][:], op=ALU.mult),
                  reads=['ga%d' % i], writes=['pa%d' % i, 't1_%d' % i])
            ph.op('dve', lambda e, i=i: e.tensor_tensor(out=t2[i][:], in0=pb[i][:], in1=gb[i][:], op=ALU.mult),
                  reads=['gb%d' % i], writes=['pb%d' % i, 't2_%d' % i])
            ph.op('pool', lambda e, i=i: e.tensor_tensor(out=mo[i][:], in0=t1[i][:], in1=t2[i][:], op=ALU.add),
                  reads=['t1_%d' % i, 't2_%d' % i], writes=['mo%d' % i])
            ph.dma('sp', lambda e, i=i, cb=cb, ts_=ts_: e.dma_start(out=io['mT'][cb, :, ts_], in_=mo[i][:]),
                   reads=['mo%d' % i], writes=['out'])
    ph.emit()


def phase_wout(nc, G, io, cfg):
    ph = Phase(nc, 'W')
    NO = cfg['NO']
    wo = ph.sb('wo', [128, 16, 2048], BF16)
    load_w(ph, wo, 'wo', io['w_out'], 0, 2048)
    mt = [ph.sb('mt%d' % i, [128, 16, 128], BF16) for i in range(2)]
    xt = [ph.sb('xt%d' % i, [128, 2048]) for i in range(2)]
    pp = [ph.ps('pp%d' % i, [128, 512]) for i in range(4)]
    tm = [ph.sb('tm%d' % i, [128, 512]) for i in range(2)]
    it = 0
    for t in range(NO // 128):
        b = t % 2
        ts_ = slice(t * 128, (t + 1) * 128)
        ph.dma('sp', lambda e, b=b, ts_=ts_: e.dma_start(out=mt[b][:], in_=io['mT'][:, :, ts_].rearrange("k p t -> p k t")),
               writes=['mt%d' % b])
        ph.dma('sp', lambda e, b=b, ts_=ts_: e.dma_start(out=xt[b][:], in_=io['xo'][ts_, :]), reads=[], writes=['xt%d' % b])
        for ch in range(4):
            i = it % 4; j = it % 2; it += 1
            cs_ = slice(ch * 512, (ch + 1) * 512)
            for k in range(16):
                ph.op('pe', lambda e, i=i, k=k, cs_=cs_, b=b: e.matmul(pp[i][:], lhsT=mt[b][:, k, :], rhs=wo[:, k, cs_],
                                                                      start=(k == 0), stop=(k == 15)),
                      reads=['wo', 'mt%d' % b], writes=['pp%d' % i])
            ph.op('dve', lambda e, i=i, j=j, cs_=cs_: e.tensor_tensor(out=tm[j][:], in0=pp[i][:], in1=G['g1_bc'][:, cs_], op=ALU.mult),
                  writes=['pp%d' % i, 'tm%d' % j])
            ph.op('pool', lambda e, j=j, b=b, cs_=cs_: e.tensor_tensor(out=xt[b][:, cs_], in0=xt[b][:, cs_], in1=tm[j][:], op=ALU.add),
                  reads=['tm%d' % j], writes=['xt%d' % b])
        ph.dma('sp', lambda e, b=b, ts_=ts_: e.dma_start(out=io['x1'][ts_, :], in_=xt[b][:]), reads=['xt%d' % b], writes=['out'])
    ph.emit()


def phase_peer_sel(nc, G, io, cfg):
    ph = Phase(nc, 'S')
    NO = cfg['NO']
    skf = ph.sb('skf', [128, 16, 128]); skb = ph.sb('skb', [128, 16, 128], BF16)
    ph.dma('sp', lambda e: e.dma_start(out=skf[:], in_=io['skT']), writes=['skf'])
    ph.op('dve', lambda e: e.tensor_copy(out=skb[:], in_=skf[:]), reads=['skf'], writes=['skb'])
    qt = [ph.sb('qt%d' % i, [128, 16, 128], BF16) for i in range(2)]
    pS = [ph.ps('pS%d' % i, [128, 4, 128]) for i in range(4)]
    Ssb = ph.sb('Ssb', [128, 16, 128]); wk = ph.sb('wk', [128, 128]); v = ph.sb('v', [128, 16, 16])
    cand = ph.sb('cand', [128, 8, 256]); wk2 = ph.sb('wk2', [128, 256]); ts = ph.sb('ts', [128, 8, 16])
    ex = ph.sb('ex', [128, 8, 16]); Z = ph.sb('Z', [128, 8]); th = ph.sb('th', [128, 8])
    E1 = ph.sb('E1', [128, 8, 128]); E2 = ph.sb('E2', [128, 8, 128]); E1p = ph.sb('E1p', [128, 32, 8, 4])
    v4 = v[:].rearrange("p (h two) k -> p h two k", two=2)
    S4 = Ssb[:].rearrange("p (h two) n -> p h two n", two=2)
    for t in range(NO // 128):
        b = t % 2
        ts_ = slice(t * 128, (t + 1) * 128)
        ph.dma('sp', lambda e, b=b, ts_=ts_: e.dma_start(out=qt[b][:], in_=io['qT'][:, :, ts_].rearrange("k p t -> p k t")),
               writes=['qt%d' % b])
        for g in range(4):
            for j in range(4):
                hp = g * 4 + j
                ph.op('pe', lambda e, b=b, g=g, j=j, hp=hp: e.matmul(pS[g][:, j, :], lhsT=qt[b][:, hp, :], rhs=skb[:, hp, :],
                                                                   start=True, stop=True),
                      reads=['qt%d' % b, 'skb'], writes=['pS%d' % g])
            ph.op('act', lambda e, g=g: e.copy(out=Ssb[:, g * 4:(g + 1) * 4, :], in_=pS[g][:]), writes=['pS%d' % g, 'Ssb%d' % g])
        for hp in range(16):
            g = hp // 4
            ph.op('dve', lambda e, hp=hp: e.max(out=v[:, hp, 0:8], in_=Ssb[:, hp, :]), reads=['Ssb%d' % g], writes=['v'])
            ph.op('dve', lambda e, hp=hp: e.match_replace(out=wk[:], in_to_replace=v[:, hp, 0:8], in_values=Ssb[:, hp, :],
                                                          imm_value=-1e30), reads=['Ssb%d' % g], writes=['v', 'wk'])
            ph.op('dve', lambda e, hp=hp: e.max(out=v[:, hp, 8:16], in_=wk[:]), writes=['v', 'wk'])
        ph.op('dve', lambda e: e.tensor_tensor(
            out=cand[:].rearrange("p h (i j) -> p h i j", j=16), in0=bc(v4[:, :, 0, :].unsqueeze(3), [128, 8, 16, 16]),
            in1=bc(v4[:, :, 1, :].unsqueeze(2), [128, 8, 16, 16]), op=ALU.add), reads=['v'], writes=['cand'])
        for h in range(8):
            ph.op('dve', lambda e, h=h: e.max(out=ts[:, h, 0:8], in_=cand[:, h, :]), reads=['cand'], writes=['ts'])
            ph.op('dve', lambda e, h=h: e.match_replace(out=wk2[:], in_to_replace=ts[:, h, 0:8], in_values=cand[:, h, :],
                                                        imm_value=-1e30), reads=['cand'], writes=['ts', 'wk2'])
            ph.op('dve', lambda e, h=h: e.max(out=ts[:, h, 8:16], in_=wk2[:]), writes=['ts', 'wk2'])
        ph.op('dve', lambda e: e.tensor_tensor(out=ex[:], in0=ts[:], in1=bc(ts[:, :, 0:1], [128, 8, 16]), op=ALU.subtract),
              reads=['ts'], writes=['ex'])
        ph.op('act', lambda e: e.activation(out=ex[:], in_=ex[:], func=AF.Exp), writes=['ex'])
        ph.op('dve', lambda e: e.tensor_reduce(out=Z[:], in_=ex[:], axis=AX.X, op=ALU.add), reads=['ex'], writes=['Z'])
        ph.op('dve', lambda e: e.reciprocal(out=Z[:], in_=Z[:]), writes=['Z'])
        ph.op('dve', lambda e: e.tensor_tensor(out=th[:], in0=ex[:, :, 15], in1=Z[:], op=ALU.mult), reads=['ex', 'Z'],
              writes=['th'])
        ph.op('dve', lambda e: e.tensor_tensor(out=E1[:], in0=S4[:, :, 0, :], in1=bc(v4[:, :, 0, 0:1], [128, 8, 128]),
                                               op=ALU.subtract), reads=['Ssb%d' % g for g in range(4)] + ['v'], writes=['E1'])
        ph.op('act', lambda e: e.activation(out=E1[:], in_=E1[:], func=AF.Exp), writes=['E1'])
        ph.op('dve', lambda e: e.tensor_tensor(out=E1[:], in0=E1[:], in1=bc(Z[:].unsqueeze(2), [128, 8, 128]), op=ALU.mult),
              reads=['Z'], writes=['E1'])
        ph.op('pool', lambda e: e.tensor_copy(out=E1p[:], in_=E1[:].rearrange("p h (g j) -> p g h j", j=4)), reads=['E1'],
              writes=['E1p'])
        ph.op('dve', lambda e: e.tensor_tensor(out=E2[:], in0=S4[:, :, 1, :], in1=bc(v4[:, :, 1, 0:1], [128, 8, 128]),
                                               op=ALU.subtract), reads=['Ssb%d' % g for g in range(4)] + ['v'], writes=['E2'])
        ph.op('act', lambda e: e.activation(out=E2[:], in_=E2[:], func=AF.Exp), writes=['E2'])
        ph.dma('sp', lambda e, ts_=ts_: e.dma_start(out=io['E1g'][ts_, :], in_=E1p[:].rearrange("p g h j -> p (g h j)")),
               reads=['E1p'], writes=['out'])
        ph.dma('sp', lambda e, ts_=ts_: e.dma_start(out=io['E2'][ts_, :], in_=E2[:].rearrange("p h n -> p (h n)")),
               reads=['E2'], writes=['out'])
        ph.dma('sp', lambda e, ts_=ts_: e.dma_start(out=io['TH'][ts_, :], in_=th[:]), reads=['th'], writes=['out'])
    ph.emit()


def phase_peer_dense(nc, G, io, cfg):
    ph = Phase(nc, 'K')
    NO = cfg['NO']
    NE = cfg.get('NE', 128)
    NGR = NE // 4
    hpt = ph.sb('hpt', [128, 16, 512], BF16)
    E2t = ph.sb('E2t', [128, 4, 8, 128]); THt = ph.sb('THt', [128, 4, 8])
    E2flat = E2t[:].rearrange("p j h n -> p (j h n)")
    x1t = E2flat[:, 0:2048]; fw = E2flat[:, 2048:4096]
    E1g = [ph.sb('E1g%d' % i, [128, 4, 8, 4]) for i in range(2)]
    Gm = [ph.sb('Gm%d' % i, [128, 4, 8, 4, 128], BF16) for i in range(2)]
    Pt = [ph.sb('Pt%d' % i, [128, 4, 128]) for i in range(2)]
    Ub = [ph.sb('Ub%d' % i, [128, 2048], BF16) for i in range(2)]
    UT = [ph.sb('UT%d' % i, [128, 16, 128], BF16) for i in range(2)]
    Vb = [ph.sb('Vb%d' % i, [128, 2048], BF16) for i in range(4)]
    gel = [ph.sb('gel%d' % i, [128, 512], BF16) for i in range(2)]
    CT = [ph.sb('CT%d' % i, [128, 512], BF16) for i in range(4)]
    acc = ph.sb('acc', [128, 4, 2048]); ss = ph.sb('ss', [128, 1])
    pAT = ph.ps('pAT', [128, 512]); pWT = ph.ps('pWT', [128, 512])
    pUT = [ph.ps('pUT%d' % i, [128, 8, 128], BF16) for i in range(2)]
    pO = [ph.ps('pO%d' % i, [128, 512]) for i in range(4)]
    for tg in range(NO // 512):
        tsl = slice(tg * 512, (tg + 1) * 512)
        ph.dma('sp', lambda e, tsl=tsl: e.dma_start(out=hpt[:], in_=io['hpT'][:, :, tsl].rearrange("k p t -> p k t")),
               writes=['hpt'])
        ph.dma('sp', lambda e, tsl=tsl: e.dma_start(out=E2t[:].rearrange("p j h n -> p j (h n)"),
                                                    in_=io['E2'][tsl, :].rearrange("(j p) c -> p j c", p=128)), writes=['E2t'])
        ph.dma('sp', lambda e, tsl=tsl: e.dma_start(out=THt[:], in_=io['TH'][tsl, :].rearrange("(j p) c -> p j c", p=128)),
               writes=['E2t'])

        def mask(g, tsl=tsl):
            eg = E1g[g % 2]; egk = 'E1g%d' % (g % 2); gm = Gm[g % 2]; gk = 'Gm%d' % (g % 2)
            ph.dma('sp', lambda e: e.dma_start(
                out=eg[:].rearrange("p j h q -> p j (h q)"),
                in_=io['E1g'][tsl, g * 32:(g + 1) * 32].rearrange("(j p) c -> p j c", p=128)), writes=[egk])
            for j in range(4):
                for h in range(8):
                    pi = (j * 8 + h) % 2
                    ph.op('pool', lambda e, j=j, h=h, pi=pi: e.tensor_tensor(
                        out=Pt[pi][:], in0=bc(eg[:, j, h, :].unsqueeze(2), [128, 4, 128]),
                        in1=bc(E2t[:, j, h, :].unsqueeze(1), [128, 4, 128]), op=ALU.mult),
                        reads=[egk, 'E2t'], writes=['Pt%d' % pi])
                    ph.op('dve', lambda e, j=j, h=h, pi=pi: e.scalar_tensor_tensor(
                        out=gm[:, j, h, :, :], in0=Pt[pi][:], scalar=THt[:, j, h:h + 1], in1=Pt[pi][:],
                        op0=ALU.is_ge, op1=ALU.mult), reads=['Pt%d' % pi, 'E2t'], writes=[gk])

        def utr(ei):
            u = ei % 2
            ph.dma('pool', lambda e: e.dma_start(out=Ub[u][:], in_=io['expert_u'][ei * 128:(ei + 1) * 128, :]),
                   writes=['Ub%d' % u])
            for half in range(2):
                for kk in range(8):
                    k = half * 8 + kk
                    ph.op('pe', lambda e, k=k, kk=kk, half=half: e.transpose(
                        out=pUT[half][:, kk, :], in_=Ub[u][:, k * 128:(k + 1) * 128], identity=G['identb'][:]),
                        reads=['Ub%d' % u], writes=['pUT%d' % half])
                ph.op('act', lambda e, half=half: e.copy(out=UT[u][:, half * 8:(half + 1) * 8, :], in_=pUT[half][:]),
                      writes=['pUT%d' % half, 'UT%d_%d' % (u, half)])

        mask(0)
        utr(0)
        for g in range(NGR):
            gm = Gm[g % 2]; gk = 'Gm%d' % (g % 2)
            if g + 1 < NGR:
                mask(g + 1)
            for el in range(4):
                ei = g * 4 + el
                u = ei % 2
                ph.dma('pool', lambda e, el=el, ei=ei: e.dma_start(out=Vb[el][:], in_=io['expert_v'][ei * 128:(ei + 1) * 128, :]),
                       writes=['Vb%d' % el])
                if ei + 1 < NE:
                    utr(ei + 1)
                for k in range(16):
                    ph.op('pe', lambda e, u=u, k=k: e.matmul(pAT[:], lhsT=UT[u][:, k, :], rhs=hpt[:, k, :], start=(k == 0),
                                                            stop=(k == 15)),
                          reads=['UT%d_0' % u, 'UT%d_1' % u, 'hpt'], writes=['pAT'])
                for j in range(4):
                    for h in range(8):
                        ph.op('pe', lambda e, j=j, h=h, el=el, gm=gm: e.matmul(
                            pWT[:, j * 128:(j + 1) * 128], lhsT=gm[:, j, h, el, :], rhs=G['identb'][:], start=(h == 0),
                            stop=(h == 7)), reads=[gk], writes=['pWT'])
                gl = gel[u]
                ph.op('act', lambda e, gl=gl: e.activation(out=gl[:], in_=pAT[:], func=AF.Gelu), writes=['pAT', 'gel%d' % u])
                ph.op('dve', lambda e, gl=gl, el=el: e.tensor_tensor(out=CT[el][:], in0=pWT[:], in1=gl[:], op=ALU.mult),
                      reads=['gel%d' % u], writes=['pWT', 'CT%d' % el])
            for j in range(4):
                for cc in range(4):
                    for el in range(4):
                        ph.op('pe', lambda e, j=j, cc=cc, el=el: e.matmul(
                            pO[cc][:], lhsT=CT[el][:, j * 128:(j + 1) * 128], rhs=Vb[el][:, cc * 512:(cc + 1) * 512],
                            start=(el == 0), stop=(el == 3)), reads=['CT%d' % el, 'Vb%d' % el], writes=['pO%d' % cc])
                    if g == 0:
                        ph.op('act', lambda e, j=j, cc=cc: e.copy(out=acc[:, j, cc * 512:(cc + 1) * 512], in_=pO[cc][:]),
                              writes=['pO%d' % cc, 'acc%d' % j])
                    else:
                        ph.op('dve', lambda e, j=j, cc=cc: e.tensor_tensor(
                            out=acc[:, j, cc * 512:(cc + 1) * 512], in0=pO[cc][:], in1=acc[:, j, cc * 512:(cc + 1) * 512],
                            op=ALU.add), writes=['pO%d' % cc, 'acc%d' % j])
        ph.dma('sp', lambda e: e.dma_start(out=fw, in_=bc(io['final_norm_w'], [128, 2048])), writes=['E2t'])
        junk = Ub[0]
        for j in range(4):
            rows = slice(tg * 512 + j * 128, tg * 512 + (j + 1) * 128)
            ak = 'acc%d' % j
            ph.dma('sp', lambda e, rows=rows: e.dma_start(out=x1t, in_=io['x1'][rows, :]), reads=['E2t'], writes=['x1t'])
            ph.op('pool', lambda e, j=j: e.tensor_tensor(out=acc[:, j, :], in0=acc[:, j, :], in1=G['g2_bc'][:], op=ALU.mult),
                  writes=[ak])
            ph.op('dve', lambda e, j=j: e.tensor_tensor(out=acc[:, j, :], in0=acc[:, j, :], in1=x1t, op=ALU.add),
                  reads=['x1t'], writes=[ak])
            ph.op('act', lambda e, j=j: e.activation(out=junk[:], in_=acc[:, j, :], func=AF.Square, accum_out=ss[:, 0:1]),
                  reads=[ak], writes=['Ub0', 'ss'])
            ph.op('dve', lambda e: e.tensor_scalar(out=ss[:], in0=ss[:], scalar1=1.0 / 2048, scalar2=EPS, op0=ALU.mult,
                                                   op1=ALU.add), writes=['ss'])
            ph.op('act', lambda e: e.activation(out=ss[:], in_=ss[:], func=AF.Sqrt), writes=['ss'])
            ph.op('dve', lambda e: e.reciprocal(out=ss[:], in_=ss[:]), writes=['ss'])
            ph.op('dve', lambda e, j=j: e.scalar_tensor_tensor(out=acc[:, j, :], in0=acc[:, j, :], scalar=ss[:, 0:1], in1=fw,
                                                               op0=ALU.mult, op1=ALU.mult), reads=['ss', 'E2t'], writes=[ak])
            ph.dma('sp', lambda e, j=j, rows=rows: e.dma_start(out=io['y'][rows, :], in_=acc[:, j, :]), reads=[ak],
                   writes=['yout'])
    ph.emit()


IN_SPECS = None


def build(cfg, upto=99):
    NS, NCTX, NO, NLAT = cfg['NS'], cfg['NCTX'], cfg['NO'], cfg['NS'] - cfg['NCTX']
    NT = NS // 128
    nc = bass.Bass("TRN2", target_bir_lowering=False)
    io = {}

    def inp(n, shape, dt=F32):
        io[n] = nc.dram_tensor(n, list(shape), dt, kind="ExternalInput").ap()

    def scr(n, shape, dt=F32):
        io[n] = nc.dram_tensor(n, list(shape), dt).ap()

    inp('xs', [NS, D]); inp('xo', [NO, D]); inp('own_idx', [128, NO // 128], I32)
    inp('c_t', [128, 16, 2]); inp('w_mod', [D, 6 * D]); inp('b_modT', [128, 96]); inp('norm1T', [128, 16])
    inp('norm2T', [128, 16]); inp('final_norm_w', [1, D])
    inp('w_in', [D, P_IN]); inp('b_in', [1, P_IN]); inp('b_qk', [128, 16]); inp('b_gate', [8, 4]); inp('b_gab', [128, 32])
    inp('zeros16', [128, 16])
    inp('conv_wT', [128, 16, 5]); inp('conv_bT', [128, 16]); inp('m_norm_w', [1, 1024]); inp('a_norm_w', [1, 128])
    inp('lambdas', [1, 256]); inp('ropeS', [NS, 64]); inp('ropeO', [NO, 64])
    inp('w_pa', [1024, D]); inp('w_pb', [1024, D]); inp('w_out', [D, D]); inp('w_pq', [D, D]); inp('skT', [128, 16, 128])
    nexp_rows = NEXP if upto >= 19 else 128
    inp('expert_u', [nexp_rows, D]); inp('expert_v', [nexp_rows, D])
    inp('ident', [128, 128]); inp('maskf', [128, 128]); inp('maskb', [128, 128]); inp('ones', [128, 128]); inp('sel', [8, 8, 128])
    io['y'] = nc.dram_tensor('y', [NO, D], F32, kind="ExternalOutput").ap()
    scr('gvec', [2, D]); scr('hTs', [16, 128, NS], BF16); scr('hTo', [16, 128, NO], BF16)
    scr('QKpre', [16, 128, NS + 6]); scr('GT', [4, 8, NS]); scr('QT_m', [8, 128, NS], BF16); scr('KT_m', [8, 128, NS], BF16)
    scr('Ktok', [NT, 128, 8, 128], BF16); scr('Vtok', [NT, 128, 8, 128], BF16); scr('KaT', [8, 128, NS], BF16)
    scr('Va', [NT, 128, 8, 128], BF16)
    scr('TS', [2, 128, NT * 32]); scr('DEC', [2, 128, NT * 8]); scr('MF', [2, 8, NS])
    scr('Hf', [NLAT, 1024]); scr('Hb', [NLAT, 1024]); scr('Zo', [NO, 1024]); scr('QaT', [8, 128, NO], BF16)
    scr('GaT', [16, 128, NO], BF16); scr('GbT', [16, 128, NO], BF16); scr('ymT', [8, 128, NO], BF16); scr('ydT', [8, 128, NO], BF16)
    scr('mT', [16, 128, NO], BF16); scr('x1', [NO, D]); scr('hpT', [16, 128, NO], BF16); scr('qT', [16, 128, NO], BF16)
    scr('E1g', [NO, 1024]); scr('E2', [NO, 1024]); scr('TH', [NO, 8])

    es = ExitStack()
    SEMSTATE[0] = SemState(nc, es)
    G = {}
    for n, shp, dt in (('modT', [128, 96, 2], F32), ('a1', [128, 16], F32), ('a1c', [128, 16], F32), ('a2', [128, 16], F32),
                       ('g1_bc', [128, 2048], F32), ('g2_bc', [128, 2048], F32), ('ident', [128, 128], F32),
                       ('identb', [128, 128], BF16), ('maskf', [128, 128], F32), ('maskb', [128, 128], F32),
                       ('ones', [128, 128], F32), ('onesb', [128, 128], BF16), ('sel', [8, 8, 128], F32), ('nlam', [128, 1], F32)):
        G[n] = es.enter_context(nc.sbuf_tensor('G_' + n, shp, dt))
    modT = G['modT']
    NCT = NCTX // 128
    oblk = [(i * 512, min(512, NO - i * 512)) for i in range((NO + 511) // 512)]
    steps = [
        lambda: phase_init(nc, G, io),
        lambda: phase_mod(nc, G, io),
        lambda: phase_norm(nc, G, 'B', io['xs'], io['hTs'], NS,
                           [(0, NCT, G['a1c'], modT[:, 0:16, 1]), (NCT, NT, G['a1'], modT[:, 0:16, 0])]),
        lambda: phase_norm(nc, G, 'Bo', io['xo'], io['hTo'], NO, [(0, 9999, G['a1'], modT[:, 0:16, 0])]),
        lambda: phase_gemm_fm(nc, G, 'C1', io, cfg, io['hTs'], NS, blocks(cfg), [
            dict(c0=OFF['qm'], ncb=16, bias=io['b_qk'], func=AF.Identity, odt=F32,
                 dst=lambda cb, s0, n: io['QKpre'][cb, :, pcol(cfg, s0):pcol(cfg, s0) + n]),
            dict(c0=OFF['g'], ncb=4, m=8, bias=io['b_gate'], func=AF.Identity, odt=F32,
                 dst=lambda cb, s0, n: io['GT'][cb, :, s0:s0 + n])], io['w_in']),
        lambda: phase_gemm_tm(nc, G, 'C2', io, cfg, io['hTs'], NS, io['w_in'], [
            dict(c0=OFF['vm'], ncols=1024, kind='bf16', dst=lambda t: io['Vtok'][t].rearrange("p h v -> p (h v)")),
            dict(c0=OFF['ka'], ncols=1024, kind='rope',
                 dst=lambda t: io['KaT'][:, :, t * 128:(t + 1) * 128].rearrange("h p t -> p h t")),
            dict(c0=OFF['va'], ncols=1024, kind='bf16', dst=lambda t: io['Va'][t].rearrange("p h v -> p (h v)"))],
            rope_tab=io['ropeS']),
        lambda: phase_conv(nc, G, io, cfg),
        lambda: phase_gates(nc, G, io, cfg),
        lambda: phase_scan(nc, G, io, cfg),
        lambda: phase_gemm_fm(nc, G, 'G1', io, cfg, io['hTo'], NO, oblk, [
            dict(c0=OFF['ga'], ncb=16, bias=io['b_gab'][:, 0:16], func=AF.Sigmoid, odt=BF16,
                 dst=lambda cb, s0, n: io['GaT'][cb, :, s0:s0 + n]),
            dict(c0=OFF['gb'], ncb=16, bias=io['b_gab'][:, 16:32], func=AF.Sigmoid, odt=BF16,
                 dst=lambda cb, s0, n: io['GbT'][cb, :, s0:s0 + n])], io['w_in']),
        lambda: phase_gemm_tm(nc, G, 'G2', io, cfg, io['hTo'], NO, io['w_in'], [
            dict(c0=OFF['zm'], ncols=1024, kind='sig', dst=lambda t: io['Zo'][t * 128:(t + 1) * 128, :]),
            dict(c0=OFF['qa'], ncols=1024, kind='rope',
                 dst=lambda t: io['QaT'][:, :, t * 128:(t + 1) * 128].rearrange("h p t -> p h t"))],
            rope_tab=io['ropeO']),
        lambda: phase_attn(nc, G, io, cfg),
        lambda: phase_ym(nc, G, io, cfg),
        lambda: phase_merge(nc, G, io, cfg),
        lambda: phase_wout(nc, G, io, cfg),
        lambda: phase_norm(nc, G, 'N2', io['x1'], io['hpT'], NO, [(0, 9999, G['a2'], modT[:, 48:64, 0])]),
        lambda: phase_gemm_fm(nc, G, 'Q', io, cfg, io['hpT'], NO, oblk, [
            dict(c0=0, ncb=16, bias=io['zeros16'], func=AF.Identity, odt=BF16,
                 dst=lambda cb, s0, n: io['qT'][cb, :, s0:s0 + n])], io['w_pq']),
        lambda: phase_peer_sel(nc, G, io, cfg),
        lambda: phase_peer_dense(nc, G, io, cfg),
    ]
    for i, st in enumerate(steps):
        if i >= upto:
            break
        st()
    es.close()
    return nc, io


def rope_tables(nrows_lat, grid_w=64):
    rows = nrows_lat // grid_w
    row = np.repeat(np.arange(rows, dtype=np.float32), grid_w)
    col = np.tile(np.arange(grid_w, dtype=np.float32), rows)
    inv = (np.float32(10000.0) ** (-np.arange(0, 32, 2, dtype=np.float32) / np.float32(32))).astype(np.float32)
    ang = np.concatenate([row[:, None] * inv, col[:, None] * inv], axis=-1).astype(np.float32)
    return np.concatenate([np.cos(ang), np.sin(ang)], axis=-1).astype(np.float32)


def fm(v):
    return np.ascontiguousarray(np.asarray(v, np.float32).reshape(-1, 128).T)


def host_inputs(inputs, cfg, b, r):
    NCTX, NO = cfg['NCTX'], cfg['NO']
    f = lambda a: np.ascontiguousarray(np.asarray(a, np.float32))
    x = f(inputs['x'][b]); ctx = f(inputs['ctx'][b])
    NLAT = x.shape[0]
    lo = r * NO
    m = {}
    m['xs'] = np.concatenate([ctx, x], axis=0)
    m['xo'] = np.ascontiguousarray(x[lo:lo + NO])
    m['own_idx'] = np.ascontiguousarray(np.arange(lo, lo + NO, dtype=np.int32).reshape(NO // 128, 128).T)
    m['c_t'] = np.ascontiguousarray(np.stack([fm(inputs['c'][b]), fm(inputs['c_ctx'])], axis=-1))
    m['w_mod'] = f(inputs['w_mod'][0]); m['b_modT'] = fm(inputs['b_mod'][0])
    m['norm1T'] = fm(inputs['norm1_w'][0]); m['norm2T'] = fm(inputs['norm2_w'][0])
    m['final_norm_w'] = f(inputs['final_norm_w']).reshape(1, D)
    m['w_in'] = f(inputs['w_in'][0]); bi = f(inputs['b_in'][0]); m['b_in'] = bi.reshape(1, P_IN)
    m['b_qk'] = fm(bi[0:2048]); m['b_gate'] = np.ascontiguousarray(bi[4096:4128].reshape(4, 8).T)
    m['b_gab'] = fm(bi[7200:11296]); m['zeros16'] = np.zeros((128, 16), np.float32)
    cw = f(inputs['conv_w'][0])
    m['conv_wT'] = np.ascontiguousarray(cw.reshape(5, 16, 128).transpose(2, 1, 0))
    m['conv_bT'] = fm(inputs['conv_b'][0])
    m['m_norm_w'] = f(inputs['m_norm_w'][0]).reshape(1, 1024); m['a_norm_w'] = f(inputs['a_norm_w'][0]).reshape(1, 128)
    m['lambdas'] = f(inputs['lambdas'][0]).reshape(1, 256)
    rt = rope_tables(NLAT)
    rs = np.zeros((NCTX, 64), np.float32); rs[:, 0:32] = 1.0
    m['ropeS'] = np.concatenate([rs, rt], axis=0); m['ropeO'] = np.ascontiguousarray(rt[lo:lo + NO])
    m['w_pa'] = f(inputs['w_pa'][0]); m['w_pb'] = f(inputs['w_pb'][0]); m['w_out'] = f(inputs['w_out'][0])
    m['w_pq'] = f(inputs['w_pq'][0])
    sk = f(inputs['sub_keys'][0])
    m['skT'] = np.ascontiguousarray(sk.reshape(16, 128, 128).transpose(2, 0, 1))
    m['expert_u'] = f(inputs['expert_u'][0]); m['expert_v'] = f(inputs['expert_v'][0])
    m['ident'] = np.eye(128, dtype=np.float32)
    s_ = np.arange(128)[:, None]; t_ = np.arange(128)[None, :]
    m['maskf'] = np.where(s_ > t_, BIG, 0.0).astype(np.float32); m['maskb'] = np.where(s_ < t_, BIG, 0.0).astype(np.float32)
    m['ones'] = np.ones((128, 128), np.float32)
    sel = np.zeros((8, 8, 128), np.float32)
    for j in range(8):
        sel[j, j, :] = 1.0
    m['sel'] = sel
    return m


def kernel(**inputs):
    x = np.asarray(inputs['x'])
    B, T, _ = x.shape
    NCTX = np.asarray(inputs['ctx']).shape[1]
    R = 8 // B
    cfg = dict(NCTX=NCTX, NS=NCTX + T, NO=T // R)
    nc, _ = build(cfg)
    in_maps = [host_inputs(inputs, cfg, c // R, c % R) for c in range(8)]
    res = run_bass_kernel_spmd(nc, in_maps, core_ids=list(range(8)))
    out = np.empty((B, T, D), np.float32)
    for c in range(8):
        b, r = c // R, c % R
        out[b, r * cfg['NO']:(r + 1) * cfg['NO']] = res.results[c]['y']
    return out
```

```python
import numpy as np
from contextlib import ExitStack
import concourse.bass as bass
import concourse.mybir as mybir
from concourse.bass_utils import run_bass_kernel_spmd

F32 = mybir.dt.float32
BF16 = mybir.dt.bfloat16
I32 = mybir.dt.int32
AF = mybir.ActivationFunctionType
ALU = mybir.AluOpType
AX = mybir.AxisListType

D = 2048
KC = 16
EPS = 1e-6
BIG = 30000.0
OFF = dict(qm=0, km=1024, vm=2048, zm=3072, g=4096, qa=4128, ka=5152, va=6176, ga=7200, gb=9248)
P_IN = 11296
NEXP = 16384
LAM_INIT = 0.8 - 0.6

ENG_ATTR = {'pe': 'tensor', 'dve': 'vector', 'act': 'scalar', 'pool': 'gpsimd', 'sp': 'sync'}
NDMASEM = 12


class Phase:
    def __init__(self, nc, name):
        self.nc = nc
        self.name = name
        self.ops = []
        self.state = {}
        st = SEMSTATE[0]
        self.ncomp = dict(st.ncomp)
        self.ncomp0 = dict(st.ncomp)
        self.ndma = {e: 0 for e in ENG_ATTR}
        self.dmasem_total = dict(st.dtot)
        self.dmasem_last = {}
        self.es = ExitStack()

    def sb(self, name, shape, dt=F32):
        return self.es.enter_context(self.nc.sbuf_tensor(self.name + '_' + name, list(shape), dt))

    def ps(self, name, shape, dt=F32):
        esz = 4 if dt == F32 else 2
        n = int(np.prod(shape[1:]))
        assert n * esz <= 2048
        t = self.es.enter_context(self.nc.psum_tensor(self.name + '_' + name, [128, 2048 // esz], dt))
        v = t[:shape[0], 0:n]
        if len(shape) == 3:
            v = v.rearrange("p (a b) -> p a b", b=shape[2])
        return v

    def _deps(self, reads, writes):
        deps = set()
        for k in reads:
            st = self.state.get(k)
            if st and st[0] is not None:
                deps.add(st[0])
        for k in writes:
            st = self.state.get(k)
            if st:
                if st[0] is not None:
                    deps.add(st[0])
                deps.update(st[1])
        return deps

    def _commit(self, oid, reads, writes):
        for k in reads:
            self.state.setdefault(k, [None, []])[1].append(oid)
        for k in writes:
            self.state[k] = [oid, []]

    def op(self, eng, fn, reads=(), writes=()):
        oid = len(self.ops)
        deps = self._deps(reads, writes)
        self.ncomp[eng] += 1
        self.ops.append(dict(id=oid, eng=eng, fn=fn, dma=False, deps=deps, seq=self.ncomp[eng]))
        self._commit(oid, reads, writes)
        return oid

    def dma(self, eng, fn, reads=(), writes=()):
        oid = len(self.ops)
        deps = self._deps(reads, writes)
        slot = self.ndma[eng] % NDMASEM
        self.ndma[eng] += 1
        key = (eng, slot)
        prev = self.dmasem_last.get(key)
        if prev is not None:
            deps.add(prev)
        tot = self.dmasem_total.get(key, 0) + 16
        self.dmasem_total[key] = tot
        self.dmasem_last[key] = oid
        self.ops.append(dict(id=oid, eng=eng, fn=fn, dma=True, deps=deps, semkey=key, semval=tot))
        self._commit(oid, reads, writes)
        return oid

    def emit(self):
        nc = self.nc
        st = SEMSTATE[0]
        csem, dsem = st.csem, st.dsem
        waited = {e: {} for e in ENG_ATTR}
        for o in self.ops:
            w = {}
            for d in o['deps']:
                do = self.ops[d]
                if do['dma']:
                    s, v = ('d', do['semkey']), do['semval']
                else:
                    if do['eng'] == 'pe' and o['eng'] == 'pe' and not o['dma']:
                        continue
                    s, v = ('c', do['eng']), do['seq']
                if w.get(s, 0) < v:
                    w[s] = v
            wl = []
            for s, v in w.items():
                if waited[o['eng']].get(s, 0) < v:
                    waited[o['eng']][s] = v
                    wl.append((s, v))
            o['waits'] = wl
        finals = [(('d', k), v) for k, v in self.dmasem_total.items() if v > st.dtot.get(k, 0)]
        finals += [(('c', e), self.ncomp[e]) for e in ENG_ATTR if self.ncomp[e] > self.ncomp0[e]]

        def semof(s):
            return dsem[s[1]] if s[0] == 'd' else csem[s[1]]

        with nc.Block() as block:
            for e, attr in ENG_ATTR.items():
                mine = [o for o in self.ops if o['eng'] == e]
                if not mine and e != 'sp':
                    continue

                def body(engine, mine=mine, e=e):
                    for o in mine:
                        for s, v in o['waits']:
                            engine.wait_ge(semof(s), v)
                        ins = o['fn'](engine)
                        if o['dma']:
                            ins.then_inc(dsem[o['semkey']], 16)
                        else:
                            ins.then_inc(csem[e], 1)
                    if e == 'sp':
                        for s, v in finals:
                            engine.wait_ge(semof(s), v)
                getattr(block, attr)(body)
        st.ncomp = dict(self.ncomp)
        st.dtot = dict(self.dmasem_total)
        self.es.close()


class SemState:
    def __init__(self, nc, es):
        self.csem = {e: es.enter_context(nc.semaphore('c_' + e)) for e in ENG_ATTR}
        self.dsem = {(e, i): es.enter_context(nc.semaphore('d_%s%d' % (e, i))) for e in ('sp', 'pool') for i in range(NDMASEM)}
        self.ncomp = {e: 0 for e in ENG_ATTR}
        self.dtot = {}


SEMSTATE = [None]


def bc(ap, shape):
    return ap.to_broadcast(list(shape))


def blocks(cfg):
    out = []
    s = 0
    while s < cfg['NCTX']:
        n = min(512, cfg['NCTX'] - s)
        out.append((s, n)); s += n
    while s < cfg['NS']:
        n = min(512, cfg['NS'] - s)
        out.append((s, n)); s += n
    return out


def pcol(cfg, s):
    return s + 2 if s < cfg['NCTX'] else s + 4


def load_w(ph, wt, key, w_dram, c0, c1):
    K = w_dram.shape[0]
    wv = w_dram.rearrange("(k p) c -> p k c", p=128)
    for k in range(K // 128):
        ph.dma('pool', lambda e, k=k: e.dma_start(out=wt[:, k, 0:c1 - c0], in_=wv[:, k, c0:c1]), writes=[key])


def phase_init(nc, G, io):
    ph = Phase(nc, 'I')
    for n in ('ident', 'maskf', 'maskb', 'ones'):
        ph.dma('sp', lambda e, n=n: e.dma_start(out=G[n][:], in_=io[n]), writes=[n])
    ph.dma('sp', lambda e: e.dma_start(out=G['sel'][:], in_=io['sel']), writes=['sel'])
    ph.op('dve', lambda e: e.tensor_copy(out=G['identb'][:], in_=G['ident'][:]), reads=['ident'], writes=['identb'])
    ph.op('dve', lambda e: e.tensor_copy(out=G['onesb'][:], in_=G['ones'][:]), reads=['ones'], writes=['onesb'])
    lt = ph.sb('lt', [128, 4, 64]); pr = ph.sb('pr', [128, 2, 64]); sm = ph.sb('sm', [128, 2])
    ph.dma('sp', lambda e: e.dma_start(out=lt[:].rearrange("p a b -> p (a b)"), in_=bc(io['lambdas'], [128, 256])),
           writes=['lt'])
    ph.op('dve', lambda e: e.tensor_tensor(out=pr[:, 0, :], in0=lt[:, 0, :], in1=lt[:, 1, :], op=ALU.mult),
          reads=['lt'], writes=['pr0'])
    ph.op('dve', lambda e: e.tensor_tensor(out=pr[:, 1, :], in0=lt[:, 2, :], in1=lt[:, 3, :], op=ALU.mult),
          reads=['lt'], writes=['pr1'])
    ph.op('dve', lambda e: e.tensor_reduce(out=sm[:], in_=pr[:], axis=AX.X, op=ALU.add), reads=['pr0', 'pr1'],
          writes=['sm'])
    ph.op('act', lambda e: e.activation(out=sm[:], in_=sm[:], func=AF.Exp), reads=['sm'], writes=['sm'])
    ph.op('dve', lambda e: e.tensor_tensor(out=G['nlam'][:], in0=sm[:, 1:2], in1=sm[:, 0:1], op=ALU.subtract),
          reads=['sm'], writes=['nlam'])
    ph.op('dve', lambda e: e.tensor_scalar(out=G['nlam'][:], in0=G['nlam'][:], scalar1=-LAM_INIT, scalar2=None,
                                           op0=ALU.add), reads=['nlam'], writes=['nlam'])
    ph.emit()


def phase_mod(nc, G, io):
    ph = Phase(nc, 'A')
    ct = ph.sb('ct', [128, 16, 2]); sc = ph.sb('sc', [128, 16, 2])
    bm = ph.sb('bm', [128, 96]); n1 = ph.sb('n1', [128, 16]); n2 = ph.sb('n2', [128, 16])
    slab = [ph.sb('slab%d' % i, [128, 16, 512]) for i in range(2)]
    pm = ph.ps('pm', [128, 96, 2])
    tmp = ph.sb('tmp', [128, 16])
    ph.dma('sp', lambda e: e.dma_start(out=ct[:], in_=io['c_t']), writes=['ct'])
    ph.dma('sp', lambda e: e.dma_start(out=bm[:], in_=io['b_modT']), writes=['bm'])
    ph.dma('sp', lambda e: e.dma_start(out=n1[:], in_=io['norm1T']), writes=['n1'])
    ph.dma('sp', lambda e: e.dma_start(out=n2[:], in_=io['norm2T']), writes=['n2'])
    ph.op('act', lambda e: e.activation(out=sc[:], in_=ct[:], func=AF.Silu), reads=['ct'], writes=['sc'])
    wv = io['w_mod'].rearrange("(k p) c -> p k c", p=128)
    for s in range(24):
        sl = slab[s % 2]
        ph.dma('sp', lambda e, sl=sl, s=s: e.dma_start(out=sl[:], in_=wv[:, :, s * 512:(s + 1) * 512]),
               writes=['slab%d' % (s % 2)])
        for jj in range(4):
            j = s * 4 + jj
            for k in range(16):
                ph.op('pe', lambda e, sl=sl, jj=jj, j=j, k=k: e.matmul(
                    pm[:, j, :], lhsT=sl[:, k, jj * 128:(jj + 1) * 128], rhs=sc[:, k, :],
                    start=(k == 0), stop=(k == 15)),
                    reads=['slab%d' % (s % 2), 'sc'], writes=['pm'])
    modT = G['modT']
    ph.op('dve', lambda e: e.tensor_tensor(out=modT[:], in0=pm[:], in1=bc(bm[:].unsqueeze(2), [128, 96, 2]),
                                           op=ALU.add), reads=['bm'], writes=['modT', 'pm'])
    for name, nrm, lo, col in (('a1', n1, 16, 0), ('a1c', n1, 16, 1), ('a2', n2, 64, 0)):
        t = G[name]
        ph.op('dve', lambda e, lo=lo, col=col: e.tensor_scalar(out=tmp[:], in0=modT[:, lo:lo + 16, col], scalar1=1.0,
                                                              scalar2=None, op0=ALU.add),
              reads=['modT'], writes=['tmp'])
        ph.op('dve', lambda e, t=t, nrm=nrm: e.tensor_tensor(out=t[:], in0=tmp[:], in1=nrm[:], op=ALU.mult),
              reads=['tmp', 'n1', 'n2'], writes=[name])
    gv = io['gvec']
    for gi, lo in ((0, 32), (1, 80)):
        ph.dma('sp', lambda e, gi=gi, lo=lo: e.dma_start(
            out=gv[gi].rearrange("(k p) -> p k", p=128), in_=modT[:, lo:lo + 16, 0],
            allow_slow_non_contiguous=True), reads=['modT'], writes=['gv%d' % gi])
        ph.dma('sp', lambda e, gi=gi: e.dma_start(
            out=G['g%d_bc' % (gi + 1)][:], in_=bc(gv[gi:gi + 1, :], [128, 2048])),
            reads=['gv%d' % gi], writes=['gbc%d' % gi])
    ph.emit()


def phase_norm(nc, G, name, x, hT, ntok, segs, xres=None):
    ph = Phase(nc, name)
    ntile = (ntok + 127) // 128
    xt = [ph.sb('xt%d' % i, [128, 2048]) for i in range(2)]
    junk = ph.sb('junk', [128, 2048], BF16)
    xn = [ph.sb('xn%d' % i, [128, 2048]) for i in range(2)]
    ss = ph.sb('ss', [128, 2]); rs = ph.sb('rs', [128, 2])
    pt = [ph.ps('pt%d' % i, [128, 4, 128]) for i in range(4)]
    stg = [ph.sb('stg%d' % i, [128, 16, 512], BF16) for i in range(2)]
    ident = G['ident']
    for t in range(ntile):
        np_ = min(128, ntok - t * 128)
        b = t % 2
        for (lo, hi, a_t, s_t) in segs:
            if lo <= t < hi:
                A, S = a_t, s_t
        grp = t // 4; sg = stg[grp % 2]; sgk = 'stg%d' % (grp % 2)
        ph.dma('sp', lambda e, b=b, t=t, np_=np_: e.dma_start(out=xt[b][:np_, :], in_=x[t * 128:t * 128 + np_, :]),
               writes=['xt%d' % b])
        ph.op('act', lambda e, b=b, np_=np_: e.activation(out=junk[:np_, :], in_=xt[b][:np_, :], func=AF.Square,
                                                           accum_out=ss[:np_, b:b + 1]),
              reads=['xt%d' % b], writes=['junk', 'ss%d' % b])
        ph.op('dve', lambda e, b=b, np_=np_: e.tensor_scalar(out=rs[:np_, b:b + 1], in0=ss[:np_, b:b + 1],
                                                             scalar1=1.0 / 2048, scalar2=EPS, op0=ALU.mult, op1=ALU.add),
              reads=['ss%d' % b], writes=['rs%d' % b])
        ph.op('act', lambda e, b=b, np_=np_: e.activation(out=rs[:np_, b:b + 1], in_=rs[:np_, b:b + 1], func=AF.Sqrt),
              reads=['rs%d' % b], writes=['rs%d' % b])
        ph.op('dve', lambda e, b=b, np_=np_: e.reciprocal(out=rs[:np_, b:b + 1], in_=rs[:np_, b:b + 1]),
              reads=['rs%d' % b], writes=['rs%d' % b])
        ph.op('dve', lambda e, b=b, np_=np_: e.tensor_scalar(out=xn[b][:np_, :], in0=xt[b][:np_, :],
                                                             scalar1=rs[:np_, b:b + 1], scalar2=None, op0=ALU.mult),
              reads=['xt%d' % b, 'rs%d' % b], writes=['xn%d' % b])
        for q in range(4):
            pq = q
            for kk in range(4):
                k = q * 4 + kk
                ph.op('pe', lambda e, b=b, k=k, kk=kk, pq=pq, np_=np_: e.transpose(
                    out=pt[pq][:, kk, :np_], in_=xn[b][:np_, k * 128:(k + 1) * 128], identity=ident[:np_, :np_]),
                    reads=['xn%d' % b], writes=['pt%d' % pq])
            c0 = (t % 4) * 128
            eng = 'dve' if q % 2 == 0 else 'act'
            for kk in range(4):
                k = q * 4 + kk
                if eng == 'dve':
                    ph.op('dve', lambda e, k=k, kk=kk, pq=pq, np_=np_, sg=sg, c0=c0, A=A, S=S: e.tensor_scalar(
                        out=sg[:, k, c0:c0 + np_], in0=pt[pq][:, kk, :np_], scalar1=A[:, k:k + 1], scalar2=S[:, k:k + 1],
                        op0=ALU.mult, op1=ALU.add), writes=[sgk + '_%d' % k, 'pt%d' % pq])
                else:
                    ph.op('act', lambda e, k=k, kk=kk, pq=pq, np_=np_, sg=sg, c0=c0, A=A, S=S: e.activation(
                        out=sg[:, k, c0:c0 + np_], in_=pt[pq][:, kk, :np_], func=AF.Identity,
                        scale=A[:, k:k + 1], bias=S[:, k:k + 1]), writes=[sgk + '_%d' % k, 'pt%d' % pq])
        if t % 4 == 3 or t == ntile - 1:
            t0 = grp * 512
            n = t * 128 + np_ - t0
            ph.dma('sp', lambda e, sg=sg, t0=t0, n=n: e.dma_start(
                out=hT[:, :, t0:t0 + n].rearrange("k p t -> p k t"), in_=sg[:, :, :n]),
                reads=[sgk + '_%d' % k for k in range(16)], writes=['hT'])
    ph.emit()


def phase_gemm_fm(nc, G, name, io, cfg, hT, ntok, blks, jobs, w_dram, kchunks=KC):
    ph = Phase(nc, name)
    tot_cols = sum(j['ncb'] * j.get('m', 128) for j in jobs)
    wt = ph.sb('w', [128, kchunks, tot_cols], BF16)
    off = 0
    for j in jobs:
        cw = j['ncb'] * j.get('m', 128)
        j['woff'] = off
        wv = w_dram.rearrange("(k p) c -> p k c", p=128)
        for k in range(kchunks):
            ph.dma('pool', lambda e, k=k, off=off, cw=cw, j=j: e.dma_start(
                out=wt[:, k, off:off + cw], in_=wv[:, k, j['c0']:j['c0'] + cw]), writes=['w'])
        off += cw
        j['bt'] = ph.sb('b%d' % j['c0'], [128, j['ncb']])
        ph.dma('sp', lambda e, j=j: e.dma_start(out=j['bt'][:j.get('m', 128), :], in_=j['bias']), writes=['bias'])
    hb = [ph.sb('h%d' % i, [128, kchunks, 512], BF16) for i in range(2)]
    pp = [ph.ps('pp%d' % i, [128, 512]) for i in range(4)]
    so = {}
    for odt in set(j['odt'] for j in jobs):
        so[odt] = [ph.sb('so%s%d' % (str(odt)[-4:], i), [128, 512], odt) for i in range(4)]
    cnt = 0
    for bi, (s0, n) in enumerate(blks):
        h = hb[bi % 2]; hk = 'h%d' % (bi % 2)
        ph.dma('sp', lambda e, h=h, s0=s0, n=n: e.dma_start(
            out=h[:, :, :n], in_=hT[:, :, s0:s0 + n].rearrange("k p t -> p k t")), writes=[hk])
        for j in jobs:
            m = j.get('m', 128)
            for cb in range(j['ncb']):
                p = pp[cnt % 4]; pk = 'pp%d' % (cnt % 4)
                o = so[j['odt']][cnt % 4]; ok = 'so%s%d' % (str(j['odt'])[-4:], cnt % 4)
                cnt += 1
                for k in range(kchunks):
                    ph.op('pe', lambda e, p=p, k=k, j=j, cb=cb, m=m, h=h, n=n: e.matmul(
                        p[:m, :n], lhsT=wt[:, k, j['woff'] + cb * m:j['woff'] + (cb + 1) * m], rhs=h[:, k, :n],
                        start=(k == 0), stop=(k == kchunks - 1)), reads=['w', hk], writes=[pk])
                ph.op('act', lambda e, p=p, o=o, j=j, cb=cb, m=m, n=n: e.activation(
                    out=o[:m, :n], in_=p[:m, :n], func=j['func'], bias=j['bt'][:m, cb:cb + 1], scale=1.0),
                    reads=['bias'], writes=[pk, ok])
                ph.dma('sp', lambda e, o=o, j=j, cb=cb, m=m, s0=s0, n=n: e.dma_start(
                    out=j['dst'](cb, s0, n), in_=o[:m, :n]), reads=[ok], writes=['out'])
    ph.emit()


def rope_ops(ph, X, cs, O, xkeys, cskey, okey):
    Xv = X[:].rearrange("p (g two d) -> p g two d", two=2, d=32)
    Ov = O[:].rearrange("p (g two d) -> p g two d", two=2, d=32)
    cb_ = bc(cs[:, 0:32].unsqueeze(1), [128, 16, 32])
    sb_ = bc(cs[:, 32:64].unsqueeze(1), [128, 16, 32])
    t1, t2 = ph.rope_tmp
    x1, x2 = Xv[:, :, 0, :], Xv[:, :, 1, :]
    rd = list(xkeys) + [cskey]
    ph.op('pool', lambda e: e.tensor_tensor(out=t1[:], in0=x1, in1=cb_, op=ALU.mult), reads=rd, writes=['rt1'])
    ph.op('pool', lambda e: e.tensor_tensor(out=t2[:], in0=x2, in1=sb_, op=ALU.mult), reads=rd, writes=['rt2'])
    ph.op('pool', lambda e: e.tensor_tensor(out=Ov[:, :, 0, :], in0=t1[:], in1=t2[:], op=ALU.subtract),
          reads=['rt1', 'rt2'], writes=[okey + 'a'])
    ph.op('pool', lambda e: e.tensor_tensor(out=t1[:], in0=x2, in1=cb_, op=ALU.mult), reads=rd, writes=['rt1'])
    ph.op('pool', lambda e: e.tensor_tensor(out=t2[:], in0=x1, in1=sb_, op=ALU.mult), reads=rd, writes=['rt2'])
    ph.op('pool', lambda e: e.tensor_tensor(out=Ov[:, :, 1, :], in0=t1[:], in1=t2[:], op=ALU.add),
          reads=['rt1', 'rt2'], writes=[okey + 'b'])


def phase_gemm_tm(nc, G, name, io, cfg, hT, ntok, w_dram, secs, rope_tab=None):
    ph = Phase(nc, name)
    tot = sum(s['ncols'] for s in secs)
    wt = ph.sb('w', [128, KC, tot], BF16)
    bb = ph.sb('bb', [128, tot])
    off = 0
    wv = w_dram.rearrange("(k p) c -> p k c", p=128)
    for s in secs:
        s['woff'] = off
        for k in range(KC):
            ph.dma('pool', lambda e, k=k, off=off, s=s: e.dma_start(
                out=wt[:, k, off:off + s['ncols']], in_=wv[:, k, s['c0']:s['c0'] + s['ncols']]), writes=['w'])
        ph.dma('sp', lambda e, off=off, s=s: e.dma_start(
            out=bb[:, off:off + s['ncols']], in_=bc(io['b_in'][0:1, s['c0']:s['c0'] + s['ncols']], [128, s['ncols']])),
            writes=['bb'])
        off += s['ncols']
    ntile = ntok // 128
    hb = [ph.sb('h%d' % i, [128, KC, 128], BF16) for i in range(2)]
    pp = [ph.ps('pp%d' % i, [128, 512]) for i in range(4)]
    ptr = [ph.ps('ptr%d' % i, [128, 8, 128], BF16) for i in range(2)]
    obf = [ph.sb('obf%d' % i, [128, 1024], BF16) for i in range(2)]
    of32 = [ph.sb('of%d' % i, [128, 1024]) for i in range(2)]
    cs = [ph.sb('cs%d' % i, [128, 64]) for i in range(2)]
    ph.rope_tmp = (ph.sb('rt1', [128, 16, 32]), ph.sb('rt2', [128, 16, 32]))
    fmst = [ph.sb('fmst%d' % i, [128, 8, 128], BF16) for i in range(2)]
    ropeb = [ph.sb('ropeb%d' % i, [128, 1024], BF16) for i in range(2)]
    cnt = 0
    for t in range(ntile):
        h = hb[t % 2]; hk = 'h%d' % (t % 2)
        ph.dma('sp', lambda e, h=h, t=t: e.dma_start(
            out=h[:], in_=hT[:, :, t * 128:(t + 1) * 128].rearrange("k p t -> p k t")), writes=[hk])
        for si, s in enumerate(secs):
            kind = s['kind']
            ob = obf[(t * len(secs) + si) % 2]; obk = 'obf%d' % ((t * len(secs) + si) % 2)
            of = of32[t % 2]; ofk = 'of%d' % (t % 2)
            for ch in range(s['ncols'] // 512):
                p = pp[cnt % 4]; pk = 'pp%d' % (cnt % 4); cnt += 1
                c = s['woff'] + ch * 512
                for k in range(KC):
                    ph.op('pe', lambda e, p=p, k=k, c=c, h=h: e.matmul(
                        p[:], lhsT=h[:, k, :], rhs=wt[:, k, c:c + 512], start=(k == 0), stop=(k == KC - 1)),
                        reads=['w', hk], writes=[pk])
                dstt = of if kind in ('rope', 'sig') else ob
                dk = ofk if kind in ('rope', 'sig') else obk
                ph.op('dve', lambda e, p=p, c=c, ch=ch, dstt=dstt: e.tensor_tensor(
                    out=dstt[:, ch * 512:(ch + 1) * 512], in0=p[:], in1=bb[:, c:c + 512], op=ALU.add),
                    reads=['bb'], writes=[pk, dk + '_%d' % ch])
            nch = s['ncols'] // 512
            if kind == 'bf16':
                ph.dma('sp', lambda e, ob=ob, s=s, t=t: e.dma_start(out=s['dst'](t), in_=ob[:, :s['ncols']]),
                       reads=[obk + '_%d' % c_ for c_ in range(nch)], writes=['out'])
            elif kind == 'sig':
                ph.op('act', lambda e, of=of: e.activation(out=of[:], in_=of[:], func=AF.Sigmoid),
                      reads=[], writes=[ofk + '_%d' % c_ for c_ in range(nch)])
                ph.dma('sp', lambda e, of=of, s=s, t=t: e.dma_start(out=s['dst'](t), in_=of[:, :s['ncols']]),
                       reads=[ofk + '_%d' % c_ for c_ in range(nch)], writes=['out'])
            elif kind == 'rope':
                c_s = cs[t % 2]; csk = 'cs%d' % (t % 2)
                ph.dma('sp', lambda e, c_s=c_s, t=t: e.dma_start(out=c_s[:], in_=rope_tab[t * 128:(t + 1) * 128, :]),
                       writes=[csk])
                rb = ropeb[t % 2]; rbk = 'ropeb%d' % (t % 2)
                rope_ops(ph, of, c_s, rb, [ofk + '_0', ofk + '_1'], csk, rbk)
                fs = fmst[t % 2]; fk = 'fmst%d' % (t % 2)
                pt_ = ptr[t % 2]; ptk = 'ptr%d' % (t % 2)
                for hh in range(8):
                    ph.op('pe', lambda e, hh=hh, rb=rb, pt_=pt_: e.transpose(
                        out=pt_[:, hh, :], in_=rb[:, hh * 128:(hh + 1) * 128], identity=G['identb'][:]),
                        reads=[rbk + 'a', rbk + 'b'], writes=[ptk])
                ph.op('act', lambda e, fs=fs, pt_=pt_: e.copy(out=fs[:], in_=pt_[:]), writes=[ptk, fk])
                ph.dma('sp', lambda e, fs=fs, s=s, t=t: e.dma_start(out=s['dst'](t), in_=fs[:]),
                       reads=[fk], writes=['out'])
    ph.emit()


def phase_conv(nc, G, io, cfg):
    ph = Phase(nc, 'D')
    NS, NCTX = cfg['NS'], cfg['NCTX']
    blks = blocks(cfg)
    cw = ph.sb('cw', [128, 16, 5]); cbi = ph.sb('cbi', [128, 16]); zt = ph.sb('zt', [128, 16, 2])
    win = [ph.sb('win%d' % i, [128, 516]) for i in range(4)]
    acc = [ph.sb('acc%d' % i, [128, 512]) for i in range(4)]
    sg = [ph.sb('sg%d' % i, [128, 512]) for i in range(4)]
    ob = [ph.sb('ob%d' % i, [128, 512], BF16) for i in range(4)]
    ptr = [ph.ps('ptr%d' % i, [128, 4, 128], BF16) for i in range(2)]
    kst = [ph.sb('kst%d' % i, [128, 4, 128], BF16) for i in range(2)]
    ph.dma('sp', lambda e: e.dma_start(out=cw[:], in_=io['conv_wT']), writes=['cw'])
    ph.dma('sp', lambda e: e.dma_start(out=cbi[:], in_=io['conv_bT']), writes=['cw'])
    ph.op('dve', lambda e: e.memset(zt[:], 0.0), writes=['zt'])
    QK = io['QKpre']
    for g0 in (0, NCTX + 2, NS + 4):
        ph.dma('sp', lambda e, g0=g0: e.dma_start(out=QK[:, :, g0:g0 + 2].rearrange("c p t -> p c t"), in_=zt[:]),
               reads=['zt'], writes=['gap'])
    it = 0
    for cb in range(16):
        h = cb % 8
        scale = 128 ** -0.5 if cb < 8 else 1.0
        dstT = io['QT_m'] if cb < 8 else io['KT_m']
        for (s0, n) in blks:
            b = it % 4; b2 = it % 2; it += 1
            c0 = pcol(cfg, s0)
            w_, a_, s_, o_ = win[b], acc[b], sg[b], ob[b]
            ph.dma('sp', lambda e, w_=w_, cb=cb, c0=c0, n=n: e.dma_start(out=w_[:, :n + 4], in_=QK[cb, :, c0 - 2:c0 + n + 2]),
                   reads=['gap'], writes=['win%d' % b])
            ph.op('dve', lambda e, w_=w_, a_=a_, cb=cb, n=n: e.tensor_scalar(
                out=a_[:, :n], in0=w_[:, 0:n], scalar1=cw[:, cb, 0:1], scalar2=cbi[:, cb:cb + 1], op0=ALU.mult, op1=ALU.add),
                reads=['win%d' % b, 'cw'], writes=['acc%d' % b])
            for j in range(1, 5):
                ph.op('dve', lambda e, w_=w_, a_=a_, cb=cb, n=n, j=j: e.scalar_tensor_tensor(
                    out=a_[:, :n], in0=w_[:, j:j + n], scalar=cw[:, cb, j:j + 1], in1=a_[:, :n], op0=ALU.mult, op1=ALU.add),
                    reads=['win%d' % b, 'cw'], writes=['acc%d' % b])
            ph.op('act', lambda e, a_=a_, s_=s_, n=n: e.activation(out=s_[:, :n], in_=a_[:, :n], func=AF.Sigmoid),
                  reads=['acc%d' % b], writes=['sg%d' % b])
            ph.op('dve', lambda e, a_=a_, s_=s_, o_=o_, n=n, scale=scale: e.scalar_tensor_tensor(
                out=o_[:, :n], in0=a_[:, :n], scalar=scale, in1=s_[:, :n], op0=ALU.mult, op1=ALU.mult),
                reads=['acc%d' % b, 'sg%d' % b], writes=['ob%d' % b])
            ph.dma('sp', lambda e, o_=o_, h=h, s0=s0, n=n, dstT=dstT: e.dma_start(out=dstT[h, :, s0:s0 + n], in_=o_[:, :n]),
                   reads=['ob%d' % b], writes=['out'])
            if cb >= 8:
                nt_ = n // 128
                for ti in range(nt_):
                    ph.op('pe', lambda e, o_=o_, ti=ti, b2=b2: e.transpose(
                        out=ptr[b2][:, ti, :], in_=o_[:, ti * 128:(ti + 1) * 128], identity=G['identb'][:]),
                        reads=['ob%d' % b], writes=['ptr%d' % b2])
                ph.op('act', lambda e, b2=b2, nt_=nt_: e.copy(out=kst[b2][:, :nt_, :], in_=ptr[b2][:, :nt_, :]),
                      writes=['ptr%d' % b2, 'kst%d' % b2])
                t0 = s0 // 128
                ph.dma('sp', lambda e, b2=b2, nt_=nt_, t0=t0, h=h: e.dma_start(
                    out=io['Ktok'][t0:t0 + nt_, :, h, :].rearrange("t p k -> p t k"), in_=kst[b2][:, :nt_, :]),
                    reads=['kst%d' % b2], writes=['out'])
    ph.emit()


def phase_gates(nc, G, io, cfg):
    ph = Phase(nc, 'E')
    NS, NCTX, NT = cfg['NS'], cfg['NCTX'], cfg['NS'] // 128
    NCT = NCTX // 128
    T = [ph.sb('T%d' % i, [8, NS]) for i in range(5)]
    zer = ph.sb('zer', [8, 1]); one = ph.sb('one', [8, 1])
    ph.op('dve', lambda e: e.memset(zer[:], 0.0), writes=['zer'])
    ph.op('dve', lambda e: e.memset(one[:], 1.0), writes=['one'])
    MP = ph.sb('MP', [8, NT]); MN = ph.sb('MN', [8, NT]); dc = ph.sb('dc', [8, NT]); xd = ph.sb('xd', [8, NT, 8])
    tot = ph.sb('tot', [8, 4])
    tsp = [ph.ps('tsp%d' % i, [128, 16, 8]) for i in range(2)]
    dps = ph.ps('dps', [128, 512])
    tss = ph.sb('tss', [128, NT, 4, 8]); dcs = ph.sb('dcs', [128, NT * 8])
    GT = io['GT']
    segs = [(0, NCTX), (NCTX, NS)]
    LF, P, A, M, E = T
    for d in range(2):
        ti, tf = (0, 1) if d == 0 else (2, 3)
        ph.dma('sp', lambda e, tf=tf: e.dma_start(out=LF[:], in_=GT[tf]), writes=['LF'])
        ph.op('act', lambda e: e.activation(out=LF[:], in_=LF[:], func=AF.Exp, scale=-1.0), writes=['LF'])
        ph.op('act', lambda e: e.activation(out=LF[:], in_=LF[:], func=AF.Ln, bias=1.0, scale=1.0), writes=['LF'])
        ph.op('dve', lambda e: e.tensor_scalar(out=LF[:], in0=LF[:], scalar1=-1.0, scalar2=None, op0=ALU.mult),
              writes=['LF'])
        for (a0, a1) in segs:
            ph.op('dve', lambda e, a0=a0, a1=a1: e.tensor_tensor_scan(
                out=P[:, a0:a1], data0=bc(one[:, 0:1], [8, a1 - a0]), data1=LF[:, a0:a1], initial=0.0,
                op0=ALU.mult, op1=ALU.add), reads=['LF', 'one'], writes=['P'])
        ph.op('dve', lambda e: e.tensor_copy(out=tot[:, 0:1], in_=P[:, NCTX - 1:NCTX]), reads=['P'], writes=['tot'])
        ph.op('dve', lambda e: e.tensor_copy(out=tot[:, 1:2], in_=P[:, NS - 1:NS]), reads=['P'], writes=['tot'])
        ph.op('dve', lambda e: e.tensor_tensor(out=tot[:, 2:3], in0=tot[:, 0:1], in1=tot[:, 1:2], op=ALU.add),
              writes=['tot'])
        if d == 0:
            ph.op('dve', lambda e: e.tensor_scalar(out=P[:, NCTX:NS], in0=P[:, NCTX:NS], scalar1=tot[:, 0:1],
                                                   scalar2=None, op0=ALU.add), reads=['tot'], writes=['P'])
        else:
            ph.op('dve', lambda e: e.tensor_tensor(out=P[:], in0=LF[:], in1=P[:], op=ALU.subtract), reads=['LF'],
                  writes=['P'])
            ph.op('dve', lambda e: e.tensor_scalar(out=P[:, 0:NCTX], in0=P[:, 0:NCTX], scalar1=tot[:, 0:1],
                                                   scalar2=None, op0=ALU.add), reads=['tot'], writes=['P'])
            ph.op('dve', lambda e: e.tensor_scalar(out=P[:, NCTX:NS], in0=P[:, NCTX:NS], scalar1=tot[:, 2:3],
                                                   scalar2=None, op0=ALU.add), reads=['tot'], writes=['P'])
        ph.dma('sp', lambda e, ti=ti: e.dma_start(out=A[:], in_=GT[ti]), writes=['A'])
        ph.op('dve', lambda e: e.tensor_tensor(out=A[:], in0=A[:], in1=P[:], op=ALU.subtract), reads=['P'], writes=['A'])
        if d == 0:
            ph.op('dve', lambda e: e.tensor_tensor_scan(out=M[:], data0=bc(zer[:, 0:1], [8, NS]), data1=A[:], initial=0.0,
                                                        op0=ALU.add, op1=ALU.max), reads=['A', 'zer'], writes=['M'])
        else:
            ph.op('dve', lambda e: e.tensor_tensor_scan(
                out=M[:, 0:NCTX][:, ::-1], data0=bc(zer[:, 0:1], [8, NCTX]), data1=A[:, 0:NCTX][:, ::-1], initial=0.0,
                op0=ALU.add, op1=ALU.max), reads=['A', 'zer'], writes=['M'])
            ph.op('dve', lambda e: e.tensor_tensor_scan(
                out=M[:, NCTX:NS][:, ::-1], data0=bc(zer[:, 0:1], [8, NS - NCTX]), data1=A[:, NCTX:NS][:, ::-1],
                initial=M[:, 0:1], op0=ALU.add, op1=ALU.max), reads=['A', 'zer'], writes=['M'])
        ph.dma('sp', lambda e, d=d: e.dma_start(out=io['MF'][d], in_=M[:]), reads=['M'], writes=['MFout'])
        Mv = M[:].rearrange("h (c t) -> h c t", t=128)
        if d == 0:
            ph.op('dve', lambda e: e.tensor_copy(out=MN[:], in_=Mv[:, :, 127]), reads=['M'], writes=['MN'])
            ph.op('dve', lambda e: e.memset(MP[:, 0:1], 0.0), writes=['MP'])
            ph.op('dve', lambda e: e.tensor_copy(out=MP[:, 1:NT], in_=Mv[:, 0:NT - 1, 127]), reads=['M'], writes=['MP'])
        else:
            ph.op('dve', lambda e: e.tensor_copy(out=MN[:], in_=Mv[:, :, 0]), reads=['M'], writes=['MN'])
            ph.op('dve', lambda e: e.tensor_copy(out=MP[:, 0:NT - 1], in_=Mv[:, 1:NT, 0]), reads=['M'], writes=['MP'])
            ph.op('dve', lambda e: e.memset(MP[:, NCT - 1:NCT], 0.0), writes=['MP'])
            ph.op('dve', lambda e: e.tensor_copy(out=MP[:, NT - 1:NT], in_=M[:, 0:1]), reads=['M'], writes=['MP'])
        ph.op('dve', lambda e: e.tensor_tensor(out=dc[:], in0=MP[:], in1=MN[:], op=ALU.subtract), reads=['MP', 'MN'],
              writes=['dc'])
        ph.op('act', lambda e: e.activation(out=dc[:], in_=dc[:], func=AF.Exp), writes=['dc'])
        ph.op('dve', lambda e: e.tensor_tensor(out=xd[:], in0=bc(dc[:].unsqueeze(2), [8, NT, 8]),
                                               in1=bc(G['sel'][:, :, 0].unsqueeze(1), [8, NT, 8]), op=ALU.mult),
              reads=['dc'], writes=['xd'])
        xdf = xd[:].rearrange("j c h -> j (c h)")
        for c0 in range(0, NT * 8, 512):
            n = min(512, NT * 8 - c0)
            ph.op('pe', lambda e, c0=c0, n=n: e.matmul(dps[:, :n], lhsT=G['ones'][0:8, :], rhs=xdf[:, c0:c0 + n],
                                                        start=True, stop=True), reads=['xd'], writes=['dps'])
            ph.op('act', lambda e, c0=c0, n=n: e.copy(out=dcs[:, c0:c0 + n], in_=dps[:, :n]), writes=['dps', 'dcs'])
        ph.dma('sp', lambda e, d=d: e.dma_start(out=io['DEC'][d], in_=dcs[:]), reads=['dcs'], writes=['DECout'])
        MPb = bc(MP[:].unsqueeze(2), [8, NT, 128]); MNb = bc(MN[:].unsqueeze(2), [8, NT, 128])
        Ev = E[:].rearrange("h (c t) -> h c t", t=128)
        Av = A[:].rearrange("h (c t) -> h c t", t=128)
        for q in range(4):
            if q == 0:
                src = A; rk = ['A']
            elif q == 1:
                ph.op('dve', lambda e: e.tensor_tensor(out=Ev, in0=MPb, in1=Mv, op=ALU.subtract), reads=['M', 'MP'],
                      writes=['E'])
                ph.op('act', lambda e: e.activation(out=E[:], in_=E[:], func=AF.Exp), writes=['E'])
                src = E; rk = ['E']
            elif q == 2:
                ph.op('dve', lambda e: e.tensor_tensor(out=E[:], in0=P[:], in1=M[:], op=ALU.add), reads=['M', 'P'],
                      writes=['E'])
                ph.op('act', lambda e: e.activation(out=E[:], in_=E[:], func=AF.Exp, scale=-1.0), writes=['E'])
                src = E; rk = ['E']
            else:
                ph.op('dve', lambda e: e.tensor_tensor(out=Ev, in0=Av, in1=MNb, op=ALU.subtract), reads=['A', 'MN'],
                      writes=['E'])
                ph.op('act', lambda e: e.activation(out=E[:], in_=E[:], func=AF.Exp), writes=['E'])
                src = E; rk = ['E']
            for c0 in range(0, NT, 16):
                ncc = min(16, NT - c0)
                pb = (c0 // 16) % 2
                for c in range(c0, c0 + ncc):
                    ph.op('pe', lambda e, c=c, c0=c0, pb=pb, src=src: e.transpose(
                        out=tsp[pb][:, c - c0, :], in_=src[:, c * 128:(c + 1) * 128], identity=G['ident'][0:8, 0:8]),
                        reads=rk, writes=['tsp%d' % pb])
                ph.op('act', lambda e, c0=c0, ncc=ncc, pb=pb, q=q: e.copy(out=tss[:, c0:c0 + ncc, q, :],
                                                                          in_=tsp[pb][:, :ncc, :]),
                      writes=['tsp%d' % pb, 'tss'])
        ph.dma('sp', lambda e, d=d: e.dma_start(out=io['TS'][d], in_=tss[:].rearrange("p c q h -> p (c q h)")),
               reads=['tss'], writes=['TSout'])
    ph.emit()


def phase_scan(nc, G, io, cfg):
    ph = Phase(nc, 'H')
    NS, NCTX, NT = cfg['NS'], cfg['NCTX'], cfg['NS'] // 128
    NCT = NCTX // 128
    S = ph.sb('S', [128, 16, 129]); Sb = ph.sb('Sb', [128, 16, 129], BF16)
    ph.op('dve', lambda e: e.memset(S[:], 0.0), writes=['S%d' % i for i in range(16)])
    ph.op('pool', lambda e: e.memset(Sb[:], 0.0), writes=['Sb%d' % i for i in range(16)])
    TS = [ph.sb('TS%d' % d, [128, NT, 4, 8]) for d in range(2)]
    DEC = [ph.sb('DEC%d' % d, [128, NT, 8]) for d in range(2)]
    MF = [ph.sb('MF%d' % d, [8, NS]) for d in range(2)]
    for d in range(2):
        ph.dma('sp', lambda e, d=d: e.dma_start(out=TS[d][:].rearrange("p c q h -> p (c q h)"), in_=io['TS'][d]),
               writes=['TS'])
        ph.dma('sp', lambda e, d=d: e.dma_start(out=DEC[d][:].rearrange("p c h -> p (c h)"), in_=io['DEC'][d]),
               writes=['TS'])
        ph.dma('sp', lambda e, d=d: e.dma_start(out=MF[d][:], in_=io['MF'][d]), writes=['TS'])
    NB = 2
    QT = [[ph.sb('QT%d_%d' % (d, i), [128, 8, 128], BF16) for i in range(NB)] for d in range(2)]
    KT = [[ph.sb('KT%d_%d' % (d, i), [128, 8, 128], BF16) for i in range(NB)] for d in range(2)]
    Kk = [[ph.sb('Kk%d_%d' % (d, i), [128, 8, 128], BF16) for i in range(NB)] for d in range(2)]
    Va = [[ph.sb('Va%d_%d' % (d, i), [128, 8, 129], BF16) for i in range(NB)] for d in range(2)]
    for d in range(2):
        for i in range(NB):
            ph.op('pool', lambda e, d=d, i=i: e.memset(Va[d][i][:, :, 128:129], 1.0), writes=['Va%d_%d' % (d, i)])
    ho = [[ph.sb('ho%d_%d' % (d, i), [128, 8, 128]) for i in range(2)] for d in range(2)]
    psA = [ph.ps('psA%d' % i, [128, 128]) for i in range(2)]
    psB = [ph.ps('psB%d' % i, [128, 128]) for i in range(2)]
    psC = ph.ps('psC', [128, 129]); psD = ph.ps('psD', [128, 129])
    psE = [ph.ps('psE%d' % i, [128, 129]) for i in range(2)]
    Dm = [ph.sb('Dm%d' % i, [128, 128]) for i in range(2)]
    SD = [ph.sb('SD%d' % i, [128, 128], BF16) for i in range(2)]
    isb = [ph.sb('isb%d' % i, [128, 129]) for i in range(2)]
    tt = [ph.sb('tt%d' % i, [128, 129]) for i in range(2)]
    dn = [ph.sb('dn%d' % i, [128, 1]) for i in range(2)]
    VW = [ph.sb('VW%d' % i, [128, 129], BF16) for i in range(2)]
    order = [list(range(NT)), list(range(NCT - 1, -1, -1)) + list(range(NT - 1, NCT - 1, -1))]
    masks = [G['maskf'], G['maskb']]
    items = [(step, d, h) for step in range(NT) for d in range(2) for h in range(8)]
    loaded = set()

    def loads(step, d):
        if (step, d) in loaded:
            return
        loaded.add((step, d))
        c = order[d][step]; bi = step % NB; bk = '%d_%d' % (d, bi)
        sl = slice(c * 128, (c + 1) * 128)
        if c >= NCT:
            ph.dma('sp', lambda e: e.dma_start(out=QT[d][bi][:], in_=io['QT_m'][:, :, sl].rearrange("h p t -> p h t")),
                   writes=['QT' + bk])
            ph.dma('sp', lambda e: e.dma_start(out=KT[d][bi][:], in_=io['KT_m'][:, :, sl].rearrange("h p t -> p h t")),
                   writes=['KT' + bk])
        ph.dma('sp', lambda e: e.dma_start(out=Kk[d][bi][:], in_=io['Ktok'][c]), writes=['Kk' + bk])
        ph.dma('sp', lambda e: e.dma_start(out=Va[d][bi][:, :, 0:128], in_=io['Vtok'][c]), writes=['Va' + bk])

    def stage_a(i):
        step, d, h = items[i]
        loads(step, d)
        c = order[d][step]; bi = step % NB; bk = '%d_%d' % (d, bi); i2 = i % 2
        if c < NCT:
            return
        sl = slice(c * 128, (c + 1) * 128)
        ph.op('pe', lambda e: e.matmul(psA[i2][:], lhsT=KT[d][bi][:, h, :], rhs=QT[d][bi][:, h, :], start=True, stop=True),
              reads=['KT' + bk, 'QT' + bk], writes=['psA%d' % i2])
        ph.op('pe', lambda e: e.matmul(psB[i2][:], lhsT=G['sel'][:, h, :], rhs=MF[d][:, sl], start=True, stop=False),
              reads=['TS'], writes=['psB%d' % i2])
        ph.op('pe', lambda e: e.matmul(psB[i2][:], lhsT=G['ident'][:], rhs=masks[d][:], start=False, stop=True),
              reads=[], writes=['psB%d' % i2])
        ph.op('act', lambda e: e.activation(out=Dm[i2][:], in_=psB[i2][:], func=AF.Exp, scale=-1.0,
                                            bias=TS[d][:, c, 0, h:h + 1]), reads=['TS'], writes=['psB%d' % i2, 'Dm%d' % i2])
        ph.op('dve', lambda e: e.tensor_tensor(out=SD[i2][:], in0=psA[i2][:], in1=Dm[i2][:], op=ALU.mult),
              reads=['Dm%d' % i2], writes=['psA%d' % i2, 'SD%d' % i2])

    def stage_b(i):
        step, d, h = items[i]
        c = order[d][step]; bi = step % NB; bk = '%d_%d' % (d, bi); i2 = i % 2
        sk = d * 8 + h
        hob = ho[d][step % 2]; hok = 'ho%d_%d' % (d, step % 2)
        if c >= NCT:
            ph.op('pe', lambda e: e.matmul(psC[:], lhsT=SD[i2][:], rhs=Va[d][bi][:, h, :], start=True, stop=True),
                  reads=['SD%d' % i2, 'Va' + bk], writes=['psC'])
            ph.op('pe', lambda e: e.matmul(psD[:], lhsT=QT[d][bi][:, h, :], rhs=Sb[:, sk, :], start=True, stop=True),
                  reads=['QT' + bk, 'Sb%d' % sk], writes=['psD'])
            ph.op('act', lambda e: e.activation(out=isb[i2][:], in_=psD[:], func=AF.Identity, scale=TS[d][:, c, 1, h:h + 1]),
                  reads=['TS'], writes=['psD', 'isb%d' % i2])
            ph.op('dve', lambda e: e.tensor_tensor(out=tt[i2][:], in0=psC[:], in1=isb[i2][:], op=ALU.add),
                  reads=['isb%d' % i2], writes=['psC', 'tt%d' % i2])
            ph.op('dve', lambda e: e.tensor_scalar(out=dn[i2][:], in0=tt[i2][:, 128:129], scalar1=-1.0,
                                                   scalar2=tt[i2][:, 128:129], op0=ALU.mult, op1=ALU.max),
                  reads=['tt%d' % i2], writes=['dn%d' % i2])
            ph.op('dve', lambda e: e.tensor_scalar(out=dn[i2][:], in0=dn[i2][:], scalar1=TS[d][:, c, 2, h:h + 1],
                                                   scalar2=None, op0=ALU.max), reads=['TS'], writes=['dn%d' % i2])
            ph.op('dve', lambda e: e.reciprocal(out=dn[i2][:], in_=dn[i2][:]), writes=['dn%d' % i2])
            ph.op('dve', lambda e: e.tensor_scalar(out=hob[:, h, :], in0=tt[i2][:, 0:128], scalar1=dn[i2][:, 0:1],
                                                   scalar2=None, op0=ALU.mult),
                  reads=['tt%d' % i2, 'dn%d' % i2], writes=[hok + '_%d' % h])
        ph.op('pool', lambda e: e.tensor_scalar(out=VW[i2][:], in0=Va[d][bi][:, h, :], scalar1=TS[d][:, c, 3, h:h + 1],
                                                scalar2=1.0, op0=ALU.mult, op1=ALU.mult), reads=['Va' + bk, 'TS'], writes=['VW%d' % i2])
        ph.op('pe', lambda e: e.matmul(psE[i2][:], lhsT=Kk[d][bi][:, h, :], rhs=VW[i2][:], start=True, stop=True),
              reads=['Kk' + bk, 'VW%d' % i2], writes=['psE%d' % i2])
        ph.op('dve', lambda e: e.scalar_tensor_tensor(out=S[:, sk, :], in0=S[:, sk, :], scalar=DEC[d][:, c, h:h + 1],
                                                      in1=psE[i2][:], op0=ALU.mult, op1=ALU.add),
              reads=['TS'], writes=['psE%d' % i2, 'S%d' % sk])
        ph.op('act', lambda e: e.copy(out=Sb[:, sk, :], in_=S[:, sk, :]), reads=['S%d' % sk], writes=['Sb%d' % sk])
        if h == 7 and c >= NCT:
            dst = io['Hf'] if d == 0 else io['Hb']
            r0 = (c - NCT) * 128
            ph.dma('sp', lambda e: e.dma_start(out=dst[r0:r0 + 128, :], in_=hob[:].rearrange("p h v -> p (h v)")),
                   reads=[hok + '_%d' % hh for hh in range(8)], writes=['out'])

    stage_a(0)
    for i in range(len(items)):
        if i + 1 < len(items):
            stage_a(i + 1)
        stage_b(i)
    ph.emit()


def phase_attn(nc, G, io, cfg):
    ph = Phase(nc, 'T')
    NS, NO, NT = cfg['NS'], cfg['NO'], cfg['NS'] // 128
    scale = 64 ** -0.5
    KTt = [ph.sb('KT%d' % i, [128, NS], BF16) for i in range(2)]
    Vt = [ph.sb('V%d' % i, [128, NT, 129], BF16) for i in range(2)]
    for i in range(2):
        ph.op('pool', lambda e, i=i: e.memset(Vt[i][:, :, 128:129], 1.0), writes=['V%d' % i])
    QTt = [ph.sb('Q%d' % i, [128, 256], BF16) for i in range(2)]
    ps1 = [ph.ps('ps1_%d' % i, [128, 256]) for i in range(2)]
    ps2 = [ph.ps('ps2_%d' % i, [128, 256]) for i in range(2)]
    psO = [ph.ps('psO%d' % i, [128, 129]) for i in range(4)]
    Pt = [ph.sb('Pt%d' % i, [128, 2, 256], BF16) for i in range(2)]
    anw = ph.sb('anw', [128, 128])
    ph.dma('sp', lambda e: e.dma_start(out=anw[:], in_=bc(io['a_norm_w'], [128, 128])), writes=['anw'])
    ph.op('dve', lambda e: e.tensor_scalar(out=anw[:], in0=anw[:], scalar1=1.0 - LAM_INIT, scalar2=None, op0=ALU.mult),
          writes=['anw'])
    r = ph.sb('r', [128, 4]); o = [ph.sb('o%d' % i, [128, 128]) for i in range(2)]
    junk = ph.sb('junk', [128, 128]); ss = ph.sb('ss', [128, 2])
    yb = [ph.sb('yb%d' % i, [128, 128]) for i in range(2)]
    yst = [ph.sb('yst%d' % i, [128, 256], BF16) for i in range(2)]
    it = 0
    for h in range(8):
        hb = h % 2
        ph.dma('sp', lambda e, h=h, hb=hb: e.dma_start(out=KTt[hb][:], in_=io['KaT'][h]), writes=['KT%d' % hb])
        for t0 in range(0, NT, 16):
            t1 = min(NT, t0 + 16)
            ph.dma('sp', lambda e, h=h, hb=hb, t0=t0, t1=t1: e.dma_start(
                out=Vt[hb][:, t0:t1, 0:128], in_=io['Va'][t0:t1, :, h, :].rearrange("t p v -> p t v")), writes=['V%d' % hb])
        for qb in range(NO // 256):
            qi = it % 2; it += 1
            ph.dma('sp', lambda e, h=h, qb=qb, qi=qi: e.dma_start(out=QTt[qi][:], in_=io['QaT'][h, :, qb * 256:(qb + 1) * 256]),
                   writes=['Q%d' % qi])
            def st_(kt):
                b = kt % 2
                ks = slice(kt * 128, (kt + 1) * 128)
                ph.op('pe', lambda e, ks=ks, b=b, hb=hb, qi=qi: e.matmul(
                    ps1[b][:], lhsT=KTt[hb][0:64, ks], rhs=QTt[qi][0:64, :], start=True, stop=True),
                    reads=['KT%d' % hb, 'Q%d' % qi], writes=['ps1_%d' % b])
                ph.op('pe', lambda e, ks=ks, b=b, hb=hb, qi=qi: e.matmul(
                    ps2[b][:], lhsT=KTt[hb][64:128, ks], rhs=QTt[qi][64:128, :], start=True, stop=True),
                    reads=['KT%d' % hb, 'Q%d' % qi], writes=['ps2_%d' % b])

            st_(0)
            for kt in range(NT):
                b = kt % 2
                if kt + 1 < NT:
                    st_(kt + 1)
                ph.op('act', lambda e, b=b: e.activation(out=Pt[b][:, 0, :], in_=ps1[b][:], func=AF.Exp, scale=scale),
                      writes=['ps1_%d' % b, 'Pt%d' % b])
                ph.op('act', lambda e, b=b: e.activation(out=Pt[b][:, 1, :], in_=ps2[b][:], func=AF.Exp, scale=scale),
                      writes=['ps2_%d' % b, 'Pt%d' % b])
                for mp in range(2):
                    for qs in range(2):
                        oi = mp * 2 + qs
                        ph.op('pe', lambda e, b=b, qs=qs, oi=oi, kt=kt, mp=mp, hb=hb: e.matmul(
                            psO[oi][:], lhsT=Pt[b][:, mp, qs * 128:(qs + 1) * 128], rhs=Vt[hb][:, kt, :],
                            start=(kt == 0), stop=(kt == NT - 1)),
                            reads=['Pt%d' % b, 'V%d' % hb], writes=['psO%d' % oi])
            ysb = yst[qi]; ysk = 'yst%d' % qi
            for qs in range(2):
                o_ = o[qs]; ok = 'o%d' % qs
                ph.op('dve', lambda e, qs=qs: e.reciprocal(out=r[:, qs:qs + 1], in_=psO[qs][:, 128:129]),
                      writes=['psO%d' % qs, 'r%d' % qs])
                ph.op('dve', lambda e, qs=qs: e.reciprocal(out=r[:, 2 + qs:3 + qs], in_=psO[2 + qs][:, 128:129]),
                      writes=['psO%d' % (2 + qs), 'r%d' % (2 + qs)])
                ph.op('dve', lambda e, qs=qs: e.tensor_tensor(out=r[:, 2 + qs:3 + qs], in0=r[:, 2 + qs:3 + qs],
                                                              in1=G['nlam'][:], op=ALU.mult), writes=['r%d' % (2 + qs)])
                ph.op('dve', lambda e, qs=qs, o_=o_: e.tensor_scalar(out=o_[:], in0=psO[qs][:, 0:128], scalar1=r[:, qs:qs + 1],
                                                                     scalar2=None, op0=ALU.mult),
                      reads=['r%d' % qs], writes=['psO%d' % qs, ok])
                ph.op('dve', lambda e, qs=qs, o_=o_: e.scalar_tensor_tensor(
                    out=o_[:], in0=psO[2 + qs][:, 0:128], scalar=r[:, 2 + qs:3 + qs], in1=o_[:], op0=ALU.mult, op1=ALU.add),
                    reads=['r%d' % (2 + qs)], writes=['psO%d' % (2 + qs), ok])
                ph.op('act', lambda e, qs=qs, o_=o_: e.activation(out=junk[:], in_=o_[:], func=AF.Square,
                                                                  accum_out=ss[:, qs:qs + 1]),
                      reads=[ok], writes=['junk', 'ss%d' % qs])
                ph.op('dve', lambda e, qs=qs: e.tensor_scalar(out=ss[:, qs:qs + 1], in0=ss[:, qs:qs + 1], scalar1=1.0 / 128,
                                                              scalar2=EPS, op0=ALU.mult, op1=ALU.add), writes=['ss%d' % qs])
                ph.op('act', lambda e, qs=qs: e.activation(out=ss[:, qs:qs + 1], in_=ss[:, qs:qs + 1], func=AF.Sqrt),
                      writes=['ss%d' % qs])
                ph.op('dve', lambda e, qs=qs: e.reciprocal(out=ss[:, qs:qs + 1], in_=ss[:, qs:qs + 1]), writes=['ss%d' % qs])
                ph.op('dve', lambda e, qs=qs, o_=o_: e.scalar_tensor_tensor(
                    out=yb[qs][:], in0=o_[:], scalar=ss[:, qs:qs + 1], in1=anw[:], op0=ALU.mult, op1=ALU.mult),
                    reads=[ok, 'ss%d' % qs, 'anw'], writes=['yb%d' % qs])
            for qs in range(2):
                ph.op('pe', lambda e, qs=qs: e.transpose(out=psO[qs][:, 0:128], in_=yb[qs][:], identity=G['ident'][:]),
                      reads=['yb%d' % qs], writes=['psO%d' % qs])
                ph.op('act', lambda e, ysb=ysb, qs=qs: e.copy(out=ysb[:, qs * 128:(qs + 1) * 128], in_=psO[qs][:, 0:128]),
                      writes=['psO%d' % qs, ysk])
            ph.dma('sp', lambda e, ysb=ysb, h=h, qb=qb: e.dma_start(out=io['ydT'][h, :, qb * 256:(qb + 1) * 256], in_=ysb[:]),
                   reads=[ysk], writes=['out'])
    ph.emit()


def phase_ym(nc, G, io, cfg):
    ph = Phase(nc, 'Y')
    NO = cfg['NO']
    idx = ph.sb('idx', [128, NO // 128], I32)
    ph.dma('sp', lambda e: e.dma_start(out=idx[:], in_=io['own_idx']), writes=['idx'])
    mw = ph.sb('mw', [128, 1024])
    ph.dma('sp', lambda e: e.dma_start(out=mw[:], in_=bc(io['m_norm_w'], [128, 1024])), writes=['mw'])
    hf = [ph.sb('hf%d' % i, [128, 1024]) for i in range(2)]
    hbt = [ph.sb('hb%d' % i, [128, 1024]) for i in range(2)]
    zt = [ph.sb('z%d' % i, [128, 1024]) for i in range(2)]
    sq = ph.sb('sq', [128, 1024]); ssh = ph.sb('ssh', [128, 8])
    yb = [ph.sb('yb%d' % i, [128, 1024], BF16) for i in range(2)]
    ptr = ph.ps('ptr', [128, 8, 128], BF16)
    yst = [ph.sb('yst%d' % i, [128, 8, 128], BF16) for i in range(2)]
    for t in range(NO // 128):
        b = t % 2
        ph.dma('pool', lambda e, b=b, t=t: e.indirect_dma_start(
            out=hf[b][:], out_offset=None, in_=io['Hf'][:, :],
            in_offset=bass.IndirectOffsetOnAxis(ap=idx[:, t:t + 1], axis=0)), reads=['idx'], writes=['hf%d' % b])
        ph.dma('pool', lambda e, b=b, t=t: e.indirect_dma_start(
            out=hbt[b][:], out_offset=None, in_=io['Hb'][:, :],
            in_offset=bass.IndirectOffsetOnAxis(ap=idx[:, t:t + 1], axis=0)), reads=['idx'], writes=['hb%d' % b])
        ph.dma('sp', lambda e, b=b, t=t: e.dma_start(out=zt[b][:], in_=io['Zo'][t * 128:(t + 1) * 128, :]), writes=['z%d' % b])
        ph.op('dve', lambda e, b=b: e.tensor_tensor(out=hf[b][:], in0=hf[b][:], in1=hbt[b][:], op=ALU.add),
              reads=['hb%d' % b], writes=['hf%d' % b])
        ph.op('pool', lambda e, b=b: e.tensor_tensor(out=sq[:], in0=hf[b][:], in1=hf[b][:], op=ALU.mult),
              reads=['hf%d' % b], writes=['sq'])
        ph.op('dve', lambda e: e.tensor_reduce(out=ssh[:], in_=sq[:].rearrange("p (h v) -> p h v", v=128), axis=AX.X,
                                               op=ALU.add), reads=['sq'], writes=['ssh'])
        ph.op('dve', lambda e: e.tensor_scalar(out=ssh[:], in0=ssh[:], scalar1=1.0 / 128, scalar2=EPS, op0=ALU.mult,
                                               op1=ALU.add), writes=['ssh'])
        ph.op('act', lambda e: e.activation(out=ssh[:], in_=ssh[:], func=AF.Sqrt), writes=['ssh'])
        ph.op('dve', lambda e: e.reciprocal(out=ssh[:], in_=ssh[:]), writes=['ssh'])
        ph.op('dve', lambda e, b=b: e.tensor_tensor(
            out=hf[b][:].rearrange("p (h v) -> p h v", v=128), in0=hf[b][:].rearrange("p (h v) -> p h v", v=128),
            in1=bc(ssh[:].unsqueeze(2), [128, 8, 128]), op=ALU.mult), reads=['ssh'], writes=['hf%d' % b])
        ph.op('pool', lambda e, b=b: e.tensor_tensor(out=zt[b][:], in0=zt[b][:], in1=mw[:], op=ALU.mult), reads=['mw'],
              writes=['z%d' % b])
        ph.op('dve', lambda e, b=b: e.tensor_tensor(out=yb[b][:], in0=hf[b][:], in1=zt[b][:], op=ALU.mult),
              reads=['hf%d' % b, 'z%d' % b], writes=['yb%d' % b])
        for hh in range(8):
            ph.op('pe', lambda e, b=b, hh=hh: e.transpose(out=ptr[:, hh, :], in_=yb[b][:, hh * 128:(hh + 1) * 128],
                                                          identity=G['identb'][:]), reads=['yb%d' % b], writes=['ptr'])
        ph.op('act', lambda e, b=b: e.copy(out=yst[b][:], in_=ptr[:]), writes=['ptr', 'yst%d' % b])
        ph.dma('sp', lambda e, b=b, t=t: e.dma_start(
            out=io['ymT'][:, :, t * 128:(t + 1) * 128].rearrange("h p t -> p h t"), in_=yst[b][:]),
            reads=['yst%d' % b], writes=['out'])
    ph.emit()


def phase_merge(nc, G, io, cfg):
    ph = Phase(nc, 'J')
    NO = cfg['NO']
    wa = ph.sb('wa', [128, 8, 2048], BF16); wb = ph.sb('wb', [128, 8, 2048], BF16)
    load_w(ph, wa, 'wa', io['w_pa'], 0, 2048)
    load_w(ph, wb, 'wb', io['w_pb'], 0, 2048)
    ym = [ph.sb('ym%d' % i, [128, 8, 512], BF16) for i in range(2)]
    yd = [ph.sb('yd%d' % i, [128, 8, 512], BF16) for i in range(2)]
    ga = [ph.sb('ga%d' % i, [128, 512], BF16) for i in range(2)]
    gb = [ph.sb('gb%d' % i, [128, 512], BF16) for i in range(2)]
    pa = [ph.ps('pa%d' % i, [128, 512]) for i in range(2)]
    pb = [ph.ps('pb%d' % i, [128, 512]) for i in range(2)]
    t1 = [ph.sb('t1_%d' % i, [128, 512]) for i in range(2)]
    t2 = [ph.sb('t2_%d' % i, [128, 512]) for i in range(2)]
    mo = [ph.sb('mo%d' % i, [128, 512], BF16) for i in range(2)]
    it = 0
    for bi in range(NO // 512):
        bb = bi % 2
        ts_ = slice(bi * 512, (bi + 1) * 512)
        ph.dma('sp', lambda e, bb=bb, ts_=ts_: e.dma_start(out=ym[bb][:], in_=io['ymT'][:, :, ts_].rearrange("k p t -> p k t")),
               writes=['ym%d' % bb])
        ph.dma('sp', lambda e, bb=bb, ts_=ts_: e.dma_start(out=yd[bb][:], in_=io['ydT'][:, :, ts_].rearrange("k p t -> p k t")),
               writes=['yd%d' % bb])
        for cb in range(16):
            i = it % 2; it += 1
            cs_ = slice(cb * 128, (cb + 1) * 128)
            ph.dma('sp', lambda e, i=i, cb=cb, ts_=ts_: e.dma_start(out=ga[i][:], in_=io['GaT'][cb, :, ts_]), writes=['ga%d' % i])
            ph.dma('sp', lambda e, i=i, cb=cb, ts_=ts_: e.dma_start(out=gb[i][:], in_=io['GbT'][cb, :, ts_]), writes=['gb%d' % i])
            for k in range(8):
                ph.op('pe', lambda e, i=i, k=k, cs_=cs_, bb=bb: e.matmul(pa[i][:], lhsT=wa[:, k, cs_], rhs=ym[bb][:, k, :],
                                                                       start=(k == 0), stop=(k == 7)),
                      reads=['wa', 'ym%d' % bb], writes=['pa%d' % i])
            for k in range(8):
                ph.op('pe', lambda e, i=i, k=k, cs_=cs_, bb=bb: e.matmul(pb[i][:], lhsT=wb[:, k, cs_], rhs=yd[bb][:, k, :],
                                                                       start=(k == 0), stop=(k == 7)),
                      reads=['wb', 'yd%d' % bb], writes=['pb%d' % i])
            ph.op('dve', lambda e, i=i: e.tensor_tensor(out=t1[i][:], in0=pa[i][:], in1=ga[i][:], op=ALU.mult),
                  reads=['ga%d' % i], writes=['pa%d' % i, 't1_%d' % i])
            ph.op('dve', lambda e, i=i: e.tensor_tensor(out=t2[i][:], in0=pb[i][:], in1=gb[i][:], op=ALU.mult),
                  reads=['gb%d' % i], writes=['pb%d' % i, 't2_%d' % i])
            ph.op('pool', lambda e, i=i: e.tensor_tensor(out=mo[i][:], in0=t1[i][:], in1=t2[i][:], op=ALU.add),
                  reads=['t1_%d' % i, 't2_%d' % i], writes=['mo%d' % i])
            ph.dma('sp', lambda e, i=i, cb=cb, ts_=ts_: e.dma_start(out=io['mT'][cb, :, ts_], in_=mo[i][:]),
                   reads=['mo%d' % i], writes=['out'])
    ph.emit()


def phase_wout(nc, G, io, cfg):
    ph = Phase(nc, 'W')
    NO = cfg['NO']
    wo = ph.sb('wo', [128, 16, 2048], BF16)
    load_w(ph, wo, 'wo', io['w_out'], 0, 2048)
    mt = [ph.sb('mt%d' % i, [128, 16, 128], BF16) for i in range(2)]
    xt = [ph.sb('xt%d' % i, [128, 2048]) for i in range(2)]
    pp = [ph.ps('pp%d' % i, [128, 512]) for i in range(4)]
    tm = [ph.sb('tm%d' % i, [128, 512]) for i in range(2)]
    it = 0
    for t in range(NO // 128):
        b = t % 2
        ts_ = slice(t * 128, (t + 1) * 128)
        ph.dma('sp', lambda e, b=b, ts_=ts_: e.dma_start(out=mt[b][:], in_=io['mT'][:, :, ts_].rearrange("k p t -> p k t")),
               writes=['mt%d' % b])
        ph.dma('sp', lambda e, b=b, ts_=ts_: e.dma_start(out=xt[b][:], in_=io['xo'][ts_, :]), reads=[], writes=['xt%d' % b])
        for ch in range(4):
            i = it % 4; j = it % 2; it += 1
            cs_ = slice(ch * 512, (ch + 1) * 512)
            for k in range(16):
                ph.op('pe', lambda e, i=i, k=k, cs_=cs_, b=b: e.matmul(pp[i][:], lhsT=mt[b][:, k, :], rhs=wo[:, k, cs_],
                                                                      start=(k == 0), stop=(k == 15)),
                      reads=['wo', 'mt%d' % b], writes=['pp%d' % i])
            ph.op('dve', lambda e, i=i, j=j, cs_=cs_: e.tensor_tensor(out=tm[j][:], in0=pp[i][:], in1=G['g1_bc'][:, cs_], op=ALU.mult),
                  writes=['pp%d' % i, 'tm%d' % j])
            ph.op('pool', lambda e, j=j, b=b, cs_=cs_: e.tensor_tensor(out=xt[b][:, cs_], in0=xt[b][:, cs_], in1=tm[j][:], op=ALU.add),
                  reads=['tm%d' % j], writes=['xt%d' % b])
        ph.dma('sp', lambda e, b=b, ts_=ts_: e.dma_start(out=io['x1'][ts_, :], in_=xt[b][:]), reads=['xt%d' % b], writes=['out'])
    ph.emit()


def phase_peer_sel(nc, G, io, cfg):
    ph = Phase(nc, 'S')
    NO = cfg['NO']
    skf = ph.sb('skf', [128, 16, 128]); skb = ph.sb('skb', [128, 16, 128], BF16)
    ph.dma('sp', lambda e: e.dma_start(out=skf[:], in_=io['skT']), writes=['skf'])
    ph.op('dve', lambda e: e.tensor_copy(out=skb[:], in_=skf[:]), reads=['skf'], writes=['skb'])
    qt = [ph.sb('qt%d' % i, [128, 16, 128], BF16) for i in range(2)]
    pS = [ph.ps('pS%d' % i, [128, 4, 128]) for i in range(4)]
    Ssb = ph.sb('Ssb', [128, 16, 128]); wk = ph.sb('wk', [128, 128]); v = ph.sb('v', [128, 16, 16])
    cand = ph.sb('cand', [128, 8, 256]); wk2 = ph.sb('wk2', [128, 256]); ts = ph.sb('ts', [128, 8, 16])
    ex = ph.sb('ex', [128, 8, 16]); Z = ph.sb('Z', [128, 8]); th = ph.sb('th', [128, 8])
    E1 = ph.sb('E1', [128, 8, 128]); E2 = ph.sb('E2', [128, 8, 128]); E1p = ph.sb('E1p', [128, 32, 8, 4])
    v4 = v[:].rearrange("p (h two) k -> p h two k", two=2)
    S4 = Ssb[:].rearrange("p (h two) n -> p h two n", two=2)
    for t in range(NO // 128):
        b = t % 2
        ts_ = slice(t * 128, (t + 1) * 128)
        ph.dma('sp', lambda e, b=b, ts_=ts_: e.dma_start(out=qt[b][:], in_=io['qT'][:, :, ts_].rearrange("k p t -> p k t")),
               writes=['qt%d' % b])
        for g in range(4):
            for j in range(4):
                hp = g * 4 + j
                ph.op('pe', lambda e, b=b, g=g, j=j, hp=hp: e.matmul(pS[g][:, j, :], lhsT=qt[b][:, hp, :], rhs=skb[:, hp, :],
                                                                   start=True, stop=True),
                      reads=['qt%d' % b, 'skb'], writes=['pS%d' % g])
            ph.op('act', lambda e, g=g: e.copy(out=Ssb[:, g * 4:(g + 1) * 4, :], in_=pS[g][:]), writes=['pS%d' % g, 'Ssb%d' % g])
        for hp in range(16):
            g = hp // 4
            ph.op('dve', lambda e, hp=hp: e.max(out=v[:, hp, 0:8], in_=Ssb[:, hp, :]), reads=['Ssb%d' % g], writes=['v'])
            ph.op('dve', lambda e, hp=hp: e.match_replace(out=wk[:], in_to_replace=v[:, hp, 0:8], in_values=Ssb[:, hp, :],
                                                          imm_value=-1e30), reads=['Ssb%d' % g], writes=['v', 'wk'])
            ph.op('dve', lambda e, hp=hp: e.max(out=v[:, hp, 8:16], in_=wk[:]), writes=['v', 'wk'])
        ph.op('dve', lambda e: e.tensor_tensor(
            out=cand[:].rearrange("p h (i j) -> p h i j", j=16), in0=bc(v4[:, :, 0, :].unsqueeze(3), [128, 8, 16, 16]),
            in1=bc(v4[:, :, 1, :].unsqueeze(2), [128, 8, 16, 16]), op=ALU.add), reads=['v'], writes=['cand'])
        for h in range(8):
            ph.op('dve', lambda e, h=h: e.max(out=ts[:, h, 0:8], in_=cand[:, h, :]), reads=['cand'], writes=['ts'])
            ph.op('dve', lambda e, h=h: e.match_replace(out=wk2[:], in_to_replace=ts[:, h, 0:8], in_values=cand[:, h, :],
                                                        imm_value=-1e30), reads=['cand'], writes=['ts', 'wk2'])
            ph.op('dve', lambda e, h=h: e.max(out=ts[:, h, 8:16], in_=wk2[:]), writes=['ts', 'wk2'])
        ph.op('dve', lambda e: e.tensor_tensor(out=ex[:], in0=ts[:], in1=bc(ts[:, :, 0:1], [128, 8, 16]), op=ALU.subtract),
              reads=['ts'], writes=['ex'])
        ph.op('act', lambda e: e.activation(out=ex[:], in_=ex[:], func=AF.Exp), writes=['ex'])
        ph.op('dve', lambda e: e.tensor_reduce(out=Z[:], in_=ex[:], axis=AX.X, op=ALU.add), reads=['ex'], writes=['Z'])
        ph.op('dve', lambda e: e.reciprocal(out=Z[:], in_=Z[:]), writes=['Z'])
        ph.op('dve', lambda e: e.tensor_tensor(out=th[:], in0=ex[:, :, 15], in1=Z[:], op=ALU.mult), reads=['ex', 'Z'],
              writes=['th'])
        ph.op('dve', lambda e: e.tensor_tensor(out=E1[:], in0=S4[:, :, 0, :], in1=bc(v4[:, :, 0, 0:1], [128, 8, 128]),
                                               op=ALU.subtract), reads=['Ssb%d' % g for g in range(4)] + ['v'], writes=['E1'])
        ph.op('act', lambda e: e.activation(out=E1[:], in_=E1[:], func=AF.Exp), writes=['E1'])
        ph.op('dve', lambda e: e.tensor_tensor(out=E1[:], in0=E1[:], in1=bc(Z[:].unsqueeze(2), [128, 8, 128]), op=ALU.mult),
              reads=['Z'], writes=['E1'])
        ph.op('pool', lambda e: e.tensor_copy(out=E1p[:], in_=E1[:].rearrange("p h (g j) -> p g h j", j=4)), reads=['E1'],
              writes=['E1p'])
        ph.op('dve', lambda e: e.tensor_tensor(out=E2[:], in0=S4[:, :, 1, :], in1=bc(v4[:, :, 1, 0:1], [128, 8, 128]),
                                               op=ALU.subtract), reads=['Ssb%d' % g for g in range(4)] + ['v'], writes=['E2'])
        ph.op('act', lambda e: e.activation(out=E2[:], in_=E2[:], func=AF.Exp), writes=['E2'])
        ph.dma('sp', lambda e, ts_=ts_: e.dma_start(out=io['E1g'][ts_, :], in_=E1p[:].rearrange("p g h j -> p (g h j)")),
               reads=['E1p'], writes=['out'])
        ph.dma('sp', lambda e, ts_=ts_: e.dma_start(out=io['E2'][ts_, :], in_=E2[:].rearrange("p h n -> p (h n)")),
               reads=['E2'], writes=['out'])
        ph.dma('sp', lambda e, ts_=ts_: e.dma_start(out=io['TH'][ts_, :], in_=th[:]), reads=['th'], writes=['out'])
    ph.emit()


def phase_peer_prep(nc, G, io, cfg):
    ph = Phase(nc, 'P')
    NE = cfg.get('NE', 128)
    Ub = [ph.sb('Ub%d' % i, [128, 2048], BF16) for i in range(3)]
    Vs = [ph.sb('Vs%d' % i, [128, 2048], BF16) for i in range(3)]
    UT = [ph.sb('UT%d' % i, [128, 16, 128], BF16) for i in range(3)]
    pUT = [ph.ps('pUT%d' % i, [128, 8, 128], BF16) for i in range(4)]
    for ei in range(NE):
        u = ei % 3
        ph.dma('pool', lambda e, u=u, ei=ei: e.dma_start(out=Ub[u][:], in_=io['expert_u'][ei * 128:(ei + 1) * 128, :]),
               writes=['Ub%d' % u])
        ph.dma('pool', lambda e, u=u, ei=ei: e.dma_start(out=Vs[u][:], in_=io['expert_v'][ei * 128:(ei + 1) * 128, :]),
               writes=['Vs%d' % u])
        ph.dma('sp', lambda e, u=u, ei=ei: e.dma_start(out=io['Vd'][ei], in_=Vs[u][:]), reads=['Vs%d' % u], writes=['out'])
        for half in range(2):
            pi = (ei * 2 + half) % 4
            for kk in range(8):
                k = half * 8 + kk
                ph.op('pe', lambda e, u=u, k=k, kk=kk, pi=pi: e.transpose(
                    out=pUT[pi][:, kk, :], in_=Ub[u][:, k * 128:(k + 1) * 128], identity=G['identb'][:]),
                    reads=['Ub%d' % u], writes=['pUT%d' % pi])
            eng = 'act' if half == 0 else 'dve'
            if eng == 'act':
                ph.op('act', lambda e, u=u, half=half, pi=pi: e.copy(out=UT[u][:, half * 8:(half + 1) * 8, :], in_=pUT[pi][:]),
                      writes=['pUT%d' % pi, 'UT%d_%d' % (u, half)])
            else:
                ph.op('dve', lambda e, u=u, half=half, pi=pi: e.tensor_copy(out=UT[u][:, half * 8:(half + 1) * 8, :], in_=pUT[pi][:]),
                      writes=['pUT%d' % pi, 'UT%d_%d' % (u, half)])
        ph.dma('sp', lambda e, u=u, ei=ei: e.dma_start(out=io['UTd'][ei], in_=UT[u][:].rearrange("p k e -> p (k e)")),
               reads=['UT%d_0' % u, 'UT%d_1' % u], writes=['out'])
    ph.emit()


def phase_peer_dense(nc, G, io, cfg):
    ph = Phase(nc, 'K')
    NO = cfg['NO']
    NE = cfg.get('NE', 128)
    NGR = NE // 4
    hpt = ph.sb('hpt', [128, 16, 512], BF16)
    E2t = ph.sb('E2t', [128, 4, 8, 128]); THt = ph.sb('THt', [128, 4, 8])
    E2flat = E2t[:].rearrange("p j h n -> p (j h n)")
    x1t = E2flat[:, 0:2048]; fw = E2flat[:, 2048:4096]
    E1g = [ph.sb('E1g%d' % i, [128, 4, 8, 4]) for i in range(2)]
    Gm = [ph.sb('Gm%d' % i, [128, 4, 8, 4, 128], BF16) for i in range(2)]
    Pt = [ph.sb('Pt%d' % i, [128, 4, 128]) for i in range(2)]
    Ub = [ph.sb('Ub%d' % i, [128, 2048], BF16) for i in range(2)]
    UT = [ph.sb('UT%d' % i, [128, 16, 128], BF16) for i in range(2)]
    Vb = [ph.sb('Vb%d' % i, [128, 2048], BF16) for i in range(4)]
    gel = [ph.sb('gel%d' % i, [128, 512], BF16) for i in range(2)]
    CT = [ph.sb('CT%d' % i, [128, 512], BF16) for i in range(4)]
    acc = ph.sb('acc', [128, 4, 2048]); ss = ph.sb('ss', [128, 1])
    pAT = ph.ps('pAT', [128, 512]); pWT = ph.ps('pWT', [128, 512])
    pO = [ph.ps('pO%d' % i, [128, 512]) for i in range(4)]
    for tg in range(NO // 512):
        tsl = slice(tg * 512, (tg + 1) * 512)
        ph.dma('sp', lambda e, tsl=tsl: e.dma_start(out=hpt[:], in_=io['hpT'][:, :, tsl].rearrange("k p t -> p k t")),
               writes=['hpt'])
        ph.dma('sp', lambda e, tsl=tsl: e.dma_start(out=E2t[:].rearrange("p j h n -> p j (h n)"),
                                                    in_=io['E2'][tsl, :].rearrange("(j p) c -> p j c", p=128)), writes=['E2t'])
        ph.dma('sp', lambda e, tsl=tsl: e.dma_start(out=THt[:], in_=io['TH'][tsl, :].rearrange("(j p) c -> p j c", p=128)),
               writes=['E2t'])

        def mask(g, tsl=tsl):
            eg = E1g[g % 2]; egk = 'E1g%d' % (g % 2); gm = Gm[g % 2]; gk = 'Gm%d' % (g % 2)
            ph.dma('sp', lambda e: e.dma_start(
                out=eg[:].rearrange("p j h q -> p j (h q)"),
                in_=io['E1g'][tsl, g * 32:(g + 1) * 32].rearrange("(j p) c -> p j c", p=128)), writes=[egk])
            for j in range(4):
                for h in range(8):
                    pi = (j * 8 + h) % 2
                    ph.op('pool', lambda e, j=j, h=h, pi=pi: e.tensor_tensor(
                        out=Pt[pi][:], in0=bc(eg[:, j, h, :].unsqueeze(2), [128, 4, 128]),
                        in1=bc(E2t[:, j, h, :].unsqueeze(1), [128, 4, 128]), op=ALU.mult),
                        reads=[egk, 'E2t'], writes=['Pt%d' % pi])
                    ph.op('dve', lambda e, j=j, h=h, pi=pi: e.scalar_tensor_tensor(
                        out=gm[:, j, h, :, :], in0=Pt[pi][:], scalar=THt[:, j, h:h + 1], in1=Pt[pi][:],
                        op0=ALU.is_ge, op1=ALU.mult), reads=['Pt%d' % pi, 'E2t'], writes=[gk])

        def utr(ei):
            u = ei % 2
            ph.dma('sp', lambda e: e.dma_start(out=UT[u][:].rearrange("p k e -> p (k e)"), in_=io['UTd'][ei]),
                   writes=['UT%d_0' % u, 'UT%d_1' % u])

        mask(0)
        utr(0)
        for g in range(NGR):
            gm = Gm[g % 2]; gk = 'Gm%d' % (g % 2)
            if g + 1 < NGR:
                mask(g + 1)
            for el in range(4):
                ei = g * 4 + el
                u = ei % 2
                ph.dma('sp', lambda e, el=el, ei=ei: e.dma_start(out=Vb[el][:], in_=io['Vd'][ei]), writes=['Vb%d' % el])
                if ei + 1 < NE:
                    utr(ei + 1)
                for k in range(16):
                    ph.op('pe', lambda e, u=u, k=k: e.matmul(pAT[:], lhsT=UT[u][:, k, :], rhs=hpt[:, k, :], start=(k == 0),
                                                            stop=(k == 15)),
                          reads=['UT%d_0' % u, 'UT%d_1' % u, 'hpt'], writes=['pAT'])
                for j in range(4):
                    for h in range(8):
                        ph.op('pe', lambda e, j=j, h=h, el=el, gm=gm: e.matmul(
                            pWT[:, j * 128:(j + 1) * 128], lhsT=gm[:, j, h, el, :], rhs=G['identb'][:], start=(h == 0),
                            stop=(h == 7)), reads=[gk], writes=['pWT'])
                gl = gel[u]
                ph.op('act', lambda e, gl=gl: e.activation(out=gl[:], in_=pAT[:], func=AF.Gelu), writes=['pAT', 'gel%d' % u])
                ph.op('dve', lambda e, gl=gl, el=el: e.tensor_tensor(out=CT[el][:], in0=pWT[:], in1=gl[:], op=ALU.mult),
                      reads=['gel%d' % u], writes=['pWT', 'CT%d' % el])
            for j in range(4):
                for cc in range(4):
                    for el in range(4):
                        ph.op('pe', lambda e, j=j, cc=cc, el=el: e.matmul(
                            pO[cc][:], lhsT=CT[el][:, j * 128:(j + 1) * 128], rhs=Vb[el][:, cc * 512:(cc + 1) * 512],
                            start=(el == 0), stop=(el == 3)), reads=['CT%d' % el, 'Vb%d' % el], writes=['pO%d' % cc])
                    if g == 0:
                        ph.op('act', lambda e, j=j, cc=cc: e.copy(out=acc[:, j, cc * 512:(cc + 1) * 512], in_=pO[cc][:]),
                              writes=['pO%d' % cc, 'acc%d' % j])
                    else:
                        ph.op('dve', lambda e, j=j, cc=cc: e.tensor_tensor(
                            out=acc[:, j, cc * 512:(cc + 1) * 512], in0=pO[cc][:], in1=acc[:, j, cc * 512:(cc + 1) * 512],
                            op=ALU.add), writes=['pO%d' % cc, 'acc%d' % j])
        ph.dma('sp', lambda e: e.dma_start(out=fw, in_=bc(io['final_norm_w'], [128, 2048])), writes=['E2t'])
        junk = Ub[0]
        for j in range(4):
            rows = slice(tg * 512 + j * 128, tg * 512 + (j + 1) * 128)
            ak = 'acc%d' % j
            ph.dma('sp', lambda e, rows=rows: e.dma_start(out=x1t, in_=io['x1'][rows, :]), reads=['E2t'], writes=['x1t'])
            ph.op('pool', lambda e, j=j: e.tensor_tensor(out=acc[:, j, :], in0=acc[:, j, :], in1=G['g2_bc'][:], op=ALU.mult),
                  writes=[ak])
            ph.op('dve', lambda e, j=j: e.tensor_tensor(out=acc[:, j, :], in0=acc[:, j, :], in1=x1t, op=ALU.add),
                  reads=['x1t'], writes=[ak])
            ph.op('act', lambda e, j=j: e.activation(out=junk[:], in_=acc[:, j, :], func=AF.Square, accum_out=ss[:, 0:1]),
                  reads=[ak], writes=['Ub0', 'ss'])
            ph.op('dve', lambda e: e.tensor_scalar(out=ss[:], in0=ss[:], scalar1=1.0 / 2048, scalar2=EPS, op0=ALU.mult,
                                                   op1=ALU.add), writes=['ss'])
            ph.op('act', lambda e: e.activation(out=ss[:], in_=ss[:], func=AF.Sqrt), writes=['ss'])
            ph.op('dve', lambda e: e.reciprocal(out=ss[:], in_=ss[:]), writes=['ss'])
            ph.op('dve', lambda e, j=j: e.scalar_tensor_tensor(out=acc[:, j, :], in0=acc[:, j, :], scalar=ss[:, 0:1], in1=fw,
                                                               op0=ALU.mult, op1=ALU.mult), reads=['ss', 'E2t'], writes=[ak])
            ph.dma('sp', lambda e, j=j, rows=rows: e.dma_start(out=io['y'][rows, :], in_=acc[:, j, :]), reads=[ak],
                   writes=['yout'])
    ph.emit()


IN_SPECS = None


def build(cfg, upto=99):
    NS, NCTX, NO, NLAT = cfg['NS'], cfg['NCTX'], cfg['NO'], cfg['NS'] - cfg['NCTX']
    NT = NS // 128
    nc = bass.Bass("TRN2", target_bir_lowering=False)
    io = {}

    def inp(n, shape, dt=F32):
        io[n] = nc.dram_tensor(n, list(shape), dt, kind="ExternalInput").ap()

    def scr(n, shape, dt=F32):
        io[n] = nc.dram_tensor(n, list(shape), dt).ap()

    inp('xs', [NS, D]); inp('xo', [NO, D]); inp('own_idx', [128, NO // 128], I32)
    inp('c_t', [128, 16, 2]); inp('w_mod', [D, 6 * D]); inp('b_modT', [128, 96]); inp('norm1T', [128, 16])
    inp('norm2T', [128, 16]); inp('final_norm_w', [1, D])
    inp('w_in', [D, P_IN]); inp('b_in', [1, P_IN]); inp('b_qk', [128, 16]); inp('b_gate', [8, 4]); inp('b_gab', [128, 32])
    inp('zeros16', [128, 16])
    inp('conv_wT', [128, 16, 5]); inp('conv_bT', [128, 16]); inp('m_norm_w', [1, 1024]); inp('a_norm_w', [1, 128])
    inp('lambdas', [1, 256]); inp('ropeS', [NS, 64]); inp('ropeO', [NO, 64])
    inp('w_pa', [1024, D]); inp('w_pb', [1024, D]); inp('w_out', [D, D]); inp('w_pq', [D, D]); inp('skT', [128, 16, 128])
    nexp_rows = NEXP if upto >= 20 else 128
    inp('expert_u', [nexp_rows, D]); inp('expert_v', [nexp_rows, D])
    inp('ident', [128, 128]); inp('maskf', [128, 128]); inp('maskb', [128, 128]); inp('ones', [128, 128]); inp('sel', [8, 8, 128])
    io['y'] = nc.dram_tensor('y', [NO, D], F32, kind="ExternalOutput").ap()
    scr('gvec', [2, D]); scr('hTs', [16, 128, NS], BF16); scr('hTo', [16, 128, NO], BF16)
    scr('QKpre', [16, 128, NS + 6]); scr('GT', [4, 8, NS]); scr('QT_m', [8, 128, NS], BF16); scr('KT_m', [8, 128, NS], BF16)
    scr('Ktok', [NT, 128, 8, 128], BF16); scr('Vtok', [NT, 128, 8, 128], BF16); scr('KaT', [8, 128, NS], BF16)
    scr('Va', [NT, 128, 8, 128], BF16)
    scr('TS', [2, 128, NT * 32]); scr('DEC', [2, 128, NT * 8]); scr('MF', [2, 8, NS])
    scr('Hf', [NLAT, 1024]); scr('Hb', [NLAT, 1024]); scr('Zo', [NO, 1024]); scr('QaT', [8, 128, NO], BF16)
    scr('GaT', [16, 128, NO], BF16); scr('GbT', [16, 128, NO], BF16); scr('ymT', [8, 128, NO], BF16); scr('ydT', [8, 128, NO], BF16)
    scr('mT', [16, 128, NO], BF16); scr('x1', [NO, D]); scr('hpT', [16, 128, NO], BF16); scr('qT', [16, 128, NO], BF16)
    scr('E1g', [NO, 1024]); scr('E2', [NO, 1024]); scr('TH', [NO, 8])
    scr('UTd', [128, 128, 2048], BF16); scr('Vd', [128, 128, 2048], BF16)

    es = ExitStack()
    SEMSTATE[0] = SemState(nc, es)
    G = {}
    for n, shp, dt in (('modT', [128, 96, 2], F32), ('a1', [128, 16], F32), ('a1c', [128, 16], F32), ('a2', [128, 16], F32),
                       ('g1_bc', [128, 2048], F32), ('g2_bc', [128, 2048], F32), ('ident', [128, 128], F32),
                       ('identb', [128, 128], BF16), ('maskf', [128, 128], F32), ('maskb', [128, 128], F32),
                       ('ones', [128, 128], F32), ('onesb', [128, 128], BF16), ('sel', [8, 8, 128], F32), ('nlam', [128, 1], F32)):
        G[n] = es.enter_context(nc.sbuf_tensor('G_' + n, shp, dt))
    modT = G['modT']
    NCT = NCTX // 128
    oblk = [(i * 512, min(512, NO - i * 512)) for i in range((NO + 511) // 512)]
    steps = [
        lambda: phase_init(nc, G, io),
        lambda: phase_mod(nc, G, io),
        lambda: phase_norm(nc, G, 'B', io['xs'], io['hTs'], NS,
                           [(0, NCT, G['a1c'], modT[:, 0:16, 1]), (NCT, NT, G['a1'], modT[:, 0:16, 0])]),
        lambda: phase_norm(nc, G, 'Bo', io['xo'], io['hTo'], NO, [(0, 9999, G['a1'], modT[:, 0:16, 0])]),
        lambda: phase_gemm_fm(nc, G, 'C1', io, cfg, io['hTs'], NS, blocks(cfg), [
            dict(c0=OFF['qm'], ncb=16, bias=io['b_qk'], func=AF.Identity, odt=F32,
                 dst=lambda cb, s0, n: io['QKpre'][cb, :, pcol(cfg, s0):pcol(cfg, s0) + n]),
            dict(c0=OFF['g'], ncb=4, m=8, bias=io['b_gate'], func=AF.Identity, odt=F32,
                 dst=lambda cb, s0, n: io['GT'][cb, :, s0:s0 + n])], io['w_in']),
        lambda: phase_gemm_tm(nc, G, 'C2', io, cfg, io['hTs'], NS, io['w_in'], [
            dict(c0=OFF['vm'], ncols=1024, kind='bf16', dst=lambda t: io['Vtok'][t].rearrange("p h v -> p (h v)")),
            dict(c0=OFF['ka'], ncols=1024, kind='rope',
                 dst=lambda t: io['KaT'][:, :, t * 128:(t + 1) * 128].rearrange("h p t -> p h t")),
            dict(c0=OFF['va'], ncols=1024, kind='bf16', dst=lambda t: io['Va'][t].rearrange("p h v -> p (h v)"))],
            rope_tab=io['ropeS']),
        lambda: phase_conv(nc, G, io, cfg),
        lambda: phase_gates(nc, G, io, cfg),
        lambda: phase_scan(nc, G, io, cfg),
        lambda: phase_gemm_fm(nc, G, 'G1', io, cfg, io['hTo'], NO, oblk, [
            dict(c0=OFF['ga'], ncb=16, bias=io['b_gab'][:, 0:16], func=AF.Sigmoid, odt=BF16,
                 dst=lambda cb, s0, n: io['GaT'][cb, :, s0:s0 + n]),
            dict(c0=OFF['gb'], ncb=16, bias=io['b_gab'][:, 16:32], func=AF.Sigmoid, odt=BF16,
                 dst=lambda cb, s0, n: io['GbT'][cb, :, s0:s0 + n])], io['w_in']),
        lambda: phase_gemm_tm(nc, G, 'G2', io, cfg, io['hTo'], NO, io['w_in'], [
            dict(c0=OFF['zm'], ncols=1024, kind='sig', dst=lambda t: io['Zo'][t * 128:(t + 1) * 128, :]),
            dict(c0=OFF['qa'], ncols=1024, kind='rope',
                 dst=lambda t: io['QaT'][:, :, t * 128:(t + 1) * 128].rearrange("h p t -> p h t"))],
            rope_tab=io['ropeO']),
        lambda: phase_attn(nc, G, io, cfg),
        lambda: phase_ym(nc, G, io, cfg),
        lambda: phase_merge(nc, G, io, cfg),
        lambda: phase_wout(nc, G, io, cfg),
        lambda: phase_norm(nc, G, 'N2', io['x1'], io['hpT'], NO, [(0, 9999, G['a2'], modT[:, 48:64, 0])]),
        lambda: phase_gemm_fm(nc, G, 'Q', io, cfg, io['hpT'], NO, oblk, [
            dict(c0=0, ncb=16, bias=io['zeros16'], func=AF.Identity, odt=BF16,
                 dst=lambda cb, s0, n: io['qT'][cb, :, s0:s0 + n])], io['w_pq']),
        lambda: phase_peer_sel(nc, G, io, cfg),
        lambda: phase_peer_prep(nc, G, io, cfg),
        lambda: phase_peer_dense(nc, G, io, cfg),
    ]
    for i, st in enumerate(steps):
        if i >= upto:
            break
        st()
    es.close()
    return nc, io


def rope_tables(nrows_lat, grid_w=64):
    rows = nrows_lat // grid_w
    row = np.repeat(np.arange(rows, dtype=np.float32), grid_w)
    col = np.tile(np.arange(grid_w, dtype=np.float32), rows)
    inv = (np.float32(10000.0) ** (-np.arange(0, 32, 2, dtype=np.float32) / np.float32(32))).astype(np.float32)
    ang = np.concatenate([row[:, None] * inv, col[:, None] * inv], axis=-1).astype(np.float32)
    return np.concatenate([np.cos(ang), np.sin(ang)], axis=-1).astype(np.float32)


def fm(v):
    return np.ascontiguousarray(np.asarray(v, np.float32).reshape(-1, 128).T)


def host_inputs(inputs, cfg, b, r):
    NCTX, NO = cfg['NCTX'], cfg['NO']
    f = lambda a: np.ascontiguousarray(np.asarray(a, np.float32))
    x = f(inputs['x'][b]); ctx = f(inputs['ctx'][b])
    NLAT = x.shape[0]
    lo = r * NO
    m = {}
    m['xs'] = np.concatenate([ctx, x], axis=0)
    m['xo'] = np.ascontiguousarray(x[lo:lo + NO])
    m['own_idx'] = np.ascontiguousarray(np.arange(lo, lo + NO, dtype=np.int32).reshape(NO // 128, 128).T)
    m['c_t'] = np.ascontiguousarray(np.stack([fm(inputs['c'][b]), fm(inputs['c_ctx'])], axis=-1))
    m['w_mod'] = f(inputs['w_mod'][0]); m['b_modT'] = fm(inputs['b_mod'][0])
    m['norm1T'] = fm(inputs['norm1_w'][0]); m['norm2T'] = fm(inputs['norm2_w'][0])
    m['final_norm_w'] = f(inputs['final_norm_w']).reshape(1, D)
    m['w_in'] = f(inputs['w_in'][0]); bi = f(inputs['b_in'][0]); m['b_in'] = bi.reshape(1, P_IN)
    m['b_qk'] = fm(bi[0:2048]); m['b_gate'] = np.ascontiguousarray(bi[4096:4128].reshape(4, 8).T)
    m['b_gab'] = fm(bi[7200:11296]); m['zeros16'] = np.zeros((128, 16), np.float32)
    cw = f(inputs['conv_w'][0])
    m['conv_wT'] = np.ascontiguousarray(cw.reshape(5, 16, 128).transpose(2, 1, 0))
    m['conv_bT'] = fm(inputs['conv_b'][0])
    m['m_norm_w'] = f(inputs['m_norm_w'][0]).reshape(1, 1024); m['a_norm_w'] = f(inputs['a_norm_w'][0]).reshape(1, 128)
    m['lambdas'] = f(inputs['lambdas'][0]).reshape(1, 256)
    rt = rope_tables(NLAT)
    rs = np.zeros((NCTX, 64), np.float32); rs[:, 0:32] = 1.0
    m['ropeS'] = np.concatenate([rs, rt], axis=0); m['ropeO'] = np.ascontiguousarray(rt[lo:lo + NO])
    m['w_pa'] = f(inputs['w_pa'][0]); m['w_pb'] = f(inputs['w_pb'][0]); m['w_out'] = f(inputs['w_out'][0])
    m['w_pq'] = f(inputs['w_pq'][0])
    sk = f(inputs['sub_keys'][0])
    m['skT'] = np.ascontiguousarray(sk.reshape(16, 128, 128).transpose(2, 0, 1))
    m['expert_u'] = f(inputs['expert_u'][0]); m['expert_v'] = f(inputs['expert_v'][0])
    m['ident'] = np.eye(128, dtype=np.float32)
    s_ = np.arange(128)[:, None]; t_ = np.arange(128)[None, :]
    m['maskf'] = np.where(s_ > t_, BIG, 0.0).astype(np.float32); m['maskb'] = np.where(s_ < t_, BIG, 0.0).astype(np.float32)
    m['ones'] = np.ones((128, 128), np.float32)
    sel = np.zeros((8, 8, 128), np.float32)
    for j in range(8):
        sel[j, j, :] = 1.0
    m['sel'] = sel
    return m


def kernel(**inputs):
    x = np.asarray(inputs['x'])
    B, T, _ = x.shape
    NCTX = np.asarray(inputs['ctx']).shape[1]
    R = 8 // B
    cfg = dict(NCTX=NCTX, NS=NCTX + T, NO=T // R)
    nc, _ = build(cfg)
    in_maps = [host_inputs(inputs, cfg, c // R, c % R) for c in range(8)]
    res = run_bass_kernel_spmd(nc, in_maps, core_ids=list(range(8)))
    out = np.empty((B, T, D), np.float32)
    for c in range(8):
        b, r = c // R, c % R
        out[b, r * cfg['NO']:(r + 1) * cfg['NO']] = res.results[c]['y']
    return out
```
